# Optimizing a Trainium2 kernel written in Bass

```python
import math
import jax, jax.numpy as jnp
from jax import lax
import numpy as np

D_MODEL = 1024
BATCH = 16
SEQ = 2048
DEPTH = 1
DEC_BATCH = 8
DEC_SEQ = 2048
PAST_LEN = 128

MIX_W = D_MODEL
POOL_W = MIX_W // 2
POOL_WINDOWS = (2, 4, 8, 16)
N_POOL_GROUPS = len(POOL_WINDOWS)
POOL_GC = POOL_W // N_POOL_GROUPS
N_HEADS = 8
QK_NOPE = 64
QK_ROPE = 32
QK_DIM = QK_NOPE + QK_ROPE
V_DIM = 64
ATT_W = N_HEADS * V_DIM
Q_LORA = 384
KV_LORA = 256
ROPE_BASE = 10000.0
Q_BLOCK = 128
ATTN_SCALE = 1.0 / math.sqrt(QK_DIM)
IN_W = POOL_W + Q_LORA + KV_LORA + QK_ROPE
D_FF = int(math.ceil(D_MODEL * 8 / 3 / 256) * 256)
PLE_DIM = 256
EPS = 1e-6

kernel_name = "hybrid_pool_mla_encoder"


def rms_norm(x, g):
    xf = x.astype(jnp.float32)
    y = xf * lax.rsqrt(jnp.mean(xf * xf, axis=-1, keepdims=True) + EPS)
    return (y * g.astype(jnp.float32)).astype(x.dtype)


def rope_tables(S, dtype):
    inv = ROPE_BASE ** (-jnp.arange(0, QK_ROPE, 2, dtype=jnp.float32) / QK_ROPE)
    ang = jnp.arange(S, dtype=jnp.float32)[:, None] * inv[None, :]
    return jnp.cos(ang).astype(dtype), jnp.sin(ang).astype(dtype)


def apply_rope(x, cos, sin):
    x1, x2 = jnp.split(x, 2, axis=-1)
    c = cos[None, :, None, :]
    s = sin[None, :, None, :]
    return jnp.concatenate([x1 * c - x2 * s, x1 * s + x2 * c], axis=-1)


def multiscale_pool(u, w_pool, pool_scale):
    B, S, _ = u.shape
    ug = u.reshape(B, S, N_POOL_GROUPS, POOL_GC)
    csum = jnp.cumsum(ug.astype(jnp.float32), axis=1)
    cs = jnp.concatenate([jnp.zeros_like(csum[:, :1]), csum], axis=1)
    t = jnp.arange(S)
    means = []
    for g, w in enumerate(POOL_WINDOWS):
        lo = jnp.clip(t - w // 2, 0, S)
        hi = jnp.clip(t - w // 2 + w, 0, S)
        csg = cs[:, :, g]
        cnt = (hi - lo).astype(jnp.float32)[None, :, None]
        means.append((csg[:, hi] - csg[:, lo]) / cnt)
    mean = jnp.stack(means, axis=2).astype(u.dtype)
    y = jnp.einsum('bsgc,gcd->bsgd', mean - ug, w_pool)
    return y.reshape(B, S, POOL_W) * pool_scale


def block_attention(q, k, v):
    B, S, H, D = q.shape
    nb = S // Q_BLOCK
    qb = q.reshape(B, nb, Q_BLOCK, H, D).transpose(1, 0, 2, 3, 4)

    def one(qblk):
        s = jnp.einsum('bqhd,bkhd->bhqk', qblk, k).astype(jnp.float32) * ATTN_SCALE
        p = jax.nn.softmax(s, axis=-1).astype(v.dtype)
        return jnp.einsum('bhqk,bkhd->bqhd', p, v)

    o = lax.map(one, qb)
    return o.transpose(1, 0, 2, 3, 4).reshape(B, S, H * V_DIM)


def encoder_layer(h, p, cos, sin, ln1, w_in, w_pool, pool_scale, q_a_norm, w_qb, kv_a_norm, w_kvb,
                  q_norm, k_norm, w_o, ln2, w_gate, w_up, w_down, ple_norm, w_ple_gate, w_ple_proj):
    B, S, _ = h.shape
    u = rms_norm(h, ln1)
    z = u @ w_in
    pool_in = z[..., :POOL_W]
    c_q = z[..., POOL_W:POOL_W + Q_LORA]
    c_kv = z[..., POOL_W + Q_LORA:POOL_W + Q_LORA + KV_LORA]
    k_r = z[..., POOL_W + Q_LORA + KV_LORA:]

    y_pool = multiscale_pool(pool_in, w_pool, pool_scale)

    q = (rms_norm(c_q, q_a_norm) @ w_qb).reshape(B, S, N_HEADS, QK_DIM)
    kv = (rms_norm(c_kv, kv_a_norm) @ w_kvb).reshape(B, S, N_HEADS, QK_NOPE + V_DIM)
    k_nope, v = kv[..., :QK_NOPE], kv[..., QK_NOPE:]
    k = jnp.concatenate([k_nope, jnp.broadcast_to(k_r[:, :, None, :], (B, S, N_HEADS, QK_ROPE))], axis=-1)
    q = rms_norm(q, q_norm)
    k = rms_norm(k, k_norm)
    q = jnp.concatenate([q[..., :QK_NOPE], apply_rope(q[..., QK_NOPE:], cos, sin)], axis=-1)
    k = jnp.concatenate([k[..., :QK_NOPE], apply_rope(k[..., QK_NOPE:], cos, sin)], axis=-1)
    y_att = block_attention(q, k, v)

    h = h + jnp.concatenate([y_pool, y_att], axis=-1) @ w_o

    u2 = rms_norm(h, ln2)
    h = h + (jax.nn.silu(u2 @ w_gate) * (u2 @ w_up)) @ w_down

    gate = jax.nn.sigmoid(rms_norm(h, ple_norm) @ w_ple_gate)
    return h + gate * (p @ w_ple_proj)


def setup_inputs(seed: int = 0) -> dict:
    key = jax.random.key(seed)
    ks = jax.random.split(key, 24)
    f = jnp.float32

    def nrm(k, shape, fan_in):
        return jax.random.normal(k, shape, f) * (fan_in ** -0.5)

    def gain(k, shape):
        return 1.0 + 0.1 * jax.random.normal(k, shape, f)

    L = DEPTH
    return {
        "x_prompt": jax.random.normal(ks[0], (BATCH, SEQ, D_MODEL), f),
        "x_sample": jax.random.normal(ks[1], (DEC_BATCH, DEC_SEQ, D_MODEL), f),
        "p_prompt": jax.random.normal(ks[2], (DEPTH, BATCH, SEQ, PLE_DIM), f),
        "p_sample": jax.random.normal(ks[3], (DEPTH, DEC_BATCH, DEC_SEQ, PLE_DIM), f),
        "ln1": gain(ks[4], (L, D_MODEL)),
        "w_in": nrm(ks[5], (L, D_MODEL, IN_W), D_MODEL),
        "w_pool": nrm(ks[6], (L, N_POOL_GROUPS, POOL_GC, POOL_GC), POOL_GC),
        "pool_scale": gain(ks[7], (L, POOL_W)),
        "q_a_norm": gain(ks[8], (L, Q_LORA)),
        "w_qb": nrm(ks[9], (L, Q_LORA, N_HEADS * QK_DIM), Q_LORA),
        "kv_a_norm": gain(ks[10], (L, KV_LORA)),
        "w_kvb": nrm(ks[11], (L, KV_LORA, N_HEADS * (QK_NOPE + V_DIM)), KV_LORA),
        "q_norm": gain(ks[12], (L, QK_DIM)),
        "k_norm": gain(ks[13], (L, QK_DIM)),
        "w_o": nrm(ks[14], (L, MIX_W, D_MODEL), MIX_W),
        "ln2": gain(ks[15], (L, D_MODEL)),
        "w_gate": nrm(ks[16], (L, D_MODEL, D_FF), D_MODEL),
        "w_up": nrm(ks[17], (L, D_MODEL, D_FF), D_MODEL),
        "w_down": nrm(ks[18], (L, D_FF, D_MODEL), D_FF),
        "ple_norm": gain(ks[19], (L, D_MODEL)),
        "w_ple_gate": nrm(ks[20], (L, D_MODEL, D_MODEL), D_MODEL),
        "w_ple_proj": nrm(ks[21], (L, PLE_DIM, D_MODEL), PLE_DIM),
    }


def reference(x_prompt, x_sample, p_prompt, p_sample, ln1, w_in, w_pool, pool_scale, q_a_norm, w_qb,
              kv_a_norm, w_kvb, q_norm, k_norm, w_o, ln2, w_gate, w_up, w_down, ple_norm, w_ple_gate,
              w_ple_proj):
    def run(x, p):
        cos, sin = rope_tables(x.shape[1], x.dtype)
        h = x
        for i in range(DEPTH):
            h = encoder_layer(h, p[i], cos, sin, ln1[i], w_in[i], w_pool[i], pool_scale[i], q_a_norm[i],
                              w_qb[i], kv_a_norm[i], w_kvb[i], q_norm[i], k_norm[i], w_o[i], ln2[i],
                              w_gate[i], w_up[i], w_down[i], ple_norm[i], w_ple_gate[i], w_ple_proj[i])
        return h

    y_prompt = run(x_prompt, p_prompt)
    y_sample = run(x_sample, p_sample)
    return (y_prompt, y_sample)
```

```python
import math
import os
from contextlib import ExitStack
import numpy as np
import concourse.bass as bass
import concourse.mybir as mybir
from concourse.bass_utils import run_bass_kernel_spmd

F32 = mybir.dt.float32
BF16 = mybir.dt.bfloat16
U8 = mybir.dt.uint8
AF = mybir.ActivationFunctionType
ALU = mybir.AluOpType
AX = mybir.AxisListType

S = 2048
D = 1024
NT = S // 128
IN_W = 1184
DFF = 2816
NF = DFF // 128
EPS = 1e-6
ATT_SCALE = 1.0 / math.sqrt(96.0)
POOL_WINDOWS = (2, 4, 8, 16)
NCORES = 8
NSEQ_CORE = 3


SAME_ENG_EDGES = bool(os.environ.get("KSE"))


class _Op:
    __slots__ = ("eng", "fn", "deps", "is_dma", "semkey", "ndma", "signal", "token", "idx", "nosig")


class Prog:
    ENGS = ("pe", "act", "dve", "pool", "sp")

    def __init__(self, nc):
        self.nc = nc
        self.engs = {"pe": nc.tensor, "act": nc.scalar, "dve": nc.vector,
                     "pool": nc.gpsimd, "sp": nc.sync}
        self.ops = []
        self.last_writer = {}
        self.readers = {}
        self.pending = {e: set() for e in self.ENGS}
        self.filter = None
        self.stage = None

    def op(self, eng, fn, reads=(), writes=(), dma=None, ndma=1, nosig=False, extra=None):
        if self.filter is not None and self.stage not in self.filter:
            return None
        o = _Op()
        o.nosig = nosig
        o.eng = eng
        o.fn = fn
        o.is_dma = dma is not None
        o.semkey = dma
        o.ndma = ndma
        o.signal = o.is_dma
        o.token = None
        o.idx = len(self.ops)
        deps = set()
        ops = self.ops
        for r in reads:
            w = self.last_writer.get(r)
            if w is not None:
                wo = ops[w]
                if not (wo.eng == "pe" and eng == "pe" and not wo.is_dma and not o.is_dma):
                    deps.add(w)
        for r in writes:
            w = self.last_writer.get(r)
            if w is not None:
                wo = ops[w]
                if wo.is_dma or o.is_dma or wo.eng != eng or (SAME_ENG_EDGES and eng != "pe"):
                    deps.add(w)
            for rd in self.readers.get(r, {}).values():
                ro = ops[rd]
                if ro.is_dma or o.is_dma or ro.eng != eng or (SAME_ENG_EDGES and eng != "pe"):
                    deps.add(rd)
        if self.pending[eng]:
            deps |= self.pending[eng]
            self.pending[eng] = set()
        if extra:
            deps |= set(extra)
        o.deps = deps
        rkey = ("dma", dma) if o.is_dma else eng
        for r in reads:
            self.readers.setdefault(r, {})[rkey] = o.idx
        for r in writes:
            self.last_writer[r] = o.idx
            self.readers[r] = {}
        ops.append(o)
        return o.idx

    def _all_last(self):
        last = {}
        for o in self.ops:
            if o.is_dma:
                last[("dma", o.semkey)] = o.idx
            else:
                last[o.eng] = o.idx
        return set(last.values())

    def barrier(self):
        deps = self._all_last()
        for e in self.ENGS:
            self.pending[e] = set(deps)

    def finish(self):
        deps = self._all_last()
        self.pending["sp"] = set(deps)
        self.op("sp", lambda e: None)

    def emit(self, get_sem):
        ops = self.ops
        nxt = {}
        last_ok = {}
        for o in reversed(ops):
            if o.is_dma:
                continue
            if o.nosig:
                nxt[o.idx] = last_ok.get(o.eng)
            else:
                last_ok[o.eng] = o.idx
        for o in ops:
            nd = set()
            for d in o.deps:
                if ops[d].nosig:
                    r = nxt[d]
                    assert r is not None and r < o.idx, ("cannot redirect nosig dep", d, r, o.idx)
                    nd.add(r)
                else:
                    nd.add(d)
            o.deps = nd
        for o in ops:
            for d in o.deps:
                ops[d].signal = True
        counts = {}
        for o in ops:
            if o.is_dma:
                key = ("dma", o.semkey)
                counts[key] = counts.get(key, 0) + 16 * o.ndma
                o.token = (key, counts[key])
            elif o.signal:
                key = ("eng", o.eng)
                counts[key] = counts.get(key, 0) + 1
                o.token = (key, counts[key])
        sems = {key: get_sem("s_" + "_".join(str(k) for k in key)) for key in counts}
        eng_know = {e: {} for e in self.ENGS}
        know = [None] * len(ops)
        prev_dma_tok = {}
        nwaits = 0
        plan = {e: [] for e in self.ENGS}
        for o in ops:
            ek = eng_know[o.eng]
            need = {}
            for d in o.deps:
                k, v = ops[d].token
                if ek.get(k, 0) < v and need.get(k, 0) < v:
                    need[k] = v
            if need:
                newk = dict(ek)
                for d in o.deps:
                    for k, v in know[d].items():
                        if newk.get(k, 0) < v:
                            newk[k] = v
                for k in list(need.keys()):
                    v = need[k]
                    for d in o.deps:
                        tk, tv = ops[d].token
                        if tk == k and tv >= v:
                            continue
                        if know[d].get(k, 0) >= v:
                            del need[k]
                            break
                nwaits += len(need)
                eng_know[o.eng] = newk
                ek = newk
            if o.is_dma:
                key = o.token[0]
                pv = prev_dma_tok.get(key, 0)
                assert ek.get(key, 0) >= pv, f"DMA slot {key} reissued while previous group may be in flight"
                prev_dma_tok[key] = o.token[1]
            plan[o.eng].append((o, list(need.items())))
            if o.is_dma or o.signal:
                kk = dict(ek)
                kk[o.token[0]] = o.token[1]
                know[o.idx] = kk
            else:
                know[o.idx] = ek

        semv = {k: 0 for k in counts}
        ptr = {e: 0 for e in self.ENGS}
        progress = True
        while progress:
            progress = False
            for e_ in self.ENGS:
                while ptr[e_] < len(plan[e_]):
                    o, waits = plan[e_][ptr[e_]]
                    if all(semv[k] >= v for k, v in waits):
                        if o.is_dma:
                            semv[o.token[0]] += 16 * o.ndma
                        elif o.signal:
                            semv[o.token[0]] += 1
                        ptr[e_] += 1
                        progress = True
                    else:
                        break
        for e_ in self.ENGS:
            assert ptr[e_] == len(plan[e_]), ("DEADLOCK", e_, ptr[e_], len(plan[e_]), plan[e_][ptr[e_]][1], semv)

        def mk(engname):
            def body(eng):
                for (o, waits) in plan[engname]:
                    for k, v in waits:
                        eng.wait_ge(sems[k], v)
                    res = o.fn(eng)
                    if o.is_dma:
                        if not isinstance(res, (list, tuple)):
                            res = [res]
                        assert len(res) == o.ndma, (len(res), o.ndma)
                        for r in res:
                            r.then_inc(sems[o.token[0]], 16)
                    elif o.signal:
                        res.then_inc(sems[o.token[0]], 1)
            return body

        with self.nc.Block() as block:
            block.tensor(mk("pe"))
            block.scalar(mk("act"))
            block.vector(mk("dve"))
            block.gpsimd(mk("pool"))
            block.sync(mk("sp"))
        return nwaits, counts


class Arena:
    def __init__(self, big, limit):
        self.big = big
        self.off = 0
        self.limit = limit
        self.peak = 0

    def t(self, shape, dt):
        esz = 4 if dt == F32 else 2
        n = esz
        for d in shape[1:]:
            n *= d
        off = self.off
        self.off += (n + 31) // 32 * 32
        self.peak = max(self.peak, self.off)
        assert self.off <= self.limit, (self.off, self.limit)
        ap = self.big[:, off:off + n].bitcast(dt)
        if len(shape) == 3:
            ap = ap.rearrange("p (a b) -> p a b", a=shape[1])
        elif len(shape) == 4:
            ap = ap.rearrange("p (a b c) -> p a b c", a=shape[1], b=shape[2])
        return ap


def build_nc(NSEQ=NSEQ_CORE):
    nc = bass.Bass("TRN2", target_bir_lowering=False)

    def din(name, shape):
        return nc.dram_tensor(name, shape, F32, kind="ExternalInput").ap()

    x = din("x", [NSEQ, S, D])
    pin_d = din("p", [NSEQ, S, 256])
    ln1_d = din("ln1", [1, D])
    ln2_d = din("ln2", [1, D])
    plen_d = din("ple_norm", [1, D])
    qan_d = din("q_a_norm", [1, 384])
    kvan_d = din("kv_a_norm", [1, 256])
    qn_d = din("q_norm", [1, 96])
    kn_d = din("k_norm", [1, 96])
    psc_d = din("pool_scale", [128, 4])
    w_in_d = din("w_in", [D, IN_W])
    w_pool_d = din("w_pool", [128, 512])
    w_qb_d = din("w_qb", [384, 768])
    w_kvb_d = din("w_kvb", [256, 1024])
    w_o_d = din("w_o", [D, D])
    w_gate_d = din("w_gate", [D, DFF])
    w_up_d = din("w_up", [D, DFF])
    w_down_d = din("w_down", [DFF, D])
    w_pg_d = din("w_ple_gate", [D, D])
    w_pp_d = din("w_ple_proj", [256, D])
    ident_d = din("ident", [128, 128])
    band_d = din("band", [128, 2560])
    cos_d = din("cosT", [128, 256])
    sin_d = din("sinT", [128, 256])
    y = nc.dram_tensor("y", [NSEQ, S, D], F32, kind="ExternalOutput").ap()
    DBG = bool(os.environ.get("KDBG"))
    if DBG:
        dbg_y = nc.dram_tensor("dbg_y", [128, 8, S], F32, kind="ExternalOutput").ap()
        dbg_q = nc.dram_tensor("dbg_q", [128, 8, S], F32, kind="ExternalOutput").ap()
        dbg_k = nc.dram_tensor("dbg_k", [128, 8, S], F32, kind="ExternalOutput").ap()
        dbg_v = nc.dram_tensor("dbg_v", [128, 16, 4, 192], F32, kind="ExternalOutput").ap()

    def dscr(name, shape):
        return nc.dram_tensor(name, shape, BF16, kind="Internal").ap()

    wo_b = dscr("wo_b", [D, D])
    wg_b = dscr("wg_b", [D, DFF])
    wu_b = dscr("wu_b", [D, DFF])
    wd_b = dscr("wd_b", [DFF, D])
    wpg_b = dscr("wpg_b", [D, D])
    wpp_b = dscr("wpp_b", [256, D])

    es = ExitStack()
    P = Prog(nc)
    with es:
        TOT = 212000
        big = es.enter_context(nc.sbuf_tensor("big", [128, TOT], U8))
        psum = es.enter_context(nc.psum_tensor("psum", [128, 8, 512], F32))
        A = Arena(big, TOT)

        def PS(b, n=1):
            return [("ps", b + i) for i in range(n)]

        def ps_bf(b):
            return psum[:, b, :].bitcast(BF16)

        ident = A.t([128, 128], BF16)
        identf = A.t([128, 128], F32)
        ones_f = A.t([128, 128], F32)
        ones_b = A.t([128, 128], BF16)
        yT = A.t([128, 8, S], BF16)
        st = A.t([128, 64], F32)
        negh = A.t([128, 8], F32)
        region0 = A.off

        Win = A.t([128, 8, IN_W], BF16)
        Wqb = A.t([128, 3, 768], BF16)
        Wkvb = A.t([128, 2, 1024], BF16)
        Wpool = A.t([128, 4, 128], BF16)
        Band = A.t([128, 20, 128], BF16)
        gln1 = A.t([128, D], F32)
        gqa = A.t([128, 384], F32)
        gkva = A.t([128, 256], F32)
        gq = A.t([128, 96], F32)
        gk = A.t([128, 96], F32)
        psc = A.t([128, 4], F32)
        cos_t = A.t([128, 16, 16], F32)
        sin_t = A.t([128, 16, 16], F32)
        CGq = A.t([128, 16, 32], F32)
        CGk = A.t([128, 16, 32], F32)
        SG1q = A.t([128, 16, 16], F32)
        SG2q = A.t([128, 16, 16], F32)
        SG1k = A.t([128, 16, 16], F32)
        SG2k = A.t([128, 16, 16], F32)
        qT = A.t([128, 8, S], BF16)
        kT = A.t([128, 8, S], BF16)
        V = A.t([128, 16, 4, 192], BF16)
        work0 = A.off
        xt = [A.t([128, D], F32) for _ in range(2)]
        sqj = A.t([128, D], BF16)
        ub = A.t([128, D], BF16)
        uT = A.t([128, 8, 128], BF16)
        pinb = [A.t([128, 512], BF16) for _ in range(4)]
        cb = A.t([128, 640], BF16)
        cT2 = [A.t([128, 5, 128], BF16) for _ in range(2)]
        krs2 = [A.t([128, 32], F32) for _ in range(2)]
        sqjk = A.t([128, 32], BF16)
        sqq = A.t([128, 768], F32)
        sqk = A.t([128, 512], F32)
        Tr = A.t([128, 8, 32], F32)
        Ar = A.t([128, 8, 32], F32)
        Br = A.t([128, 8, 32], F32)
        qfin = A.t([128, 8, 96], BF16)
        kfin = A.t([128, 8, 96], BF16)
        dsb = A.t([128, 4, 128], BF16)
        endA = A.off
        A.off = work0
        pT = [A.t([128, 2, 512], BF16) for _ in range(3)]
        rdh = [A.t([128, 512], BF16) for _ in range(2)]
        rdl = [A.t([128, 512], BF16) for _ in range(2)]
        bsb = [A.t([128, 512], F32) for _ in range(2)]
        assert A.off <= endA

        A.off = region0
        RING = 6
        ring = [A.t([128, 4096], BF16) for _ in range(RING)]
        assert A.off - region0 <= 51488, "ring must only alias phase-1-only buffers (weights/gains/rope tables)"
        Wdown = A.t([128, NF, D], BF16)
        sqjB = A.t([128, D], BF16)
        ubB = [A.t([128, D], BF16) for _ in range(2)]
        u2T = A.t([128, 8, 512], BF16)
        u3T = [A.t([128, 8, 128], BF16) for _ in range(2)]
        actT = A.t([128, NF, 512], BF16)
        sgb = [A.t([128, 512], F32) for _ in range(2)]
        gln2 = A.t([128, D], F32)
        gple = A.t([128, D], F32)
        ptl4 = A.t([128, 4, 256], F32)
        pb4 = A.t([128, 4, 256], BF16)
        pTt4 = A.t([128, 4, 2, 128], BF16)
        gsig = A.t([128, D], F32)
        assert A.off >= work0 + 14336, "h4 must start after the attention working set"
        h4 = A.t([128, 4, D], F32)
        assert A.off <= endA, "h4 must stay inside the phase-1 working area"
        endB = A.off
        print("SBUF: always", region0, "A", endA - region0, "B", endB - region0, "peak", A.peak)

        STOP = int(os.environ.get("KSTOP", "99"))
        RE = os.environ.get("KROPE", "pool")
        def rstd_from_ssq(ssq_ap, n, tmp_ap, out_ap, rs_in, rs_tmp, rs_out, scale_n):
            P.op("act", lambda e: e.activation(tmp_ap, ssq_ap, AF.Sqrt, bias=EPS, scale=1.0 / scale_n),
                 reads=[rs_in], writes=[rs_tmp])
            P.op("dve", lambda e: e.reciprocal(out_ap, tmp_ap), reads=[rs_tmp], writes=[rs_out])

        P.op("sp", lambda e: e.dma_start(out=identf, in_=ident_d), writes=["identf"], dma="c_identf")
        P.op("dve", lambda e: e.tensor_copy(ident, identf), reads=["identf"], writes=["ident"])
        P.op("pool", lambda e: e.memset(ones_f, 1.0), writes=["ones_f"])
        P.op("pool", lambda e: e.memset(ones_b, 1.0), writes=["ones_b"])
        P.op("pool", lambda e: e.memset(negh, -0.5), writes=["negh"])
        EARLY = os.environ.get("KEARLY", "1") == "1" and STOP > 5
        TAILFILL = os.environ.get("KTAILFILL", "1") == "1" and STOP > 5
        POOL_RSTD = os.environ.get("KPOOLRSTD", "1") == "1"
        POOL_RSTD_A = os.environ.get("KPOOLRSTDA", "1") == "1"
        DIRECT = os.environ.get("KDIRECT", "1") == "1"
        DIRECT_WD = os.environ.get("KDIRECTWD", "1") == "1"

        def cast_scratch():
            todo = []
            if not DIRECT:
                todo += [("wo", wo_b, w_o_d), ("wg", wg_b, w_gate_d), ("wu", wu_b, w_up_d),
                         ("wpg", wpg_b, w_pg_d), ("wpp", wpp_b, w_pp_d)]
            if not DIRECT_WD:
                todo += [("wd", wd_b, w_down_d)]
            for nm, dst, src in todo:
                P.op("pool", lambda e, dst=dst, src=src: e.dma_start(out=dst, in_=src),
                     writes=["scr_" + nm], dma="scr_" + nm)

        def prep_A():
            w_in3 = w_in_d.rearrange("(c p) n -> p c n", p=128)
            for (nm_, n0_, nn_) in (("Win_q", 512, 384), ("Win_kv", 896, 288), ("Win_p", 0, 512)):
                P.op("pool", lambda e, n0_=n0_, nn_=nn_: e.dma_start(out=Win[:, :, n0_:n0_ + nn_], in_=w_in3[:, :, n0_:n0_ + nn_]),
                     writes=[nm_], dma=nm_)
            P.op("pool", lambda e: e.dma_start(out=Wqb, in_=w_qb_d.rearrange("(c p) n -> p c n", p=128)),
                 writes=["Wqb"], dma="Wqb")
            P.op("pool", lambda e: e.dma_start(out=Wkvb, in_=w_kvb_d.rearrange("(c p) n -> p c n", p=128)),
                 writes=["Wkvb"], dma="Wkvb")
            P.op("pool", lambda e: e.dma_start(out=Wpool.rearrange("p g d -> p (g d)"), in_=w_pool_d),
                 writes=["Wpool"], dma="Wpool")
            P.op("pool", lambda e: e.dma_start(out=Band.rearrange("p m t -> p (m t)"), in_=band_d),
                 writes=["Band"], dma="Band")
            P.op("sp", lambda e: [
                e.dma_start(out=gln1, in_=ln1_d.partition_broadcast(128)),
                e.dma_start(out=gqa, in_=qan_d.partition_broadcast(128)),
                e.dma_start(out=gkva, in_=kvan_d.partition_broadcast(128)),
                e.dma_start(out=gq, in_=qn_d.partition_broadcast(128)),
                e.dma_start(out=gk, in_=kn_d.partition_broadcast(128)),
                e.dma_start(out=psc, in_=psc_d),
                e.dma_start(out=cos_t.rearrange("p t j -> p (t j)"), in_=cos_d),
                e.dma_start(out=sin_t.rearrange("p t j -> p (t j)"), in_=sin_d),
            ], writes=["gainsA"], dma="gainsA", ndma=8)

            def bc(g_ap, lo):
                return g_ap[:, lo:lo + 16].unsqueeze(1).to_broadcast([128, 16, 16])
            for (CG, SG1, SG2, g_ap, nm) in ((CGq, SG1q, SG2q, gq, "q"), (CGk, SG1k, SG2k, gk, "k")):
                P.op("dve", lambda e, CG=CG, g_ap=g_ap: e.tensor_tensor(CG[:, :, 0:16], cos_t, bc(g_ap, 64), ALU.mult),
                     reads=["gainsA"], writes=["rope" + nm])
                P.op("dve", lambda e, CG=CG, g_ap=g_ap: e.tensor_tensor(CG[:, :, 16:32], cos_t, bc(g_ap, 80), ALU.mult),
                     reads=["gainsA"], writes=["rope" + nm])
                P.op("dve", lambda e, SG2=SG2, g_ap=g_ap: e.tensor_tensor(SG2, sin_t, bc(g_ap, 64), ALU.mult),
                     reads=["gainsA"], writes=["rope" + nm])
                P.op("dve", lambda e, SG1=SG1, g_ap=g_ap: e.scalar_tensor_tensor(SG1, sin_t, -1.0, bc(g_ap, 80), ALU.mult, ALU.mult),
                     reads=["gainsA"], writes=["rope" + nm])
            P.op("pool", lambda e: e.memset(V, 0.0), writes=["V"])
            P.op("pool", lambda e: e.memset(V[:, :, :, 64:128], 1.0), reads=["V"], writes=["V"])

        def pool_tile(s, i):
            PV_ = int(os.environ.get("POOLV", "0"))
            P.stage = "B4a"
            for g in range(4):
                terms = []
                if i > 0:
                    terms.append((i - 1, 0))
                terms.append((i, 3 if i == 0 else (4 if i == NT - 1 else 1)))
                if i < NT - 1:
                    terms.append((i + 1, 2))
                for n_, (j, kind) in enumerate(terms):
                    lhs_ = ub[:, g * 128:(g + 1) * 128] if PV_ == 1 else pinb[j % 4][:, g * 128:(g + 1) * 128]
                    rhs_ = ident if PV_ == 2 else Band[:, g * 5 + kind, :]
                    P.op("pe", lambda e, g=g, lhs_=lhs_, rhs_=rhs_, n_=n_, L=len(terms): e.matmul(
                        psum[:, 6, g * 128:(g + 1) * 128], lhs_, rhs_, start=(n_ == 0), stop=(n_ == L - 1)),
                        reads=[("pin", j % 4) if PV_ != 1 else "ub", "Band" if PV_ != 2 else "ident"], writes=PS(6),
                        nosig=(n_ != len(terms) - 1))
            if PV_ == 3:
                return
            dsb_ = sqj[:, 0:512].rearrange("p (g t) -> p g t", g=4) if PV_ == 5 else dsb
            if PV_ in (0, 6):
                P.op("dve", lambda e: e.tensor_copy(dsb_, psum[:, 6, :].rearrange("p (g t) -> p g t", g=4)),
                     reads=PS(6), writes=["dsb"])
            else:
                P.op("act", lambda e: e.copy(dsb_, psum[:, 6, :].rearrange("p (g t) -> p g t", g=4)),
                     reads=PS(6), writes=["dsb"] + (["sqj"] if PV_ == 5 else []))
            if os.environ.get("POOLA"):
                return
            P.stage = "B4b"
            for g in range(4):
                P.op("pe", lambda e, g=g: e.matmul(psum[:, 7, g * 128:(g + 1) * 128], Wpool[:, g, :], dsb[:, g, :],
                                                   start=True, stop=True),
                     reads=["Wpool", "dsb"], writes=PS(7))
            for g in range(4):
                P.op("dve", lambda e, g=g: e.tensor_scalar(yT[:, g, i * 128:(i + 1) * 128],
                                                           psum[:, 7, g * 128:(g + 1) * 128], psc[:, g:g + 1], None, ALU.mult),
                     reads=PS(7) + ["gainsA"], writes=[("yT", i)])

        def phase1_tile(s, t, part):
            slot = t % 2
            xs = xt[slot]
            tp = ps_bf(0)
            cT = cT2[t % 2]
            krs = krs2[t % 2]
            RcT = ("cT", t % 2)
            Rkrs = ("krs", t % 2)
            if part == "front":
                phase1_front(s, t, slot, xs, tp, cT, krs, RcT, Rkrs)
            else:
                phase1_back(s, t, cT, krs, RcT, Rkrs)

        def phase1_front(s, t, slot, xs, tp, cT, krs, RcT, Rkrs):
            P.stage = "F1"
            P.op("sp", lambda e: e.dma_start(out=xs, in_=x[s, t * 128:(t + 1) * 128, :]),
                 writes=[("xt", slot)], dma="xt%d" % slot)
            P.op("act", lambda e: e.activation(sqj, xs, AF.Square, accum_out=st[:, 0:1]),
                 reads=[("xt", slot)], writes=["sqj", "st0"])
            rstd_from_ssq(st[:, 0:1], D, st[:, 1:2], st[:, 2:3], "st0", "st1", "st2", D)
            P.op("dve", lambda e: e.scalar_tensor_tensor(ub, xs, st[:, 2:3], gln1, ALU.mult, ALU.mult),
                 reads=[("xt", slot), "st2", "gainsA"], writes=["ub"])
            P.stage = "F2a"
            for c in range(8):
                P.op("pe", lambda e, c=c: e.transpose(tp[:, c * 128:(c + 1) * 128], ub[:, c * 128:(c + 1) * 128], ident),
                     reads=["ub", "ident"], writes=PS(0))
            P.op("act", lambda e: e.copy(uT, tp.rearrange("p (c t) -> p c t", c=8)), reads=PS(0), writes=["uT"])
            for (bk, n0, nn, wres) in ((2, 512, 384, "Win_q"), (3, 896, 288, "Win_kv"), (1, 0, 512, "Win_p")):
                for c in range(8):
                    P.op("pe", lambda e, bk=bk, n0=n0, nn=nn, c=c: e.matmul(
                        psum[:, bk, 0:nn], uT[:, c, :], Win[:, c, n0:n0 + nn], start=(c == 0), stop=(c == 7)),
                        reads=["uT", wres], writes=PS(bk))
            P.stage = "F2b"
            PIN_LATER = True
            P.op("act", lambda e: e.activation(sqj[:, 0:384], psum[:, 2, 0:384], AF.Square, accum_out=st[:, 4:5]),
                 reads=PS(2), writes=["sqj", "st4"])
            P.op("act", lambda e: e.activation(sqj[:, 0:256], psum[:, 3, 0:256], AF.Square, accum_out=st[:, 5:6]),
                 reads=PS(3), writes=["sqj", "st5"])
            P.op("act", lambda e: e.activation(st[:, 6:7], st[:, 4:5], AF.Sqrt, bias=EPS, scale=1.0 / 384),
                 reads=["st4"], writes=["st6"])
            P.op("act", lambda e: e.activation(st[:, 7:8], st[:, 5:6], AF.Sqrt, bias=EPS, scale=1.0 / 256),
                 reads=["st5"], writes=["st7"])
            P.op("dve", lambda e: e.reciprocal(st[:, 8:10], st[:, 6:8]), reads=["st6", "st7"], writes=["st8"])
            P.op("dve", lambda e: e.scalar_tensor_tensor(cb[:, 0:384], psum[:, 2, 0:384], st[:, 8:9], gqa, ALU.mult, ALU.mult),
                 reads=PS(2) + ["st8", "gainsA"], writes=["cb"])
            P.op("dve", lambda e: e.scalar_tensor_tensor(cb[:, 384:640], psum[:, 3, 0:256], st[:, 9:10], gkva, ALU.mult, ALU.mult),
                 reads=PS(3) + ["st8", "gainsA"], writes=["cb"])
            P.op("act", lambda e: e.copy(krs, psum[:, 3, 256:288]), reads=PS(3), writes=[Rkrs])
            P.op("act", lambda e: e.copy(pinb[t % 4], psum[:, 1, :]), reads=PS(1), writes=[("pin", t % 4)])
            P.stage = "F3"
            for c in range(5):
                P.op("pe", lambda e, c=c: e.transpose(tp[:, c * 128:(c + 1) * 128], cb[:, c * 128:(c + 1) * 128], ident),
                     reads=["cb", "ident"], writes=PS(0))
            P.op("act", lambda e: e.copy(cT, tp[:, 0:640].rearrange("p (c t) -> p c t", c=5)), reads=PS(0), writes=[RcT])

        def phase1_back(s, t, cT, krs, RcT, Rkrs):
            P.stage = "B1"
            for hf in range(2):
                for c in range(3):
                    P.op("pe", lambda e, hf=hf, c=c: e.matmul(psum[:, 4 + hf, 0:384], cT[:, c, :],
                                                              Wqb[:, c, hf * 384:(hf + 1) * 384], start=(c == 0), stop=(c == 2)),
                         reads=[RcT, "Wqb"], writes=PS(4 + hf))
            for hf in range(2):
                for c in range(2):
                    P.op("pe", lambda e, hf=hf, c=c: e.matmul(psum[:, 6 + hf, :], cT[:, 3 + c, :],
                                                              Wkvb[:, c, hf * 512:(hf + 1) * 512], start=(c == 0), stop=(c == 1)),
                         reads=[RcT, "Wkvb"], writes=PS(6 + hf))
            P.stage = "B2q"
            psq = psum[:, 4:6, 0:384]
            sqq3 = sqq.rearrange("p (a b) -> p a b", a=2)
            P.op("act", lambda e: e.activation(sqq3, psq, AF.Square), reads=PS(4, 2), writes=["sqq"])
            P.op("dve", lambda e: e.tensor_reduce(st[:, 16:24], sqq.rearrange("p (h d) -> p h d", h=8), AX.X, ALU.add),
                 reads=["sqq"], writes=["st16"])
            P.op("act", lambda e: e.activation(st[:, 24:32], st[:, 16:24], AF.Sqrt, bias=EPS, scale=1.0 / 96),
                 reads=["st16"], writes=["st24"])
            P.op("dve", lambda e: e.reciprocal(st[:, 32:40], st[:, 24:32]), reads=["st24"], writes=["st32"])
            Tq = sqq.rearrange("p (h d) -> p h d", h=8)
            for hf in range(2):
                P.op("dve", lambda e, hf=hf: e.tensor_tensor(
                    Tq[:, hf * 4:(hf + 1) * 4, :], psum[:, 4 + hf, 0:384].rearrange("p (h d) -> p h d", h=4),
                    st[:, 32 + hf * 4:36 + hf * 4].unsqueeze(2).to_broadcast([128, 4, 96]), ALU.mult),
                    reads=PS(4 + hf) + ["st32", "sqq"], writes=["sqq"])
            P.op("dve", lambda e: e.tensor_tensor(qfin[:, :, 0:64], Tq[:, :, 0:64],
                                                  gq[:, 0:64].unsqueeze(1).to_broadcast([128, 8, 64]), ALU.mult),
                 reads=["sqq", "gainsA"], writes=["qfin_n"])
            P.op(RE, lambda e: e.tensor_tensor(Ar, Tq[:, :, 64:96], CGq[:, t, :].unsqueeze(1).to_broadcast([128, 8, 32]), ALU.mult),
                 reads=["sqq", "ropeq"], writes=["Ar"])
            P.op(RE, lambda e: e.tensor_tensor(Br[:, :, 0:16], Tq[:, :, 80:96], SG1q[:, t, :].unsqueeze(1).to_broadcast([128, 8, 16]), ALU.mult),
                 reads=["sqq", "ropeq"], writes=["Br"])
            P.op(RE, lambda e: e.tensor_tensor(Br[:, :, 16:32], Tq[:, :, 64:80], SG2q[:, t, :].unsqueeze(1).to_broadcast([128, 8, 16]), ALU.mult),
                 reads=["sqq", "ropeq"], writes=["Br"])
            P.op(RE, lambda e: e.tensor_tensor(qfin[:, :, 64:96], Ar, Br, ALU.add), reads=["Ar", "Br"], writes=["qfin_r"])
            P.stage = "B2k"
            kv3 = psum[:, 6:8, :].rearrange("p a (h d) -> p (a h) d", d=128)
            sqk3 = sqk.rearrange("p (h d) -> p h d", h=8)
            P.op("act", lambda e: e.activation(sqk3, kv3[:, :, 0:64], AF.Square), reads=PS(6, 2), writes=["sqk"])
            P.op("dve", lambda e: e.tensor_reduce(st[:, 40:48], sqk3, AX.X, ALU.add), reads=["sqk"], writes=["st40"])
            P.op("act", lambda e: e.activation(sqjk, krs, AF.Square, accum_out=st[:, 10:11]),
                 reads=[Rkrs], writes=["sqjk", "st10"])
            kv4 = psum[:, 6:8, :].rearrange("p a (j e d) -> p (a j) e d", e=2, d=128)
            P.op("act", lambda e: e.copy(V[:, t, :, 0:64], kv4[:, :, 0, 64:128]), reads=PS(6, 2), writes=["V"])
            P.op("act", lambda e: e.copy(V[:, t, :, 128:192], kv4[:, :, 1, 64:128]), reads=PS(6, 2), writes=["V"])
            P.op("dve", lambda e: e.tensor_scalar(st[:, 40:48], st[:, 40:48], st[:, 10:11], None, ALU.add),
                 reads=["st40", "st10"], writes=["st40"])
            P.op("act", lambda e: e.activation(st[:, 48:56], st[:, 40:48], AF.Sqrt, bias=EPS, scale=1.0 / 96),
                 reads=["st40"], writes=["st48"])
            P.op("dve", lambda e: e.reciprocal(st[:, 56:64], st[:, 48:56]), reads=["st48"], writes=["st56"])
            P.op("dve", lambda e: e.tensor_tensor(sqk3, kv3[:, :, 0:64], st[:, 56:64].unsqueeze(2).to_broadcast([128, 8, 64]), ALU.mult),
                 reads=PS(6, 2) + ["st56", "sqk"], writes=["sqk"])
            P.op("dve", lambda e: e.tensor_tensor(kfin[:, :, 0:64], sqk3, gk[:, 0:64].unsqueeze(1).to_broadcast([128, 8, 64]), ALU.mult),
                 reads=["sqk", "gainsA"], writes=["kfin_n"])
            P.op(RE, lambda e: e.tensor_tensor(Tr, krs.unsqueeze(1).to_broadcast([128, 8, 32]),
                                                  st[:, 56:64].unsqueeze(2).to_broadcast([128, 8, 32]), ALU.mult),
                 reads=[Rkrs, "st56"], writes=["Tr"])
            P.op(RE, lambda e: e.tensor_tensor(Ar, Tr, CGk[:, t, :].unsqueeze(1).to_broadcast([128, 8, 32]), ALU.mult),
                 reads=["Tr", "ropek"], writes=["Ar"])
            P.op(RE, lambda e: e.tensor_tensor(Br[:, :, 0:16], Tr[:, :, 16:32], SG1k[:, t, :].unsqueeze(1).to_broadcast([128, 8, 16]), ALU.mult),
                 reads=["Tr", "ropek"], writes=["Br"])
            P.op(RE, lambda e: e.tensor_tensor(Br[:, :, 16:32], Tr[:, :, 0:16], SG2k[:, t, :].unsqueeze(1).to_broadcast([128, 8, 16]), ALU.mult),
                 reads=["Tr", "ropek"], writes=["Br"])
            P.op(RE, lambda e: e.tensor_tensor(kfin[:, :, 64:96], Ar, Br, ALU.add), reads=["Ar", "Br"], writes=["kfin_r"])
            for (src, dst, bk, nm) in ((qfin, qT, 4, "qT"), (kfin, kT, 5, "kT")):
                P.stage = "B3q" if nm == "qT" else "B3k"
                tpb = ps_bf(bk)
                for h in range(8):
                    P.op("pe", lambda e, src=src, tpb=tpb, h=h: e.transpose(tpb[0:96, h * 128:(h + 1) * 128], src[:, h, :], ident),
                         reads=[nm[0] + "fin_n", nm[0] + "fin_r", "ident"], writes=PS(bk))
                EV = os.environ.get("KEVAC", "ad")
                ev_ = EV[0] if nm == "qT" else EV[1]
                P.op("act" if ev_ == "a" else "dve",
                     lambda e, dst=dst, tpb=tpb, ev_=ev_: (e.copy if ev_ == "a" else e.tensor_copy)(
                         dst[0:96, :, t * 128:(t + 1) * 128], tpb[0:96, :].rearrange("p (h t) -> p h t", h=8)),
                     reads=PS(bk), writes=[nm])
            P.stage = "B4"
            if t >= 1 and not os.environ.get('NOPOOL'):
                pool_tile(s, t - 1)

        def attention(s):
            groups = [(h, qb, g2) for h in range(8) for qb in range(4) for g2 in range(8)]
            NG = len(groups)
            NSB = 3

            def emit_S(n):
                h, qb, g2 = groups[n]
                sb_ = (n % NSB) * 2
                for j in range(2):
                    kc = g2 * 2 + j
                    P.op("pe", lambda e, sb_=sb_, j=j, kc=kc, h=h, qb=qb: e.matmul(
                        psum[:, sb_ + j, :], kT[0:96, h, kc * 128:(kc + 1) * 128],
                        qT[0:96, h, qb * 512:(qb + 1) * 512], start=True, stop=True),
                        reads=["kT", "qT"], writes=PS(sb_ + j))

            def emit_exp(n):
                sb_ = (n % NSB) * 2
                slot = n % 3
                P.op("act", lambda e, sb_=sb_, slot=slot: e.activation(pT[slot], psum[:, sb_:sb_ + 2, :], AF.Exp, scale=ATT_SCALE),
                     reads=PS(sb_, 2), writes=[("pT", slot)])

            def emit_PV(n):
                h, qb, g2 = groups[n]
                it = n // 8
                pair, odd = h // 2, h % 2
                ob = 6 + (it % 2)
                slot = n % 3
                for j in range(2):
                    kc = g2 * 2 + j
                    lhsT = V[:, kc, pair, 64:192] if odd else V[:, kc, pair, 0:128]
                    P.op("pe", lambda e, lhsT=lhsT, ob=ob, slot=slot, j=j, kc=kc: e.matmul(
                        psum[:, ob, :], lhsT, pT[slot][:, j, :], start=(kc == 0), stop=(kc == 15)),
                        reads=["V", ("pT", slot)], writes=PS(ob))

            def norm(it):
                h, qb, _ = groups[it * 8]
                pair, odd = h // 2, h % 2
                ob = 6 + (it % 2)
                r = it % 2
                orow = slice(64, 128) if odd else slice(0, 64)
                drow = slice(0, 64) if odd else slice(64, 128)
                P.op("dve", lambda e, r=r, ob=ob, orow=orow, drow=drow: e.reciprocal(bsb[r][orow, :], psum[drow, ob, :]),
                     reads=PS(ob), writes=[("bsb", r)])
                P.op("dve", lambda e, r=r, ob=ob, orow=orow, pair=pair, qb=qb: e.tensor_tensor(
                    yT[orow, 4 + pair, qb * 512:(qb + 1) * 512], psum[orow, ob, :], bsb[r][orow, :], ALU.mult),
                    reads=PS(ob) + [("bsb", r)], writes=[("yTa", h, qb)])

            for n0 in range(NSB):
                emit_S(n0)
            for n in range(NG):
                emit_exp(n)
                if n + NSB < NG:
                    emit_S(n + NSB)
                emit_PV(n)
                if n % 8 == 7:
                    norm(n // 8)

        stream_items = []
        state = {"next_load": 0, "next_use": 0, "consumed": 0}

        def ring_load(idx, srcs):
            slot = idx % RING
            def fn(e, slot=slot, srcs=srcs):
                return [e.dma_start(out=o_(ring[slot]), in_=i_) for (o_, i_) in srcs]
            P.op("pool", fn, reads=([] if DIRECT else [s_ for s_ in ("scr_wo", "scr_wg", "scr_wu", "scr_wpg", "scr_wpp")]),
                 writes=[("ring", slot)], dma="ring%d" % slot, ndma=len(srcs), extra=state.get("extra"))

        def block_stream():
            items = []
            wo3 = (w_o_d if DIRECT else wo_b).rearrange("(c p) n -> p c n", p=128)
            for hh in range(2):
                items.append([(lambda r: r.rearrange("p (c n) -> p c n", c=4), wo3[:, hh * 4:(hh + 1) * 4, :])])
            wg3 = (w_gate_d if DIRECT else wg_b).rearrange("(c p) n -> p c n", p=128)
            wu3 = (w_up_d if DIRECT else wu_b).rearrange("(c p) n -> p c n", p=128)
            for fp in range(NF // 2):
                items.append([
                    (lambda r: r[:, 0:2048].rearrange("p (c n) -> p c n", c=8), wg3[:, :, fp * 256:(fp + 1) * 256]),
                    (lambda r: r[:, 2048:4096].rearrange("p (c n) -> p c n", c=8), wu3[:, :, fp * 256:(fp + 1) * 256]),
                ])
            wpg3 = (w_pg_d if DIRECT else wpg_b).rearrange("(c p) n -> p c n", p=128)
            for hh in range(2):
                items.append([(lambda r: r.rearrange("p (c n) -> p c n", c=4), wpg3[:, hh * 4:(hh + 1) * 4, :])])
            wpp3 = (w_pp_d if DIRECT else wpp_b).rearrange("(c p) n -> p c n", p=128)
            items.append([(lambda r: r[:, 0:2048].rearrange("p (c n) -> p c n", c=2), wpp3)])
            return items

        def ensure_loaded(upto):
            while state["next_load"] <= upto and state["next_load"] < len(stream_items):
                assert state["next_load"] < state["consumed"] + RING, "ring slot still has un-emitted consumers"
                ring_load(state["next_load"], stream_items[state["next_load"]])
                state["next_load"] += 1

        def use_item():
            idx = state["next_use"]
            state["next_use"] += 1
            ensure_loaded(idx)
            return idx % RING

        def done_items(k):
            state["consumed"] += k
            ensure_loaded(state["consumed"] + RING - 1)

        def load_Wdown():
            assert 45056 <= 18944 + 4608 + 4096 + 1024 + 5120 + 4096 + 1536 + 1024 + 384 + 384 + 32 + 1024 + 1024 + 2048 + 2048
            P.op("pool", lambda e: e.dma_start(out=Wdown, in_=(w_down_d if DIRECT_WD else wd_b).rearrange("(c p) n -> p c n", p=128)),
                 reads=([] if DIRECT_WD else ["scr_wd"]),
                 writes=["Wdown", "Win_q", "Win_kv", "Win_p", "Wqb", "Wkvb", "Wpool", "Band", "gainsA", "ropeq", "ropek"], dma="Wdown")

        def prep_B():
            load_Wdown()
            P.op("sp", lambda e: [e.dma_start(out=gln2, in_=ln2_d.partition_broadcast(128)),
                                  e.dma_start(out=gple, in_=plen_d.partition_broadcast(128))],
                 writes=["gainsB"], dma="gainsB", ndma=2)

        def norm_chain(src, gain, res_src, ui, pool_rstd=False):
            P.op("act", lambda e: e.activation(sqjB, src, AF.Square, accum_out=st[:, 0:1]),
                 reads=[res_src], writes=["sqjB", "st0"])
            if pool_rstd:
                P.op("pool", lambda e: e.tensor_scalar(st[:, 1:2], st[:, 0:1], 1.0 / D, EPS, ALU.mult, ALU.add),
                     reads=["st0"], writes=["st1"])
                P.op("pool", lambda e: e.tensor_tensor(st[:, 2:3], st[:, 1:2], negh[:, 0:1], ALU.pow),
                     reads=["st1", "negh"], writes=["st2"])
            else:
                rstd_from_ssq(st[:, 0:1], D, st[:, 1:2], st[:, 2:3], "st0", "st1", "st2", D)
            P.op("dve", lambda e: e.scalar_tensor_tensor(ubB[ui], src, st[:, 2:3], gain, ALU.mult, ALU.mult),
                 reads=[res_src, "st2", "gainsB"], writes=[("ubB", ui)])

        def transp8(dstT, res_dst, ui):
            tp = ps_bf(0)
            for c in range(8):
                P.op("pe", lambda e, c=c: e.transpose(tp[:, c * 128:(c + 1) * 128], ubB[ui][:, c * 128:(c + 1) * 128], ident),
                     reads=[("ubB", ui), "ident"], writes=PS(0))
            P.op("act", lambda e: e.copy(dstT, tp.rearrange("p (c t) -> p c t", c=8)), reads=PS(0), writes=[res_dst])

        def phase4_block(s, b):
            T0 = b * 512
            h4v = lambda tt: h4[:, tt, :].rearrange("p (a n) -> p a n", a=2)
            def load_x(bb, tt):
                tok_ = bb * 512 + tt * 128
                P.op("sp", lambda e, tt=tt, tok_=tok_: e.dma_start(out=h4[:, tt, :], in_=x[s, tok_:tok_ + 128, :]),
                     writes=[("h4", tt)], dma="h4_%d" % tt, extra=state.get("extra"))

            if b == -1:
                for tt in range(4):
                    load_x(0, tt)
                return

            def load_p(bb):
                P.op("sp", lambda e, bb=bb: e.dma_start(
                    out=ptl4, in_=pin_d[s, bb * 512:(bb + 1) * 512, :].rearrange("(t p) n -> p t n", p=128)),
                    writes=["ptl4"], dma="ptl4")

            if b == 0 or STOP == 5:
                if not EARLY:
                    for tt in range(4):
                        load_x(b, tt)
                load_p(b)
            so = [use_item(), use_item()]

            def mix(tt):
                tok = T0 + tt * 128
                setb = 1 + 2 * (tt % 2)
                for hf in range(2):
                    for c in range(8):
                        P.op("pe", lambda e, hf=hf, c=c, tok=tok, setb=setb: e.matmul(
                            psum[:, setb + hf, :], yT[:, c, tok:tok + 128],
                            ring[so[c // 4]].rearrange("p (c n) -> p c n", c=4)[:, c % 4, hf * 512:(hf + 1) * 512],
                            start=(c == 0), stop=(c == 7)),
                            reads=[("yT", tok // 128)] + [("yTa", hh, b) for hh in range(8)] + [("ring", so[c // 4])],
                            writes=PS(setb + hf))

            def add_a(tt):
                setb = 1 + 2 * (tt % 2)
                P.op("dve", lambda e, tt=tt, setb=setb: e.tensor_tensor(h4v(tt), h4v(tt), psum[:, setb:setb + 2, :], ALU.add),
                     reads=PS(setb, 2) + [("h4", tt)], writes=[("h4", tt)])

            def rest_a(tt):
                norm_chain(h4[:, tt, :], gln2, ("h4", tt), tt % 2, pool_rstd=POOL_RSTD_A)

            def Ta(tt):
                transp8(u2T[:, :, tt * 128:(tt + 1) * 128], "u2T", tt % 2)

            mix(0); add_a(0)
            mix(1); add_a(1); rest_a(0)
            mix(2); add_a(2); rest_a(1)
            Ta(0)
            mix(3); add_a(3); rest_a(2)
            Ta(1)
            rest_a(3)
            sl0 = use_item()
            rg0 = ring[sl0][:, 0:2048].rearrange("p (c n) -> p c n", c=8)
            ru0 = ring[sl0][:, 2048:4096].rearrange("p (c n) -> p c n", c=8)

            def gu0_half(hh):
                for (bk, rw) in ((5, rg0), (6, ru0)):
                    for c in range(8):
                        P.op("pe", lambda e, bk=bk, rw=rw, c=c, hh=hh: e.matmul(
                            psum[:, bk, hh * 256:(hh + 1) * 256], rw[:, c, 0:128], u2T[:, c, hh * 256:(hh + 1) * 256],
                            start=(c == 0), stop=(c == 7)),
                            reads=["u2T", ("ring", sl0)], writes=PS(bk))
            if TAILFILL:
                gu0_half(0)
            Ta(2)
            Ta(3)
            if TAILFILL:
                gu0_half(1)
            done_items(2)
            if STOP == 5:
                for tt in range(4):
                    tok = T0 + tt * 128
                    P.op("sp", lambda e, tt=tt, tok=tok: e.dma_start(out=y[s, tok:tok + 128, :], in_=h4[:, tt, :]),
                         reads=[("h4", tt)], dma="yo%d" % tt)
                state["next_use"] = state["next_load"] = state["consumed"] = len(stream_items)
                return
            for fp in range(NF // 2):
                sl = sl0 if fp == 0 else use_item()
                rg = ring[sl][:, 0:2048].rearrange("p (c n) -> p c n", c=8)
                ru = ring[sl][:, 2048:4096].rearrange("p (c n) -> p c n", c=8)
                for j in range(2):
                    f = fp * 2 + j
                    for (bk, rw) in ((5, rg), (6, ru)):
                        if TAILFILL and f == 0:
                            continue
                        for c in range(8):
                            P.op("pe", lambda e, bk=bk, rw=rw, c=c, j=j: e.matmul(
                                psum[:, bk, :], rw[:, c, j * 128:(j + 1) * 128], u2T[:, c, :], start=(c == 0), stop=(c == 7)),
                                reads=["u2T", ("ring", sl)], writes=PS(bk))
                    P.op("act", lambda e, f=f: e.activation(sgb[f % 2], psum[:, 5, :], AF.Silu),
                         reads=PS(5), writes=[("sgb", f % 2)])
                    P.op("dve", lambda e, f=f: e.tensor_tensor(actT[:, f, :], sgb[f % 2], psum[:, 6, :], ALU.mult),
                         reads=PS(6) + [("sgb", f % 2)], writes=["actT"])
                done_items(1)
            sg_ = [use_item(), use_item()]
            sp_ = use_item()
            P.op("dve", lambda e: e.tensor_copy(pb4, ptl4), reads=["ptl4"], writes=["pb4"])
            if b < 3:
                load_p(b + 1)
            tp7 = ps_bf(7)
            for tt in range(4):
                for c in range(2):
                    P.op("pe", lambda e, tt=tt, c=c: e.transpose(tp7[:, (tt * 2 + c) * 128:(tt * 2 + c + 1) * 128],
                                                                 pb4[:, tt, c * 128:(c + 1) * 128], ident),
                         reads=["pb4", "ident"], writes=PS(7))

            def down(tt):
                for hf in range(2):
                    for f in range(NF):
                        P.op("pe", lambda e, hf=hf, f=f, tt=tt: e.matmul(
                            psum[:, 1 + hf, :], actT[:, f, tt * 128:(tt + 1) * 128],
                            Wdown[:, f, hf * 512:(hf + 1) * 512], start=(f == 0), stop=(f == NF - 1)),
                            reads=["actT", "Wdown"], writes=PS(1 + hf))

            def chain_d(tt):
                for hf in range(2):
                    P.op("dve", lambda e, tt=tt, hf=hf: e.tensor_tensor(h4[:, tt, hf * 512:(hf + 1) * 512], h4[:, tt, hf * 512:(hf + 1) * 512],
                                                                        psum[:, 1 + hf, :], ALU.add),
                         reads=PS(1 + hf) + [("h4", tt)], writes=[("h4", tt)])
                norm_chain(h4[:, tt, :], gple, ("h4", tt), tt % 2, pool_rstd=POOL_RSTD)

            def Td(tt):
                transp8(u3T[tt % 2], ("u3T", tt % 2), tt % 2)

            def ple(tt):
                tok = T0 + tt * 128
                for hf in range(2):
                    for c in range(8):
                        P.op("pe", lambda e, hf=hf, c=c, tt=tt: e.matmul(
                            psum[:, 3 + hf, :], u3T[tt % 2][:, c, :],
                            ring[sg_[c // 4]].rearrange("p (c n) -> p c n", c=4)[:, c % 4, hf * 512:(hf + 1) * 512],
                            start=(c == 0), stop=(c == 7)),
                            reads=[("u3T", tt % 2), ("ring", sg_[c // 4])], writes=PS(3 + hf))
                for hf in range(2):
                    for c in range(2):
                        P.op("pe", lambda e, hf=hf, c=c, tt=tt: e.matmul(
                            psum[:, 5 + hf, :], pTt4[:, tt, c, :],
                            ring[sp_][:, 0:2048].rearrange("p (c n) -> p c n", c=2)[:, c, hf * 512:(hf + 1) * 512],
                            start=(c == 0), stop=(c == 1)),
                            reads=["pTt4", ("ring", sp_)], writes=PS(5 + hf))
                g2 = gsig.rearrange("p (a n) -> p a n", a=2)
                for hf in range(2):
                    P.op("act", lambda e, hf=hf: e.activation(gsig[:, hf * 512:(hf + 1) * 512], psum[:, 3 + hf, :], AF.Tanh, scale=0.5),
                         reads=PS(3 + hf), writes=["gsig"])
                P.op("dve", lambda e: e.scalar_tensor_tensor(g2, g2, 1.0, psum[:, 5:7, :], ALU.add, ALU.mult),
                     reads=["gsig"] + PS(5, 2), writes=["gsig"])
                P.op("dve", lambda e, tt=tt: e.scalar_tensor_tensor(h4[:, tt, :], gsig, 0.5, h4[:, tt, :], ALU.mult, ALU.add),
                     reads=["gsig", ("h4", tt)], writes=[("h4", tt)])
                P.op("sp", lambda e, tt=tt, tok=tok: e.dma_start(out=y[s, tok:tok + 128, :], in_=h4[:, tt, :]),
                     reads=[("h4", tt)], dma="yo%d" % tt)
                if b < 3:
                    load_x(b + 1, tt)

            down(0)
            P.op("dve", lambda e: e.tensor_copy(pTt4.rearrange("p t c n -> p (t c n)"), tp7), reads=PS(7), writes=["pTt4"])
            chain_d(0)
            down(1); chain_d(1)
            Td(0)
            down(2); chain_d(2)
            ple(0)
            Td(1)
            down(3); chain_d(3)
            ple(1)
            Td(2)
            ple(2)
            Td(3)
            ple(3)
            done_items(3)

        for s in range(NSEQ):
            if s > 0:
                P.barrier()
            if STOP <= 0:
                break
            if s == 0:
                cast_scratch()
            prep_A()
            if s == 0 and not (DIRECT and DIRECT_WD):
                P.barrier()
            if STOP <= 1:
                break
            NTL = int(os.environ.get("KNT", NT)) if STOP > 2 else 2
            def ph1(t, part, stages):
                P.filter = set(stages)
                phase1_tile(s, t, part)
                P.filter = None
                P.stage = None
            ph1(0, "front", ("F1", "F2a", "F2b", "F3"))
            for t in range(NTL):
                nxt = t + 1 < NTL
                if nxt:
                    ph1(t + 1, "front", ("F1",))
                ph1(t, "back", ("B1",))
                ph1(t, "back", ("B2q",))
                if nxt:
                    ph1(t + 1, "front", ("F2a",))
                ph1(t, "back", ("B2k",))
                if nxt:
                    ph1(t + 1, "front", ("F2b",))
                ph1(t, "back", ("B3q",))
                if nxt:
                    ph1(t + 1, "front", ("F3",))
                ph1(t, "back", ("B4a",))
                ph1(t, "back", ("B3k",))
                ph1(t, "back", ("B4b",))
            if NTL == NT:
                pool_tile(s, NT - 1)
                P.stage = None
            if STOP <= 3:
                break
            def start_stream():
                base = len(stream_items)
                for b in range(4):
                    stream_items.extend(block_stream())
                assert state["next_load"] == base and state["next_use"] == base and state["consumed"] == base
                done_items(0)
            if EARLY:
                state["extra"] = P._all_last()
                start_stream()
                phase4_block(s, -1)
                state["extra"] = None
            attention(s)
            if STOP <= 4:
                break
            P.barrier()
            if not EARLY:
                start_stream()
            prep_B()
            for b in range(4 if STOP != 5 else 1):
                phase4_block(s, b)
            assert state["next_use"] == len(stream_items) and state["next_load"] == len(stream_items)
        if DBG:
            P.barrier()
            P.op("pool", lambda e: e.dma_start(out=dbg_y, in_=yT), dma="dbg_y")
            P.op("pool", lambda e: e.dma_start(out=dbg_q, in_=qT), dma="dbg_q")
            P.op("pool", lambda e: e.dma_start(out=dbg_k, in_=kT), dma="dbg_k")
            P.op("pool", lambda e: e.dma_start(out=dbg_v, in_=V), dma="dbg_v")
        P.finish()
        nw, counts = P.emit(lambda name: es.enter_context(nc.semaphore(name)))
        print("ops", len(P.ops), "waits", nw, "sems", len(counts), {k: v for k, v in counts.items() if k[0] == "eng"})
    return nc


def _consts():
    ident = np.eye(128, dtype=np.float32)
    band = np.zeros((20, 128, 128), np.float32)
    for g, w in enumerate(POOL_WINDOWS):
        half = w // 2
        Bf = np.zeros((S, S), np.float32)
        tt = np.arange(S)
        lo = np.clip(tt - half, 0, S)
        hi = np.clip(tt - half + w, 0, S)
        for t_ in range(S):
            Bf[lo[t_]:hi[t_], t_] = np.float32(1.0) / np.float32(hi[t_] - lo[t_])
            Bf[t_, t_] -= 1.0
        band[g * 5 + 0] = Bf[0:128, 128:256]
        band[g * 5 + 1] = Bf[128:256, 128:256]
        band[g * 5 + 2] = Bf[256:384, 128:256]
        band[g * 5 + 3] = Bf[0:128, 0:128]
        band[g * 5 + 4] = Bf[S - 128:, S - 128:]
    inv = np.float32(10000.0) ** (-np.arange(0, 32, 2, dtype=np.float32) / np.float32(32))
    ang = np.arange(S, dtype=np.float32)[:, None] * inv[None, :].astype(np.float32)
    cosT = np.cos(ang).astype(np.float32)
    sinT = np.sin(ang).astype(np.float32)
    band = np.ascontiguousarray(band.transpose(1, 0, 2).reshape(128, 20 * 128))
    cosT = np.ascontiguousarray(cosT.reshape(NT, 128, 16).transpose(1, 0, 2).reshape(128, 256))
    sinT = np.ascontiguousarray(sinT.reshape(NT, 128, 16).transpose(1, 0, 2).reshape(128, 256))
    return ident, band, cosT, sinT


_NC_CACHE = {}


def kernel(x_prompt, x_sample, p_prompt, p_sample, ln1, w_in, w_pool, pool_scale, q_a_norm, w_qb,
           kv_a_norm, w_kvb, q_norm, k_norm, w_o, ln2, w_gate, w_up, w_down, ple_norm, w_ple_gate,
           w_ple_proj):
    f = lambda a: np.ascontiguousarray(np.asarray(a, dtype=np.float32))
    X = np.concatenate([f(x_prompt), f(x_sample)], axis=0)
    Pm = np.concatenate([f(p_prompt)[0], f(p_sample)[0]], axis=0)
    ident, band, cosT, sinT = _consts()
    common = {
        "ln1": f(ln1).reshape(1, D), "ln2": f(ln2).reshape(1, D), "ple_norm": f(ple_norm).reshape(1, D),
        "q_a_norm": f(q_a_norm).reshape(1, 384), "kv_a_norm": f(kv_a_norm).reshape(1, 256),
        "q_norm": f(q_norm).reshape(1, 96), "k_norm": f(k_norm).reshape(1, 96),
        "pool_scale": np.ascontiguousarray(f(pool_scale).reshape(4, 128).T),
        "w_in": f(w_in)[0], "w_pool": np.ascontiguousarray(f(w_pool)[0].transpose(1, 0, 2).reshape(128, 512)), "w_qb": f(w_qb)[0], "w_kvb": f(w_kvb)[0],
        "w_o": f(w_o)[0], "w_gate": f(w_gate)[0], "w_up": f(w_up)[0], "w_down": f(w_down)[0],
        "w_ple_gate": f(w_ple_gate)[0], "w_ple_proj": f(w_ple_proj)[0],
        "ident": ident, "band": band, "cosT": cosT, "sinT": sinT,
    }
    if "nc" not in _NC_CACHE:
        _NC_CACHE["nc"] = build_nc(NSEQ_CORE)
    nc = _NC_CACHE["nc"]
    in_maps = []
    for c in range(NCORES):
        m = dict(common)
        m["x"] = np.ascontiguousarray(X[c * NSEQ_CORE:(c + 1) * NSEQ_CORE])
        m["p"] = np.ascontiguousarray(Pm[c * NSEQ_CORE:(c + 1) * NSEQ_CORE])
        in_maps.append(m)
    res = run_bass_kernel_spmd(nc, in_maps, core_ids=list(range(NCORES)))
    Y = np.concatenate([r["y"] for r in res.results], axis=0)
    nb = x_prompt.shape[0]
    return (np.ascontiguousarray(Y[:nb]).astype(np.float32), np.ascontiguousarray(Y[nb:]).astype(np.float32))
```

```python
import math
import os
from contextlib import ExitStack
import numpy as np
import concourse.bass as bass
import concourse.mybir as mybir
from concourse.bass_utils import run_bass_kernel_spmd

F32 = mybir.dt.float32
BF16 = mybir.dt.bfloat16
U8 = mybir.dt.uint8
AF = mybir.ActivationFunctionType
ALU = mybir.AluOpType
AX = mybir.AxisListType

S = 2048
D = 1024
NT = S // 128
IN_W = 1184
DFF = 2816
NF = DFF // 128
EPS = 1e-6
ATT_SCALE = 1.0 / math.sqrt(96.0)
POOL_WINDOWS = (2, 4, 8, 16)
NCORES = 8
NSEQ_CORE = 3


SAME_ENG_EDGES = bool(os.environ.get("KSE"))


class _Op:
    __slots__ = ("eng", "fn", "deps", "is_dma", "semkey", "ndma", "signal", "token", "idx", "nosig")


class Prog:
    ENGS = ("pe", "act", "dve", "pool", "sp")

    def __init__(self, nc):
        self.nc = nc
        self.engs = {"pe": nc.tensor, "act": nc.scalar, "dve": nc.vector,
                     "pool": nc.gpsimd, "sp": nc.sync}
        self.ops = []
        self.last_writer = {}
        self.readers = {}
        self.pending = {e: set() for e in self.ENGS}
        self.filter = None
        self.stage = None

    def op(self, eng, fn, reads=(), writes=(), dma=None, ndma=1, nosig=False, extra=None):
        if self.filter is not None and self.stage not in self.filter:
            return None
        o = _Op()
        o.nosig = nosig
        o.eng = eng
        o.fn = fn
        o.is_dma = dma is not None
        o.semkey = dma
        o.ndma = ndma
        o.signal = o.is_dma
        o.token = None
        o.idx = len(self.ops)
        deps = set()
        ops = self.ops
        for r in reads:
            w = self.last_writer.get(r)
            if w is not None:
                wo = ops[w]
                if not (wo.eng == "pe" and eng == "pe" and not wo.is_dma and not o.is_dma):
                    deps.add(w)
        for r in writes:
            w = self.last_writer.get(r)
            if w is not None:
                wo = ops[w]
                if wo.is_dma or o.is_dma or wo.eng != eng or (SAME_ENG_EDGES and eng != "pe"):
                    deps.add(w)
            for rd in self.readers.get(r, {}).values():
                ro = ops[rd]
                if ro.is_dma or o.is_dma or ro.eng != eng or (SAME_ENG_EDGES and eng != "pe"):
                    deps.add(rd)
        if self.pending[eng]:
            deps |= self.pending[eng]
            self.pending[eng] = set()
        if extra:
            deps |= set(extra)
        o.deps = deps
        rkey = ("dma", dma) if o.is_dma else eng
        for r in reads:
            self.readers.setdefault(r, {})[rkey] = o.idx
        for r in writes:
            self.last_writer[r] = o.idx
            self.readers[r] = {}
        ops.append(o)
        return o.idx

    def _all_last(self):
        last = {}
        for o in self.ops:
            if o.is_dma:
                last[("dma", o.semkey)] = o.idx
            else:
                last[o.eng] = o.idx
        return set(last.values())

    def barrier(self):
        deps = self._all_last()
        for e in self.ENGS:
            self.pending[e] = set(deps)

    def finish(self):
        deps = self._all_last()
        self.pending["sp"] = set(deps)
        self.op("sp", lambda e: None)

    def emit(self, get_sem):
        ops = self.ops
        nxt = {}
        last_ok = {}
        for o in reversed(ops):
            if o.is_dma:
                continue
            if o.nosig:
                nxt[o.idx] = last_ok.get(o.eng)
            else:
                last_ok[o.eng] = o.idx
        for o in ops:
            nd = set()
            for d in o.deps:
                if ops[d].nosig:
                    r = nxt[d]
                    assert r is not None and r < o.idx, ("cannot redirect nosig dep", d, r, o.idx)
                    nd.add(r)
                else:
                    nd.add(d)
            o.deps = nd
        for o in ops:
            for d in o.deps:
                ops[d].signal = True
        counts = {}
        for o in ops:
            if o.is_dma:
                key = ("dma", o.semkey)
                counts[key] = counts.get(key, 0) + 16 * o.ndma
                o.token = (key, counts[key])
            elif o.signal:
                key = ("eng", o.eng)
                counts[key] = counts.get(key, 0) + 1
                o.token = (key, counts[key])
        sems = {key: get_sem("s_" + "_".join(str(k) for k in key)) for key in counts}
        eng_know = {e: {} for e in self.ENGS}
        know = [None] * len(ops)
        prev_dma_tok = {}
        nwaits = 0
        plan = {e: [] for e in self.ENGS}
        for o in ops:
            ek = eng_know[o.eng]
            need = {}
            for d in o.deps:
                k, v = ops[d].token
                if ek.get(k, 0) < v and need.get(k, 0) < v:
                    need[k] = v
            if need:
                newk = dict(ek)
                for d in o.deps:
                    for k, v in know[d].items():
                        if newk.get(k, 0) < v:
                            newk[k] = v
                for k in list(need.keys()):
                    v = need[k]
                    for d in o.deps:
                        tk, tv = ops[d].token
                        if tk == k and tv >= v:
                            continue
                        if know[d].get(k, 0) >= v:
                            del need[k]
                            break
                nwaits += len(need)
                eng_know[o.eng] = newk
                ek = newk
            if o.is_dma:
                key = o.token[0]
                pv = prev_dma_tok.get(key, 0)
                assert ek.get(key, 0) >= pv, f"DMA slot {key} reissued while previous group may be in flight"
                prev_dma_tok[key] = o.token[1]
            plan[o.eng].append((o, list(need.items())))
            if o.is_dma or o.signal:
                kk = dict(ek)
                kk[o.token[0]] = o.token[1]
                know[o.idx] = kk
            else:
                know[o.idx] = ek

        semv = {k: 0 for k in counts}
        ptr = {e: 0 for e in self.ENGS}
        progress = True
        while progress:
            progress = False
            for e_ in self.ENGS:
                while ptr[e_] < len(plan[e_]):
                    o, waits = plan[e_][ptr[e_]]
                    if all(semv[k] >= v for k, v in waits):
                        if o.is_dma:
                            semv[o.token[0]] += 16 * o.ndma
                        elif o.signal:
                            semv[o.token[0]] += 1
                        ptr[e_] += 1
                        progress = True
                    else:
                        break
        for e_ in self.ENGS:
            assert ptr[e_] == len(plan[e_]), ("DEADLOCK", e_, ptr[e_], len(plan[e_]), plan[e_][ptr[e_]][1], semv)

        def mk(engname):
            def body(eng):
                for (o, waits) in plan[engname]:
                    for k, v in waits:
                        eng.wait_ge(sems[k], v)
                    res = o.fn(eng)
                    if o.is_dma:
                        if not isinstance(res, (list, tuple)):
                            res = [res]
                        assert len(res) == o.ndma, (len(res), o.ndma)
                        for r in res:
                            r.then_inc(sems[o.token[0]], 16)
                    elif o.signal:
                        res.then_inc(sems[o.token[0]], 1)
            return body

        with self.nc.Block() as block:
            block.tensor(mk("pe"))
            block.scalar(mk("act"))
            block.vector(mk("dve"))
            block.gpsimd(mk("pool"))
            block.sync(mk("sp"))
        return nwaits, counts


class Arena:
    def __init__(self, big, limit):
        self.big = big
        self.off = 0
        self.limit = limit
        self.peak = 0

    def t(self, shape, dt):
        esz = 4 if dt == F32 else 2
        n = esz
        for d in shape[1:]:
            n *= d
        off = self.off
        self.off += (n + 31) // 32 * 32
        self.peak = max(self.peak, self.off)
        assert self.off <= self.limit, (self.off, self.limit)
        ap = self.big[:, off:off + n].bitcast(dt)
        if len(shape) == 3:
            ap = ap.rearrange("p (a b) -> p a b", a=shape[1])
        elif len(shape) == 4:
            ap = ap.rearrange("p (a b c) -> p a b c", a=shape[1], b=shape[2])
        return ap


def build_nc(NSEQ=NSEQ_CORE):
    nc = bass.Bass("TRN2", target_bir_lowering=False)

    def din(name, shape):
        return nc.dram_tensor(name, shape, F32, kind="ExternalInput").ap()

    x = din("x", [NSEQ, S, D])
    pin_d = din("p", [NSEQ, S, 256])
    ln1_d = din("ln1", [1, D])
    ln2_d = din("ln2", [1, D])
    plen_d = din("ple_norm", [1, D])
    qan_d = din("q_a_norm", [1, 384])
    kvan_d = din("kv_a_norm", [1, 256])
    qn_d = din("q_norm", [1, 96])
    kn_d = din("k_norm", [1, 96])
    psc_d = din("pool_scale", [128, 4])
    w_in_d = din("w_in", [D, IN_W])
    w_pool_d = din("w_pool", [128, 512])
    w_qb_d = din("w_qb", [384, 768])
    w_kvb_d = din("w_kvb", [256, 1024])
    w_o_d = din("w_o", [D, D])
    w_gate_d = din("w_gate", [D, DFF])
    w_up_d = din("w_up", [D, DFF])
    w_down_d = din("w_down", [DFF, D])
    w_pg_d = din("w_ple_gate", [D, D])
    w_pp_d = din("w_ple_proj", [256, D])
    ident_d = din("ident", [128, 128])
    band_d = din("band", [128, 2560])
    cos_d = din("cosT", [128, 256])
    sin_d = din("sinT", [128, 256])
    y = nc.dram_tensor("y", [NSEQ, S, D], F32, kind="ExternalOutput").ap()
    DBG = bool(os.environ.get("KDBG"))
    if DBG:
        dbg_y = nc.dram_tensor("dbg_y", [128, 8, S], F32, kind="ExternalOutput").ap()
        dbg_q = nc.dram_tensor("dbg_q", [128, 8, S], F32, kind="ExternalOutput").ap()
        dbg_k = nc.dram_tensor("dbg_k", [128, 8, S], F32, kind="ExternalOutput").ap()
        dbg_v = nc.dram_tensor("dbg_v", [128, 16, 4, 192], F32, kind="ExternalOutput").ap()

    def dscr(name, shape):
        return nc.dram_tensor(name, shape, BF16, kind="Internal").ap()

    wo_b = dscr("wo_b", [D, D])
    wg_b = dscr("wg_b", [D, DFF])
    wu_b = dscr("wu_b", [D, DFF])
    wd_b = dscr("wd_b", [DFF, D])
    wpg_b = dscr("wpg_b", [D, D])
    wpp_b = dscr("wpp_b", [256, D])

    es = ExitStack()
    P = Prog(nc)
    with es:
        TOT = 212000
        big = es.enter_context(nc.sbuf_tensor("big", [128, TOT], U8))
        psum = es.enter_context(nc.psum_tensor("psum", [128, 8, 512], F32))
        A = Arena(big, TOT)

        def PS(b, n=1):
            return [("ps", b + i) for i in range(n)]

        def ps_bf(b):
            return psum[:, b, :].bitcast(BF16)

        ident = A.t([128, 128], BF16)
        identf = A.t([128, 128], F32)
        ones_f = A.t([128, 128], F32)
        ones_b = A.t([128, 128], BF16)
        yT = A.t([128, 8, S], BF16)
        st = A.t([128, 64], F32)
        negh = A.t([128, 8], F32)
        region0 = A.off

        Win = A.t([128, 8, IN_W], BF16)
        Wqb = A.t([128, 3, 768], BF16)
        Wkvb = A.t([128, 2, 1024], BF16)
        Wpool = A.t([128, 4, 128], BF16)
        Band = A.t([128, 20, 128], BF16)
        gln1 = A.t([128, D], F32)
        gqa = A.t([128, 384], F32)
        gkva = A.t([128, 256], F32)
        gq = A.t([128, 96], F32)
        gk = A.t([128, 96], F32)
        psc = A.t([128, 4], F32)
        cos_t = A.t([128, 16, 16], F32)
        sin_t = A.t([128, 16, 16], F32)
        CGq = A.t([128, 16, 32], F32)
        CGk = A.t([128, 16, 32], F32)
        SG1q = A.t([128, 16, 16], F32)
        SG2q = A.t([128, 16, 16], F32)
        SG1k = A.t([128, 16, 16], F32)
        SG2k = A.t([128, 16, 16], F32)
        qT = A.t([128, 8, S], BF16)
        kT = A.t([128, 8, S], BF16)
        V = A.t([128, 16, 4, 192], BF16)
        work0 = A.off
        xt = [A.t([128, D], F32) for _ in range(2)]
        sqj = A.t([128, D], BF16)
        ub = A.t([128, D], BF16)
        uT = A.t([128, 8, 128], BF16)
        pinb = [A.t([128, 512], BF16) for _ in range(4)]
        cb = A.t([128, 640], BF16)
        cT2 = [A.t([128, 5, 128], BF16) for _ in range(2)]
        krs2 = [A.t([128, 32], F32) for _ in range(2)]
        sqjk = A.t([128, 32], BF16)
        sqq = A.t([128, 768], F32)
        sqk = A.t([128, 512], F32)
        Tr = A.t([128, 8, 32], F32)
        Ar = A.t([128, 8, 32], F32)
        Br = A.t([128, 8, 32], F32)
        qfin = A.t([128, 8, 96], BF16)
        kfin = A.t([128, 8, 96], BF16)
        dsb = A.t([128, 4, 128], BF16)
        endA = A.off
        A.off = work0
        pT = [A.t([128, 2, 512], BF16) for _ in range(3)]
        rdh = [A.t([128, 512], BF16) for _ in range(2)]
        rdl = [A.t([128, 512], BF16) for _ in range(2)]
        bsb = [A.t([128, 512], F32) for _ in range(2)]
        assert A.off <= endA

        A.off = region0
        RING = 6
        ring = [A.t([128, 4096], BF16) for _ in range(RING)]
        assert A.off - region0 <= 51488, "ring must only alias phase-1-only buffers (weights/gains/rope tables)"
        Wdown = A.t([128, NF, D], BF16)
        sqjB = A.t([128, D], BF16)
        ubB = [A.t([128, D], BF16) for _ in range(2)]
        u2T = A.t([128, 8, 512], BF16)
        u3T = [A.t([128, 8, 128], BF16) for _ in range(2)]
        actT = A.t([128, NF, 512], BF16)
        sgb = [A.t([128, 512], F32) for _ in range(2)]
        gln2 = A.t([128, D], F32)
        gple = A.t([128, D], F32)
        ptl4 = A.t([128, 4, 256], F32)
        pb4 = A.t([128, 4, 256], BF16)
        pTt4 = A.t([128, 4, 2, 128], BF16)
        gsig = A.t([128, D], F32)
        assert A.off >= work0 + 14336, "h4 must start after the attention working set"
        h4 = A.t([128, 4, D], F32)
        assert A.off <= endA, "h4 must stay inside the phase-1 working area"
        endB = A.off
        print("SBUF: always", region0, "A", endA - region0, "B", endB - region0, "peak", A.peak)

        STOP = int(os.environ.get("KSTOP", "99"))
        RE = os.environ.get("KROPE", "pool")
        def rstd_from_ssq(ssq_ap, n, tmp_ap, out_ap, rs_in, rs_tmp, rs_out, scale_n):
            P.op("act", lambda e: e.activation(tmp_ap, ssq_ap, AF.Sqrt, bias=EPS, scale=1.0 / scale_n),
                 reads=[rs_in], writes=[rs_tmp])
            P.op("dve", lambda e: e.reciprocal(out_ap, tmp_ap), reads=[rs_tmp], writes=[rs_out])

        P.op("sp", lambda e: e.dma_start(out=identf, in_=ident_d), writes=["identf"], dma="c_identf")
        P.op("dve", lambda e: e.tensor_copy(ident, identf), reads=["identf"], writes=["ident"])
        P.op("pool", lambda e: e.memset(ones_f, 1.0), writes=["ones_f"])
        P.op("pool", lambda e: e.memset(ones_b, 1.0), writes=["ones_b"])
        P.op("pool", lambda e: e.memset(negh, -0.5), writes=["negh"])
        EARLY = os.environ.get("KEARLY", "1") == "1" and STOP > 5
        TAILFILL = os.environ.get("KTAILFILL", "1") == "1" and STOP > 5
        POOL_RSTD = os.environ.get("KPOOLRSTD", "1") == "1"
        POOL_RSTD_A = os.environ.get("KPOOLRSTDA", "1") == "1"
        DIRECT = os.environ.get("KDIRECT", "1") == "1"
        DIRECT_WD = os.environ.get("KDIRECTWD", "1") == "1"

        def cast_scratch():
            todo = []
            if not DIRECT:
                todo += [("wo", wo_b, w_o_d), ("wg", wg_b, w_gate_d), ("wu", wu_b, w_up_d),
                         ("wpg", wpg_b, w_pg_d), ("wpp", wpp_b, w_pp_d)]
            if not DIRECT_WD:
                todo += [("wd", wd_b, w_down_d)]
            for nm, dst, src in todo:
                P.op("pool", lambda e, dst=dst, src=src: e.dma_start(out=dst, in_=src),
                     writes=["scr_" + nm], dma="scr_" + nm)

        def prep_A():
            w_in3 = w_in_d.rearrange("(c p) n -> p c n", p=128)
            for (nm_, n0_, nn_) in (("Win_q", 512, 384), ("Win_kv", 896, 288), ("Win_p", 0, 512)):
                P.op("pool", lambda e, n0_=n0_, nn_=nn_: e.dma_start(out=Win[:, :, n0_:n0_ + nn_], in_=w_in3[:, :, n0_:n0_ + nn_]),
                     writes=[nm_], dma=nm_)
            P.op("pool", lambda e: e.dma_start(out=Wqb, in_=w_qb_d.rearrange("(c p) n -> p c n", p=128)),
                 writes=["Wqb"], dma="Wqb")
            P.op("pool", lambda e: e.dma_start(out=Wkvb, in_=w_kvb_d.rearrange("(c p) n -> p c n", p=128)),
                 writes=["Wkvb"], dma="Wkvb")
            P.op("pool", lambda e: e.dma_start(out=Wpool.rearrange("p g d -> p (g d)"), in_=w_pool_d),
                 writes=["Wpool"], dma="Wpool")
            P.op("pool", lambda e: e.dma_start(out=Band.rearrange("p m t -> p (m t)"), in_=band_d),
                 writes=["Band"], dma="Band")
            P.op("sp", lambda e: [
                e.dma_start(out=gln1, in_=ln1_d.partition_broadcast(128)),
                e.dma_start(out=gqa, in_=qan_d.partition_broadcast(128)),
                e.dma_start(out=gkva, in_=kvan_d.partition_broadcast(128)),
                e.dma_start(out=gq, in_=qn_d.partition_broadcast(128)),
                e.dma_start(out=gk, in_=kn_d.partition_broadcast(128)),
                e.dma_start(out=psc, in_=psc_d),
                e.dma_start(out=cos_t.rearrange("p t j -> p (t j)"), in_=cos_d),
                e.dma_start(out=sin_t.rearrange("p t j -> p (t j)"), in_=sin_d),
            ], writes=["gainsA"], dma="gainsA", ndma=8)

            def bc(g_ap, lo):
                return g_ap[:, lo:lo + 16].unsqueeze(1).to_broadcast([128, 16, 16])
            for (CG, SG1, SG2, g_ap, nm) in ((CGq, SG1q, SG2q, gq, "q"), (CGk, SG1k, SG2k, gk, "k")):
                P.op("dve", lambda e, CG=CG, g_ap=g_ap: e.tensor_tensor(CG[:, :, 0:16], cos_t, bc(g_ap, 64), ALU.mult),
                     reads=["gainsA"], writes=["rope" + nm])
                P.op("dve", lambda e, CG=CG, g_ap=g_ap: e.tensor_tensor(CG[:, :, 16:32], cos_t, bc(g_ap, 80), ALU.mult),
                     reads=["gainsA"], writes=["rope" + nm])
                P.op("dve", lambda e, SG2=SG2, g_ap=g_ap: e.tensor_tensor(SG2, sin_t, bc(g_ap, 64), ALU.mult),
                     reads=["gainsA"], writes=["rope" + nm])
                P.op("dve", lambda e, SG1=SG1, g_ap=g_ap: e.scalar_tensor_tensor(SG1, sin_t, -1.0, bc(g_ap, 80), ALU.mult, ALU.mult),
                     reads=["gainsA"], writes=["rope" + nm])
            P.op("pool", lambda e: e.memset(V, 0.0), writes=["V"])
            P.op("pool", lambda e: e.memset(V[:, :, :, 64:128], 1.0), reads=["V"], writes=["V"])

        def pool_tile(s, i):
            PV_ = int(os.environ.get("POOLV", "0"))
            P.stage = "B4a"
            for g in range(4):
                terms = []
                if i > 0:
                    terms.append((i - 1, 0))
                terms.append((i, 3 if i == 0 else (4 if i == NT - 1 else 1)))
                if i < NT - 1:
                    terms.append((i + 1, 2))
                for n_, (j, kind) in enumerate(terms):
                    lhs_ = ub[:, g * 128:(g + 1) * 128] if PV_ == 1 else pinb[j % 4][:, g * 128:(g + 1) * 128]
                    rhs_ = ident if PV_ == 2 else Band[:, g * 5 + kind, :]
                    P.op("pe", lambda e, g=g, lhs_=lhs_, rhs_=rhs_, n_=n_, L=len(terms): e.matmul(
                        psum[:, 6, g * 128:(g + 1) * 128], lhs_, rhs_, start=(n_ == 0), stop=(n_ == L - 1)),
                        reads=[("pin", j % 4) if PV_ != 1 else "ub", "Band" if PV_ != 2 else "ident"], writes=PS(6),
                        nosig=(n_ != len(terms) - 1))
            if PV_ == 3:
                return
            dsb_ = sqj[:, 0:512].rearrange("p (g t) -> p g t", g=4) if PV_ == 5 else dsb
            if PV_ in (0, 6):
                P.op("dve", lambda e: e.tensor_copy(dsb_, psum[:, 6, :].rearrange("p (g t) -> p g t", g=4)),
                     reads=PS(6), writes=["dsb"])
            else:
                P.op("act", lambda e: e.copy(dsb_, psum[:, 6, :].rearrange("p (g t) -> p g t", g=4)),
                     reads=PS(6), writes=["dsb"] + (["sqj"] if PV_ == 5 else []))
            if os.environ.get("POOLA"):
                return
            P.stage = "B4b"
            for g in range(4):
                P.op("pe", lambda e, g=g: e.matmul(psum[:, 7, g * 128:(g + 1) * 128], Wpool[:, g, :], dsb[:, g, :],
                                                   start=True, stop=True),
                     reads=["Wpool", "dsb"], writes=PS(7))
            for g in range(4):
                P.op("dve", lambda e, g=g: e.tensor_scalar(yT[:, g, i * 128:(i + 1) * 128],
                                                           psum[:, 7, g * 128:(g + 1) * 128], psc[:, g:g + 1], None, ALU.mult),
                     reads=PS(7) + ["gainsA"], writes=[("yT", i)])

        def phase1_tile(s, t, part):
            slot = t % 2
            xs = xt[slot]
            tp = ps_bf(0)
            cT = cT2[t % 2]
            krs = krs2[t % 2]
            RcT = ("cT", t % 2)
            Rkrs = ("krs", t % 2)
            if part == "front":
                phase1_front(s, t, slot, xs, tp, cT, krs, RcT, Rkrs)
            else:
                phase1_back(s, t, cT, krs, RcT, Rkrs)

        def phase1_front(s, t, slot, xs, tp, cT, krs, RcT, Rkrs):
            P.stage = "F1"
            P.op("sp", lambda e: e.dma_start(out=xs, in_=x[s, t * 128:(t + 1) * 128, :]),
                 writes=[("xt", slot)], dma="xt%d" % slot)
            P.op("act", lambda e: e.activation(sqj, xs, AF.Square, accum_out=st[:, 0:1]),
                 reads=[("xt", slot)], writes=["sqj", "st0"])
            rstd_from_ssq(st[:, 0:1], D, st[:, 1:2], st[:, 2:3], "st0", "st1", "st2", D)
            P.op("dve", lambda e: e.scalar_tensor_tensor(ub, xs, st[:, 2:3], gln1, ALU.mult, ALU.mult),
                 reads=[("xt", slot), "st2", "gainsA"], writes=["ub"])
            P.stage = "F2a"
            for c in range(8):
                P.op("pe", lambda e, c=c: e.transpose(tp[:, c * 128:(c + 1) * 128], ub[:, c * 128:(c + 1) * 128], ident),
                     reads=["ub", "ident"], writes=PS(0))
            P.op("act", lambda e: e.copy(uT, tp.rearrange("p (c t) -> p c t", c=8)), reads=PS(0), writes=["uT"])
            for (bk, n0, nn, wres) in ((2, 512, 384, "Win_q"), (3, 896, 288, "Win_kv"), (1, 0, 512, "Win_p")):
                for c in range(8):
                    P.op("pe", lambda e, bk=bk, n0=n0, nn=nn, c=c: e.matmul(
                        psum[:, bk, 0:nn], uT[:, c, :], Win[:, c, n0:n0 + nn], start=(c == 0), stop=(c == 7)),
                        reads=["uT", wres], writes=PS(bk))
            P.stage = "F2b"
            PIN_LATER = True
            P.op("act", lambda e: e.activation(sqj[:, 0:384], psum[:, 2, 0:384], AF.Square, accum_out=st[:, 4:5]),
                 reads=PS(2), writes=["sqj", "st4"])
            P.op("act", lambda e: e.activation(sqj[:, 0:256], psum[:, 3, 0:256], AF.Square, accum_out=st[:, 5:6]),
                 reads=PS(3), writes=["sqj", "st5"])
            P.op("act", lambda e: e.activation(st[:, 6:7], st[:, 4:5], AF.Sqrt, bias=EPS, scale=1.0 / 384),
                 reads=["st4"], writes=["st6"])
            P.op("act", lambda e: e.activation(st[:, 7:8], st[:, 5:6], AF.Sqrt, bias=EPS, scale=1.0 / 256),
                 reads=["st5"], writes=["st7"])
            P.op("dve", lambda e: e.reciprocal(st[:, 8:10], st[:, 6:8]), reads=["st6", "st7"], writes=["st8"])
            P.op("dve", lambda e: e.scalar_tensor_tensor(cb[:, 0:384], psum[:, 2, 0:384], st[:, 8:9], gqa, ALU.mult, ALU.mult),
                 reads=PS(2) + ["st8", "gainsA"], writes=["cb"])
            P.op("dve", lambda e: e.scalar_tensor_tensor(cb[:, 384:640], psum[:, 3, 0:256], st[:, 9:10], gkva, ALU.mult, ALU.mult),
                 reads=PS(3) + ["st8", "gainsA"], writes=["cb"])
            P.op("act", lambda e: e.copy(krs, psum[:, 3, 256:288]), reads=PS(3), writes=[Rkrs])
            P.op("act", lambda e: e.copy(pinb[t % 4], psum[:, 1, :]), reads=PS(1), writes=[("pin", t % 4)])
            P.stage = "F3"
            for c in range(5):
                P.op("pe", lambda e, c=c: e.transpose(tp[:, c * 128:(c + 1) * 128], cb[:, c * 128:(c + 1) * 128], ident),
                     reads=["cb", "ident"], writes=PS(0))
            P.op("act", lambda e: e.copy(cT, tp[:, 0:640].rearrange("p (c t) -> p c t", c=5)), reads=PS(0), writes=[RcT])

        def phase1_back(s, t, cT, krs, RcT, Rkrs):
            P.stage = "B1"
            for hf in range(2):
                for c in range(3):
                    P.op("pe", lambda e, hf=hf, c=c: e.matmul(psum[:, 4 + hf, 0:384], cT[:, c, :],
                                                              Wqb[:, c, hf * 384:(hf + 1) * 384], start=(c == 0), stop=(c == 2)),
                         reads=[RcT, "Wqb"], writes=PS(4 + hf))
            for hf in range(2):
                for c in range(2):
                    P.op("pe", lambda e, hf=hf, c=c: e.matmul(psum[:, 6 + hf, :], cT[:, 3 + c, :],
                                                              Wkvb[:, c, hf * 512:(hf + 1) * 512], start=(c == 0), stop=(c == 1)),
                         reads=[RcT, "Wkvb"], writes=PS(6 + hf))
            P.stage = "B2q"
            psq = psum[:, 4:6, 0:384]
            sqq3 = sqq.rearrange("p (a b) -> p a b", a=2)
            P.op("act", lambda e: e.activation(sqq3, psq, AF.Square), reads=PS(4, 2), writes=["sqq"])
            P.op("dve", lambda e: e.tensor_reduce(st[:, 16:24], sqq.rearrange("p (h d) -> p h d", h=8), AX.X, ALU.add),
                 reads=["sqq"], writes=["st16"])
            P.op("act", lambda e: e.activation(st[:, 24:32], st[:, 16:24], AF.Sqrt, bias=EPS, scale=1.0 / 96),
                 reads=["st16"], writes=["st24"])
            P.op("dve", lambda e: e.reciprocal(st[:, 32:40], st[:, 24:32]), reads=["st24"], writes=["st32"])
            Tq = sqq.rearrange("p (h d) -> p h d", h=8)
            for hf in range(2):
                P.op("dve", lambda e, hf=hf: e.tensor_tensor(
                    Tq[:, hf * 4:(hf + 1) * 4, :], psum[:, 4 + hf, 0:384].rearrange("p (h d) -> p h d", h=4),
                    st[:, 32 + hf * 4:36 + hf * 4].unsqueeze(2).to_broadcast([128, 4, 96]), ALU.mult),
                    reads=PS(4 + hf) + ["st32", "sqq"], writes=["sqq"])
            P.op("dve", lambda e: e.tensor_tensor(qfin[:, :, 0:64], Tq[:, :, 0:64],
                                                  gq[:, 0:64].unsqueeze(1).to_broadcast([128, 8, 64]), ALU.mult),
                 reads=["sqq", "gainsA"], writes=["qfin_n"])
            P.op(RE, lambda e: e.tensor_tensor(Ar, Tq[:, :, 64:96], CGq[:, t, :].unsqueeze(1).to_broadcast([128, 8, 32]), ALU.mult),
                 reads=["sqq", "ropeq"], writes=["Ar"])
            P.op(RE, lambda e: e.tensor_tensor(Br[:, :, 0:16], Tq[:, :, 80:96], SG1q[:, t, :].unsqueeze(1).to_broadcast([128, 8, 16]), ALU.mult),
                 reads=["sqq", "ropeq"], writes=["Br"])
            P.op(RE, lambda e: e.tensor_tensor(Br[:, :, 16:32], Tq[:, :, 64:80], SG2q[:, t, :].unsqueeze(1).to_broadcast([128, 8, 16]), ALU.mult),
                 reads=["sqq", "ropeq"], writes=["Br"])
            P.op(RE, lambda e: e.tensor_tensor(qfin[:, :, 64:96], Ar, Br, ALU.add), reads=["Ar", "Br"], writes=["qfin_r"])
            P.stage = "B2k"
            kv3 = psum[:, 6:8, :].rearrange("p a (h d) -> p (a h) d", d=128)
            sqk3 = sqk.rearrange("p (h d) -> p h d", h=8)
            P.op("act", lambda e: e.activation(sqk3, kv3[:, :, 0:64], AF.Square), reads=PS(6, 2), writes=["sqk"])
            P.op("dve", lambda e: e.tensor_reduce(st[:, 40:48], sqk3, AX.X, ALU.add), reads=["sqk"], writes=["st40"])
            P.op("act", lambda e: e.activation(sqjk, krs, AF.Square, accum_out=st[:, 10:11]),
                 reads=[Rkrs], writes=["sqjk", "st10"])
            kv4 = psum[:, 6:8, :].rearrange("p a (j e d) -> p (a j) e d", e=2, d=128)
            P.op("act", lambda e: e.copy(V[:, t, :, 0:64], kv4[:, :, 0, 64:128]), reads=PS(6, 2), writes=["V"])
            P.op("act", lambda e: e.copy(V[:, t, :, 128:192], kv4[:, :, 1, 64:128]), reads=PS(6, 2), writes=["V"])
            P.op("dve", lambda e: e.tensor_scalar(st[:, 40:48], st[:, 40:48], st[:, 10:11], None, ALU.add),
                 reads=["st40", "st10"], writes=["st40"])
            P.op("act", lambda e: e.activation(st[:, 48:56], st[:, 40:48], AF.Sqrt, bias=EPS, scale=1.0 / 96),
                 reads=["st40"], writes=["st48"])
            P.op("dve", lambda e: e.reciprocal(st[:, 56:64], st[:, 48:56]), reads=["st48"], writes=["st56"])
            P.op("dve", lambda e: e.tensor_tensor(sqk3, kv3[:, :, 0:64], st[:, 56:64].unsqueeze(2).to_broadcast([128, 8, 64]), ALU.mult),
                 reads=PS(6, 2) + ["st56", "sqk"], writes=["sqk"])
            P.op("dve", lambda e: e.tensor_tensor(kfin[:, :, 0:64], sqk3, gk[:, 0:64].unsqueeze(1).to_broadcast([128, 8, 64]), ALU.mult),
                 reads=["sqk", "gainsA"], writes=["kfin_n"])
            P.op(RE, lambda e: e.tensor_tensor(Tr, krs.unsqueeze(1).to_broadcast([128, 8, 32]),
                                                  st[:, 56:64].unsqueeze(2).to_broadcast([128, 8, 32]), ALU.mult),
                 reads=[Rkrs, "st56"], writes=["Tr"])
            P.op(RE, lambda e: e.tensor_tensor(Ar, Tr, CGk[:, t, :].unsqueeze(1).to_broadcast([128, 8, 32]), ALU.mult),
                 reads=["Tr", "ropek"], writes=["Ar"])
            P.op(RE, lambda e: e.tensor_tensor(Br[:, :, 0:16], Tr[:, :, 16:32], SG1k[:, t, :].unsqueeze(1).to_broadcast([128, 8, 16]), ALU.mult),
                 reads=["Tr", "ropek"], writes=["Br"])
            P.op(RE, lambda e: e.tensor_tensor(Br[:, :, 16:32], Tr[:, :, 0:16], SG2k[:, t, :].unsqueeze(1).to_broadcast([128, 8, 16]), ALU.mult),
                 reads=["Tr", "ropek"], writes=["Br"])
            P.op(RE, lambda e: e.tensor_tensor(kfin[:, :, 64:96], Ar, Br, ALU.add), reads=["Ar", "Br"], writes=["kfin_r"])
            for (src, dst, bk, nm) in ((qfin, qT, 4, "qT"), (kfin, kT, 5, "kT")):
                P.stage = "B3q" if nm == "qT" else "B3k"
                tpb = ps_bf(bk)
                for h in range(8):
                    P.op("pe", lambda e, src=src, tpb=tpb, h=h: e.transpose(tpb[0:96, h * 128:(h + 1) * 128], src[:, h, :], ident),
                         reads=[nm[0] + "fin_n", nm[0] + "fin_r", "ident"], writes=PS(bk))
                EV = os.environ.get("KEVAC", "ad")
                ev_ = EV[0] if nm == "qT" else EV[1]
                P.op("act" if ev_ == "a" else "dve",
                     lambda e, dst=dst, tpb=tpb, ev_=ev_: (e.copy if ev_ == "a" else e.tensor_copy)(
                         dst[0:96, :, t * 128:(t + 1) * 128], tpb[0:96, :].rearrange("p (h t) -> p h t", h=8)),
                     reads=PS(bk), writes=[nm])
            P.stage = "B4"
            if t >= 1 and not os.environ.get('NOPOOL'):
                pool_tile(s, t - 1)

        def attention(s):
            groups = [(h, qb, g2) for h in range(8) for qb in range(4) for g2 in range(8)]
            NG = len(groups)
            NSB = 3

            def emit_S(n):
                h, qb, g2 = groups[n]
                sb_ = (n % NSB) * 2
                for j in range(2):
                    kc = g2 * 2 + j
                    P.op("pe", lambda e, sb_=sb_, j=j, kc=kc, h=h, qb=qb: e.matmul(
                        psum[:, sb_ + j, :], kT[0:96, h, kc * 128:(kc + 1) * 128],
                        qT[0:96, h, qb * 512:(qb + 1) * 512], start=True, stop=True),
                        reads=["kT", "qT"], writes=PS(sb_ + j))

            def emit_exp(n):
                sb_ = (n % NSB) * 2
                slot = n % 3
                P.op("act", lambda e, sb_=sb_, slot=slot: e.activation(pT[slot], psum[:, sb_:sb_ + 2, :], AF.Exp, scale=ATT_SCALE),
                     reads=PS(sb_, 2), writes=[("pT", slot)])

            def emit_PV(n):
                h, qb, g2 = groups[n]
                it = n // 8
                pair, odd = h // 2, h % 2
                ob = 6 + (it % 2)
                slot = n % 3
                for j in range(2):
                    kc = g2 * 2 + j
                    lhsT = V[:, kc, pair, 64:192] if odd else V[:, kc, pair, 0:128]
                    P.op("pe", lambda e, lhsT=lhsT, ob=ob, slot=slot, j=j, kc=kc: e.matmul(
                        psum[:, ob, :], lhsT, pT[slot][:, j, :], start=(kc == 0), stop=(kc == 15)),
                        reads=["V", ("pT", slot)], writes=PS(ob))

            def norm(it):
                h, qb, _ = groups[it * 8]
                pair, odd = h // 2, h % 2
                ob = 6 + (it % 2)
                r = it % 2
                orow = slice(64, 128) if odd else slice(0, 64)
                drow = slice(0, 64) if odd else slice(64, 128)
                P.op("dve", lambda e, r=r, ob=ob, orow=orow, drow=drow: e.reciprocal(bsb[r][orow, :], psum[drow, ob, :]),
                     reads=PS(ob), writes=[("bsb", r)])
                P.op("dve", lambda e, r=r, ob=ob, orow=orow, pair=pair, qb=qb: e.tensor_tensor(
                    yT[orow, 4 + pair, qb * 512:(qb + 1) * 512], psum[orow, ob, :], bsb[r][orow, :], ALU.mult),
                    reads=PS(ob) + [("bsb", r)], writes=[("yTa", h, qb)])

            for n0 in range(NSB):
                emit_S(n0)
            for n in range(NG):
                emit_exp(n)
                if n + NSB < NG:
                    emit_S(n + NSB)
                emit_PV(n)
                if n % 8 == 7:
                    norm(n // 8)

        stream_items = []
        state = {"next_load": 0, "next_use": 0, "consumed": 0}

        def ring_load(idx, srcs):
            slot = idx % RING
            def fn(e, slot=slot, srcs=srcs):
                return [e.dma_start(out=o_(ring[slot]), in_=i_) for (o_, i_) in srcs]
            P.op("pool", fn, reads=([] if DIRECT else [s_ for s_ in ("scr_wo", "scr_wg", "scr_wu", "scr_wpg", "scr_wpp")]),
                 writes=[("ring", slot)], dma="ring%d" % slot, ndma=len(srcs), extra=state.get("extra"))

        def block_stream():
            items = []
            wo3 = (w_o_d if DIRECT else wo_b).rearrange("(c p) n -> p c n", p=128)
            for hh in range(2):
                items.append([(lambda r: r.rearrange("p (c n) -> p c n", c=4), wo3[:, hh * 4:(hh + 1) * 4, :])])
            wg3 = (w_gate_d if DIRECT else wg_b).rearrange("(c p) n -> p c n", p=128)
            wu3 = (w_up_d if DIRECT else wu_b).rearrange("(c p) n -> p c n", p=128)
            for fp in range(NF // 2):
                items.append([
                    (lambda r: r[:, 0:2048].rearrange("p (c n) -> p c n", c=8), wg3[:, :, fp * 256:(fp + 1) * 256]),
                    (lambda r: r[:, 2048:4096].rearrange("p (c n) -> p c n", c=8), wu3[:, :, fp * 256:(fp + 1) * 256]),
                ])
            wpg3 = (w_pg_d if DIRECT else wpg_b).rearrange("(c p) n -> p c n", p=128)
            for hh in range(2):
                items.append([(lambda r: r.rearrange("p (c n) -> p c n", c=4), wpg3[:, hh * 4:(hh + 1) * 4, :])])
            wpp3 = (w_pp_d if DIRECT else wpp_b).rearrange("(c p) n -> p c n", p=128)
            items.append([(lambda r: r[:, 0:2048].rearrange("p (c n) -> p c n", c=2), wpp3)])
            return items

        def ensure_loaded(upto):
            while state["next_load"] <= upto and state["next_load"] < len(stream_items):
                assert state["next_load"] < state["consumed"] + RING, "ring slot still has un-emitted consumers"
                ring_load(state["next_load"], stream_items[state["next_load"]])
                state["next_load"] += 1

        def use_item():
            idx = state["next_use"]
            state["next_use"] += 1
            ensure_loaded(idx)
            return idx % RING

        def done_items(k):
            state["consumed"] += k
            ensure_loaded(state["consumed"] + RING - 1)

        def load_Wdown():
            assert 45056 <= 18944 + 4608 + 4096 + 1024 + 5120 + 4096 + 1536 + 1024 + 384 + 384 + 32 + 1024 + 1024 + 2048 + 2048
            P.op("pool", lambda e: e.dma_start(out=Wdown, in_=(w_down_d if DIRECT_WD else wd_b).rearrange("(c p) n -> p c n", p=128)),
                 reads=([] if DIRECT_WD else ["scr_wd"]),
                 writes=["Wdown", "Win_q", "Win_kv", "Win_p", "Wqb", "Wkvb", "Wpool", "Band", "gainsA", "ropeq", "ropek"], dma="Wdown")

        def prep_B():
            load_Wdown()
            P.op("sp", lambda e: [e.dma_start(out=gln2, in_=ln2_d.partition_broadcast(128)),
                                  e.dma_start(out=gple, in_=plen_d.partition_broadcast(128))],
                 writes=["gainsB"], dma="gainsB", ndma=2)

        def norm_chain(src, gain, res_src, ui, pool_rstd=False):
            P.op("act", lambda e: e.activation(sqjB, src, AF.Square, accum_out=st[:, 0:1]),
                 reads=[res_src], writes=["sqjB", "st0"])
            if pool_rstd:
                P.op("pool", lambda e: e.tensor_scalar(st[:, 1:2], st[:, 0:1], 1.0 / D, EPS, ALU.mult, ALU.add),
                     reads=["st0"], writes=["st1"])
                P.op("pool", lambda e: e.tensor_tensor(st[:, 2:3], st[:, 1:2], negh[:, 0:1], ALU.pow),
                     reads=["st1", "negh"], writes=["st2"])
            else:
                rstd_from_ssq(st[:, 0:1], D, st[:, 1:2], st[:, 2:3], "st0", "st1", "st2", D)
            P.op("dve", lambda e: e.scalar_tensor_tensor(ubB[ui], src, st[:, 2:3], gain, ALU.mult, ALU.mult),
                 reads=[res_src, "st2", "gainsB"], writes=[("ubB", ui)])

        def transp8(dstT, res_dst, ui):
            tp = ps_bf(0)
            for c in range(8):
                P.op("pe", lambda e, c=c: e.transpose(tp[:, c * 128:(c + 1) * 128], ubB[ui][:, c * 128:(c + 1) * 128], ident),
                     reads=[("ubB", ui), "ident"], writes=PS(0))
            P.op("act", lambda e: e.copy(dstT, tp.rearrange("p (c t) -> p c t", c=8)), reads=PS(0), writes=[res_dst])

        def phase4_block(s, b):
            T0 = b * 512
            h4v = lambda tt: h4[:, tt, :].rearrange("p (a n) -> p a n", a=2)
            def load_x(bb, tt):
                tok_ = bb * 512 + tt * 128
                P.op("sp", lambda e, tt=tt, tok_=tok_: e.dma_start(out=h4[:, tt, :], in_=x[s, tok_:tok_ + 128, :]),
                     writes=[("h4", tt)], dma="h4_%d" % tt, extra=state.get("extra"))

            if b == -1:
                for tt in range(4):
                    load_x(0, tt)
                return

            def load_p(bb):
                P.op("sp", lambda e, bb=bb: e.dma_start(
                    out=ptl4, in_=pin_d[s, bb * 512:(bb + 1) * 512, :].rearrange("(t p) n -> p t n", p=128)),
                    writes=["ptl4"], dma="ptl4")

            if b == 0 or STOP == 5:
                if not EARLY:
                    for tt in range(4):
                        load_x(b, tt)
                load_p(b)
            so = [use_item(), use_item()]

            def mix(tt):
                tok = T0 + tt * 128
                setb = 1 + 2 * (tt % 2)
                for hf in range(2):
                    for c in range(8):
                        P.op("pe", lambda e, hf=hf, c=c, tok=tok, setb=setb: e.matmul(
                            psum[:, setb + hf, :], yT[:, c, tok:tok + 128],
                            ring[so[c // 4]].rearrange("p (c n) -> p c n", c=4)[:, c % 4, hf * 512:(hf + 1) * 512],
                            start=(c == 0), stop=(c == 7)),
                            reads=[("yT", tok // 128)] + [("yTa", hh, b) for hh in range(8)] + [("ring", so[c // 4])],
                            writes=PS(setb + hf))

            def add_a(tt):
                setb = 1 + 2 * (tt % 2)
                P.op("dve", lambda e, tt=tt, setb=setb: e.tensor_tensor(h4v(tt), h4v(tt), psum[:, setb:setb + 2, :], ALU.add),
                     reads=PS(setb, 2) + [("h4", tt)], writes=[("h4", tt)])

            def rest_a(tt):
                norm_chain(h4[:, tt, :], gln2, ("h4", tt), tt % 2, pool_rstd=POOL_RSTD_A)

            def Ta(tt):
                transp8(u2T[:, :, tt * 128:(tt + 1) * 128], "u2T", tt % 2)

            mix(0); add_a(0)
            mix(1); add_a(1); rest_a(0)
            mix(2); add_a(2); rest_a(1)
            Ta(0)
            mix(3); add_a(3); rest_a(2)
            Ta(1)
            rest_a(3)
            sl0 = use_item()
            rg0 = ring[sl0][:, 0:2048].rearrange("p (c n) -> p c n", c=8)
            ru0 = ring[sl0][:, 2048:4096].rearrange("p (c n) -> p c n", c=8)

            def gu0_half(hh):
                for (bk, rw) in ((5, rg0), (6, ru0)):
                    for c in range(8):
                        P.op("pe", lambda e, bk=bk, rw=rw, c=c, hh=hh: e.matmul(
                            psum[:, bk, hh * 256:(hh + 1) * 256], rw[:, c, 0:128], u2T[:, c, hh * 256:(hh + 1) * 256],
                            start=(c == 0), stop=(c == 7)),
                            reads=["u2T", ("ring", sl0)], writes=PS(bk))
            if TAILFILL:
                gu0_half(0)
            Ta(2)
            Ta(3)
            if TAILFILL:
                gu0_half(1)
            done_items(2)
            if STOP == 5:
                for tt in range(4):
                    tok = T0 + tt * 128
                    P.op("sp", lambda e, tt=tt, tok=tok: e.dma_start(out=y[s, tok:tok + 128, :], in_=h4[:, tt, :]),
                         reads=[("h4", tt)], dma="yo%d" % tt)
                state["next_use"] = state["next_load"] = state["consumed"] = len(stream_items)
                return
            for fp in range(NF // 2):
                sl = sl0 if fp == 0 else use_item()
                rg = ring[sl][:, 0:2048].rearrange("p (c n) -> p c n", c=8)
                ru = ring[sl][:, 2048:4096].rearrange("p (c n) -> p c n", c=8)
                for j in range(2):
                    f = fp * 2 + j
                    for (bk, rw) in ((5, rg), (6, ru)):
                        if TAILFILL and f == 0:
                            continue
                        for c in range(8):
                            P.op("pe", lambda e, bk=bk, rw=rw, c=c, j=j: e.matmul(
                                psum[:, bk, :], rw[:, c, j * 128:(j + 1) * 128], u2T[:, c, :], start=(c == 0), stop=(c == 7)),
                                reads=["u2T", ("ring", sl)], writes=PS(bk))
                    P.op("act", lambda e, f=f: e.activation(sgb[f % 2], psum[:, 5, :], AF.Silu),
                         reads=PS(5), writes=[("sgb", f % 2)])
                    P.op("dve", lambda e, f=f: e.tensor_tensor(actT[:, f, :], sgb[f % 2], psum[:, 6, :], ALU.mult),
                         reads=PS(6) + [("sgb", f % 2)], writes=["actT"])
                done_items(1)
            sg_ = [use_item(), use_item()]
            sp_ = use_item()
            P.op("dve", lambda e: e.tensor_copy(pb4, ptl4), reads=["ptl4"], writes=["pb4"])
            if b < 3:
                load_p(b + 1)
            tp7 = ps_bf(7)
            for tt in range(4):
                for c in range(2):
                    P.op("pe", lambda e, tt=tt, c=c: e.transpose(tp7[:, (tt * 2 + c) * 128:(tt * 2 + c + 1) * 128],
                                                                 pb4[:, tt, c * 128:(c + 1) * 128], ident),
                         reads=["pb4", "ident"], writes=PS(7))

            def down(tt):
                for hf in range(2):
                    for f in range(NF):
                        P.op("pe", lambda e, hf=hf, f=f, tt=tt: e.matmul(
                            psum[:, 1 + hf, :], actT[:, f, tt * 128:(tt + 1) * 128],
                            Wdown[:, f, hf * 512:(hf + 1) * 512], start=(f == 0), stop=(f == NF - 1)),
                            reads=["actT", "Wdown"], writes=PS(1 + hf))

            def chain_d(tt):
                for hf in range(2):
                    P.op("dve", lambda e, tt=tt, hf=hf: e.tensor_tensor(h4[:, tt, hf * 512:(hf + 1) * 512], h4[:, tt, hf * 512:(hf + 1) * 512],
                                                                        psum[:, 1 + hf, :], ALU.add),
                         reads=PS(1 + hf) + [("h4", tt)], writes=[("h4", tt)])
                norm_chain(h4[:, tt, :], gple, ("h4", tt), tt % 2, pool_rstd=POOL_RSTD)

            def Td(tt):
                transp8(u3T[tt % 2], ("u3T", tt % 2), tt % 2)

            def ple(tt):
                tok = T0 + tt * 128
                for hf in range(2):
                    for c in range(8):
                        P.op("pe", lambda e, hf=hf, c=c, tt=tt: e.matmul(
                            psum[:, 3 + hf, :], u3T[tt % 2][:, c, :],
                            ring[sg_[c // 4]].rearrange("p (c n) -> p c n", c=4)[:, c % 4, hf * 512:(hf + 1) * 512],
                            start=(c == 0), stop=(c == 7)),
                            reads=[("u3T", tt % 2), ("ring", sg_[c // 4])], writes=PS(3 + hf))
                for hf in range(2):
                    for c in range(2):
                        P.op("pe", lambda e, hf=hf, c=c, tt=tt: e.matmul(
                            psum[:, 5 + hf, :], pTt4[:, tt, c, :],
                            ring[sp_][:, 0:2048].rearrange("p (c n) -> p c n", c=2)[:, c, hf * 512:(hf + 1) * 512],
                            start=(c == 0), stop=(c == 1)),
                            reads=["pTt4", ("ring", sp_)], writes=PS(5 + hf))
                g2 = gsig.rearrange("p (a n) -> p a n", a=2)
                P.op("act", lambda e: e.activation(g2, psum[:, 3:5, :], AF.Tanh, scale=0.5),
                     reads=PS(3, 2), writes=["gsig"])
                P.op("dve", lambda e: e.scalar_tensor_tensor(g2, g2, 1.0, psum[:, 5:7, :], ALU.add, ALU.mult),
                     reads=["gsig"] + PS(5, 2), writes=["gsig"])
                P.op("dve", lambda e, tt=tt: e.scalar_tensor_tensor(h4[:, tt, :], gsig, 0.5, h4[:, tt, :], ALU.mult, ALU.add),
                     reads=["gsig", ("h4", tt)], writes=[("h4", tt)])
                P.op("sp", lambda e, tt=tt, tok=tok: e.dma_start(out=y[s, tok:tok + 128, :], in_=h4[:, tt, :]),
                     reads=[("h4", tt)], dma="yo%d" % tt)
                if b < 3:
                    load_x(b + 1, tt)

            down(0)
            P.op("dve", lambda e: e.tensor_copy(pTt4.rearrange("p t c n -> p (t c n)"), tp7), reads=PS(7), writes=["pTt4"])
            chain_d(0)
            down(1); chain_d(1)
            Td(0)
            down(2); chain_d(2)
            ple(0)
            Td(1)
            down(3); chain_d(3)
            ple(1)
            Td(2)
            ple(2)
            Td(3)
            ple(3)
            done_items(3)

        for s in range(NSEQ):
            if s > 0:
                P.barrier()
            if STOP <= 0:
                break
            if s == 0:
                cast_scratch()
            prep_A()
            if s == 0 and not (DIRECT and DIRECT_WD):
                P.barrier()
            if STOP <= 1:
                break
            NTL = int(os.environ.get("KNT", NT)) if STOP > 2 else 2
            def ph1(t, part, stages):
                P.filter = set(stages)
                phase1_tile(s, t, part)
                P.filter = None
                P.stage = None
            ph1(0, "front", ("F1", "F2a", "F2b", "F3"))
            for t in range(NTL):
                nxt = t + 1 < NTL
                if nxt:
                    ph1(t + 1, "front", ("F1",))
                ph1(t, "back", ("B1",))
                ph1(t, "back", ("B2q",))
                if nxt:
                    ph1(t + 1, "front", ("F2a",))
                ph1(t, "back", ("B2k",))
                if nxt:
                    ph1(t + 1, "front", ("F2b",))
                ph1(t, "back", ("B3q",))
                if nxt:
                    ph1(t + 1, "front", ("F3",))
                ph1(t, "back", ("B4a",))
                ph1(t, "back", ("B3k",))
                ph1(t, "back", ("B4b",))
            if NTL == NT:
                pool_tile(s, NT - 1)
                P.stage = None
            if STOP <= 3:
                break
            def start_stream():
                base = len(stream_items)
                for b in range(4):
                    stream_items.extend(block_stream())
                assert state["next_load"] == base and state["next_use"] == base and state["consumed"] == base
                done_items(0)
            if EARLY:
                state["extra"] = P._all_last()
                start_stream()
                phase4_block(s, -1)
                state["extra"] = None
            attention(s)
            if STOP <= 4:
                break
            P.barrier()
            if not EARLY:
                start_stream()
            prep_B()
            for b in range(4 if STOP != 5 else 1):
                phase4_block(s, b)
            assert state["next_use"] == len(stream_items) and state["next_load"] == len(stream_items)
        if DBG:
            P.barrier()
            P.op("pool", lambda e: e.dma_start(out=dbg_y, in_=yT), dma="dbg_y")
            P.op("pool", lambda e: e.dma_start(out=dbg_q, in_=qT), dma="dbg_q")
            P.op("pool", lambda e: e.dma_start(out=dbg_k, in_=kT), dma="dbg_k")
            P.op("pool", lambda e: e.dma_start(out=dbg_v, in_=V), dma="dbg_v")
        P.finish()
        nw, counts = P.emit(lambda name: es.enter_context(nc.semaphore(name)))
        print("ops", len(P.ops), "waits", nw, "sems", len(counts), {k: v for k, v in counts.items() if k[0] == "eng"})
    return nc


def _consts():
    ident = np.eye(128, dtype=np.float32)
    band = np.zeros((20, 128, 128), np.float32)
    for g, w in enumerate(POOL_WINDOWS):
        half = w // 2
        Bf = np.zeros((S, S), np.float32)
        tt = np.arange(S)
        lo = np.clip(tt - half, 0, S)
        hi = np.clip(tt - half + w, 0, S)
        for t_ in range(S):
            Bf[lo[t_]:hi[t_], t_] = np.float32(1.0) / np.float32(hi[t_] - lo[t_])
            Bf[t_, t_] -= 1.0
        band[g * 5 + 0] = Bf[0:128, 128:256]
        band[g * 5 + 1] = Bf[128:256, 128:256]
        band[g * 5 + 2] = Bf[256:384, 128:256]
        band[g * 5 + 3] = Bf[0:128, 0:128]
        band[g * 5 + 4] = Bf[S - 128:, S - 128:]
    inv = np.float32(10000.0) ** (-np.arange(0, 32, 2, dtype=np.float32) / np.float32(32))
    ang = np.arange(S, dtype=np.float32)[:, None] * inv[None, :].astype(np.float32)
    cosT = np.cos(ang).astype(np.float32)
    sinT = np.sin(ang).astype(np.float32)
    band = np.ascontiguousarray(band.transpose(1, 0, 2).reshape(128, 20 * 128))
    cosT = np.ascontiguousarray(cosT.reshape(NT, 128, 16).transpose(1, 0, 2).reshape(128, 256))
    sinT = np.ascontiguousarray(sinT.reshape(NT, 128, 16).transpose(1, 0, 2).reshape(128, 256))
    return ident, band, cosT, sinT


_NC_CACHE = {}


def kernel(x_prompt, x_sample, p_prompt, p_sample, ln1, w_in, w_pool, pool_scale, q_a_norm, w_qb,
           kv_a_norm, w_kvb, q_norm, k_norm, w_o, ln2, w_gate, w_up, w_down, ple_norm, w_ple_gate,
           w_ple_proj):
    f = lambda a: np.ascontiguousarray(np.asarray(a, dtype=np.float32))
    X = np.concatenate([f(x_prompt), f(x_sample)], axis=0)
    Pm = np.concatenate([f(p_prompt)[0], f(p_sample)[0]], axis=0)
    ident, band, cosT, sinT = _consts()
    common = {
        "ln1": f(ln1).reshape(1, D), "ln2": f(ln2).reshape(1, D), "ple_norm": f(ple_norm).reshape(1, D),
        "q_a_norm": f(q_a_norm).reshape(1, 384), "kv_a_norm": f(kv_a_norm).reshape(1, 256),
        "q_norm": f(q_norm).reshape(1, 96), "k_norm": f(k_norm).reshape(1, 96),
        "pool_scale": np.ascontiguousarray(f(pool_scale).reshape(4, 128).T),
        "w_in": f(w_in)[0], "w_pool": np.ascontiguousarray(f(w_pool)[0].transpose(1, 0, 2).reshape(128, 512)), "w_qb": f(w_qb)[0], "w_kvb": f(w_kvb)[0],
        "w_o": f(w_o)[0], "w_gate": f(w_gate)[0], "w_up": f(w_up)[0], "w_down": f(w_down)[0],
        "w_ple_gate": f(w_ple_gate)[0], "w_ple_proj": f(w_ple_proj)[0],
        "ident": ident, "band": band, "cosT": cosT, "sinT": sinT,
    }
    if "nc" not in _NC_CACHE:
        _NC_CACHE["nc"] = build_nc(NSEQ_CORE)
    nc = _NC_CACHE["nc"]
    in_maps = []
    for c in range(NCORES):
        m = dict(common)
        m["x"] = np.ascontiguousarray(X[c * NSEQ_CORE:(c + 1) * NSEQ_CORE])
        m["p"] = np.ascontiguousarray(Pm[c * NSEQ_CORE:(c + 1) * NSEQ_CORE])
        in_maps.append(m)
    res = run_bass_kernel_spmd(nc, in_maps, core_ids=list(range(NCORES)))
    Y = np.concatenate([r["y"] for r in res.results], axis=0)
    nb = x_prompt.shape[0]
    return (np.ascontiguousarray(Y[:nb]).astype(np.float32), np.ascontiguousarray(Y[nb:]).astype(np.float32))
```

```python
import math
import os
from contextlib import ExitStack
import numpy as np
import concourse.bass as bass
import concourse.mybir as mybir
from concourse.bass_utils import run_bass_kernel_spmd

F32 = mybir.dt.float32
BF16 = mybir.dt.bfloat16
U8 = mybir.dt.uint8
AF = mybir.ActivationFunctionType
ALU = mybir.AluOpType
AX = mybir.AxisListType

S = 2048
D = 1024
NT = S // 128
IN_W = 1184
DFF = 2816
NF = DFF // 128
EPS = 1e-6
ATT_SCALE = 1.0 / math.sqrt(96.0)
POOL_WINDOWS = (2, 4, 8, 16)
NCORES = 8
NSEQ_CORE = 3


SAME_ENG_EDGES = bool(os.environ.get("KSE"))


class _Op:
    __slots__ = ("eng", "fn", "deps", "is_dma", "semkey", "ndma", "signal", "token", "idx", "nosig")


class Prog:
    ENGS = ("pe", "act", "dve", "pool", "sp")

    def __init__(self, nc):
        self.nc = nc
        self.engs = {"pe": nc.tensor, "act": nc.scalar, "dve": nc.vector,
                     "pool": nc.gpsimd, "sp": nc.sync}
        self.ops = []
        self.last_writer = {}
        self.readers = {}
        self.pending = {e: set() for e in self.ENGS}
        self.filter = None
        self.stage = None

    def op(self, eng, fn, reads=(), writes=(), dma=None, ndma=1, nosig=False, extra=None):
        if self.filter is not None and self.stage not in self.filter:
            return None
        o = _Op()
        o.nosig = nosig
        o.eng = eng
        o.fn = fn
        o.is_dma = dma is not None
        o.semkey = dma
        o.ndma = ndma
        o.signal = o.is_dma
        o.token = None
        o.idx = len(self.ops)
        deps = set()
        ops = self.ops
        for r in reads:
            w = self.last_writer.get(r)
            if w is not None:
                wo = ops[w]
                if not (wo.eng == "pe" and eng == "pe" and not wo.is_dma and not o.is_dma):
                    deps.add(w)
        for r in writes:
            w = self.last_writer.get(r)
            if w is not None:
                wo = ops[w]
                if wo.is_dma or o.is_dma or wo.eng != eng or (SAME_ENG_EDGES and eng != "pe"):
                    deps.add(w)
            for rd in self.readers.get(r, {}).values():
                ro = ops[rd]
                if ro.is_dma or o.is_dma or ro.eng != eng or (SAME_ENG_EDGES and eng != "pe"):
                    deps.add(rd)
        if self.pending[eng]:
            deps |= self.pending[eng]
            self.pending[eng] = set()
        if extra:
            deps |= set(extra)
        o.deps = deps
        rkey = ("dma", dma) if o.is_dma else eng
        for r in reads:
            self.readers.setdefault(r, {})[rkey] = o.idx
        for r in writes:
            self.last_writer[r] = o.idx
            self.readers[r] = {}
        ops.append(o)
        return o.idx

    def _all_last(self):
        last = {}
        for o in self.ops:
            if o.is_dma:
                last[("dma", o.semkey)] = o.idx
            else:
                last[o.eng] = o.idx
        return set(last.values())

    def barrier(self):
        deps = self._all_last()
        for e in self.ENGS:
            self.pending[e] = set(deps)

    def finish(self):
        deps = self._all_last()
        self.pending["sp"] = set(deps)
        self.op("sp", lambda e: None)

    def emit(self, get_sem):
        ops = self.ops
        nxt = {}
        last_ok = {}
        for o in reversed(ops):
            if o.is_dma:
                continue
            if o.nosig:
                nxt[o.idx] = last_ok.get(o.eng)
            else:
                last_ok[o.eng] = o.idx
        for o in ops:
            nd = set()
            for d in o.deps:
                if ops[d].nosig:
                    r = nxt[d]
                    assert r is not None and r < o.idx, ("cannot redirect nosig dep", d, r, o.idx)
                    nd.add(r)
                else:
                    nd.add(d)
            o.deps = nd
        for o in ops:
            for d in o.deps:
                ops[d].signal = True
        counts = {}
        for o in ops:
            if o.is_dma:
                key = ("dma", o.semkey)
                counts[key] = counts.get(key, 0) + 16 * o.ndma
                o.token = (key, counts[key])
            elif o.signal:
                key = ("eng", o.eng)
                counts[key] = counts.get(key, 0) + 1
                o.token = (key, counts[key])
        sems = {key: get_sem("s_" + "_".join(str(k) for k in key)) for key in counts}
        eng_know = {e: {} for e in self.ENGS}
        know = [None] * len(ops)
        prev_dma_tok = {}
        nwaits = 0
        plan = {e: [] for e in self.ENGS}
        for o in ops:
            ek = eng_know[o.eng]
            need = {}
            for d in o.deps:
                k, v = ops[d].token
                if ek.get(k, 0) < v and need.get(k, 0) < v:
                    need[k] = v
            if need:
                newk = dict(ek)
                for d in o.deps:
                    for k, v in know[d].items():
                        if newk.get(k, 0) < v:
                            newk[k] = v
                for k in list(need.keys()):
                    v = need[k]
                    for d in o.deps:
                        tk, tv = ops[d].token
                        if tk == k and tv >= v:
                            continue
                        if know[d].get(k, 0) >= v:
                            del need[k]
                            break
                nwaits += len(need)
                eng_know[o.eng] = newk
                ek = newk
            if o.is_dma:
                key = o.token[0]
                pv = prev_dma_tok.get(key, 0)
                assert ek.get(key, 0) >= pv, f"DMA slot {key} reissued while previous group may be in flight"
                prev_dma_tok[key] = o.token[1]
            plan[o.eng].append((o, list(need.items())))
            if o.is_dma or o.signal:
                kk = dict(ek)
                kk[o.token[0]] = o.token[1]
                know[o.idx] = kk
            else:
                know[o.idx] = ek

        semv = {k: 0 for k in counts}
        ptr = {e: 0 for e in self.ENGS}
        progress = True
        while progress:
            progress = False
            for e_ in self.ENGS:
                while ptr[e_] < len(plan[e_]):
                    o, waits = plan[e_][ptr[e_]]
                    if all(semv[k] >= v for k, v in waits):
                        if o.is_dma:
                            semv[o.token[0]] += 16 * o.ndma
                        elif o.signal:
                            semv[o.token[0]] += 1
                        ptr[e_] += 1
                        progress = True
                    else:
                        break
        for e_ in self.ENGS:
            assert ptr[e_] == len(plan[e_]), ("DEADLOCK", e_, ptr[e_], len(plan[e_]), plan[e_][ptr[e_]][1], semv)

        def mk(engname):
            def body(eng):
                for (o, waits) in plan[engname]:
                    for k, v in waits:
                        eng.wait_ge(sems[k], v)
                    res = o.fn(eng)
                    if o.is_dma:
                        if not isinstance(res, (list, tuple)):
                            res = [res]
                        assert len(res) == o.ndma, (len(res), o.ndma)
                        for r in res:
                            r.then_inc(sems[o.token[0]], 16)
                    elif o.signal:
                        res.then_inc(sems[o.token[0]], 1)
            return body

        with self.nc.Block() as block:
            block.tensor(mk("pe"))
            block.scalar(mk("act"))
            block.vector(mk("dve"))
            block.gpsimd(mk("pool"))
            block.sync(mk("sp"))
        return nwaits, counts


class Arena:
    def __init__(self, big, limit):
        self.big = big
        self.off = 0
        self.limit = limit
        self.peak = 0

    def t(self, shape, dt):
        esz = 4 if dt == F32 else 2
        n = esz
        for d in shape[1:]:
            n *= d
        off = self.off
        self.off += (n + 31) // 32 * 32
        self.peak = max(self.peak, self.off)
        assert self.off <= self.limit, (self.off, self.limit)
        ap = self.big[:, off:off + n].bitcast(dt)
        if len(shape) == 3:
            ap = ap.rearrange("p (a b) -> p a b", a=shape[1])
        elif len(shape) == 4:
            ap = ap.rearrange("p (a b c) -> p a b c", a=shape[1], b=shape[2])
        return ap


def build_nc(NSEQ=NSEQ_CORE):
    nc = bass.Bass("TRN2", target_bir_lowering=False)

    def din(name, shape):
        return nc.dram_tensor(name, shape, F32, kind="ExternalInput").ap()

    x = din("x", [NSEQ, S, D])
    pin_d = din("p", [NSEQ, S, 256])
    ln1_d = din("ln1", [1, D])
    ln2_d = din("ln2", [1, D])
    plen_d = din("ple_norm", [1, D])
    qan_d = din("q_a_norm", [1, 384])
    kvan_d = din("kv_a_norm", [1, 256])
    qn_d = din("q_norm", [1, 96])
    kn_d = din("k_norm", [1, 96])
    psc_d = din("pool_scale", [128, 4])
    w_in_d = din("w_in", [D, IN_W])
    w_pool_d = din("w_pool", [128, 512])
    w_qb_d = din("w_qb", [384, 768])
    w_kvb_d = din("w_kvb", [256, 1024])
    w_o_d = din("w_o", [D, D])
    w_gate_d = din("w_gate", [D, DFF])
    w_up_d = din("w_up", [D, DFF])
    w_down_d = din("w_down", [DFF, D])
    w_pg_d = din("w_ple_gate", [D, D])
    w_pp_d = din("w_ple_proj", [256, D])
    ident_d = din("ident", [128, 128])
    band_d = din("band", [128, 2560])
    cos_d = din("cosT", [128, 256])
    sin_d = din("sinT", [128, 256])
    y = nc.dram_tensor("y", [NSEQ, S, D], F32, kind="ExternalOutput").ap()
    DBG = bool(os.environ.get("KDBG"))
    if DBG:
        dbg_y = nc.dram_tensor("dbg_y", [128, 8, S], F32, kind="ExternalOutput").ap()
        dbg_q = nc.dram_tensor("dbg_q", [128, 8, S], F32, kind="ExternalOutput").ap()
        dbg_k = nc.dram_tensor("dbg_k", [128, 8, S], F32, kind="ExternalOutput").ap()
        dbg_v = nc.dram_tensor("dbg_v", [128, 16, 4, 192], F32, kind="ExternalOutput").ap()

    def dscr(name, shape):
        return nc.dram_tensor(name, shape, BF16, kind="Internal").ap()

    wo_b = dscr("wo_b", [D, D])
    wg_b = dscr("wg_b", [D, DFF])
    wu_b = dscr("wu_b", [D, DFF])
    wd_b = dscr("wd_b", [DFF, D])
    wpg_b = dscr("wpg_b", [D, D])
    wpp_b = dscr("wpp_b", [256, D])

    es = ExitStack()
    P = Prog(nc)
    with es:
        TOT = 212000
        big = es.enter_context(nc.sbuf_tensor("big", [128, TOT], U8))
        psum = es.enter_context(nc.psum_tensor("psum", [128, 8, 512], F32))
        A = Arena(big, TOT)

        def PS(b, n=1):
            return [("ps", b + i) for i in range(n)]

        def ps_bf(b):
            return psum[:, b, :].bitcast(BF16)

        ident = A.t([128, 128], BF16)
        identf = A.t([128, 128], F32)
        ones_f = A.t([128, 128], F32)
        ones_b = A.t([128, 128], BF16)
        yT = A.t([128, 8, S], BF16)
        st = A.t([128, 64], F32)
        negh = A.t([128, 8], F32)
        region0 = A.off

        Win = A.t([128, 8, IN_W], BF16)
        Wqb = A.t([128, 3, 768], BF16)
        Wkvb = A.t([128, 2, 1024], BF16)
        Wpool = A.t([128, 4, 128], BF16)
        Band = A.t([128, 20, 128], BF16)
        gln1 = A.t([128, D], F32)
        gqa = A.t([128, 384], F32)
        gkva = A.t([128, 256], F32)
        gq = A.t([128, 96], F32)
        gk = A.t([128, 96], F32)
        psc = A.t([128, 4], F32)
        cos_t = A.t([128, 16, 16], F32)
        sin_t = A.t([128, 16, 16], F32)
        CGq = A.t([128, 16, 32], F32)
        CGk = A.t([128, 16, 32], F32)
        SG1q = A.t([128, 16, 16], F32)
        SG2q = A.t([128, 16, 16], F32)
        SG1k = A.t([128, 16, 16], F32)
        SG2k = A.t([128, 16, 16], F32)
        qT = A.t([128, 8, S], BF16)
        kT = A.t([128, 8, S], BF16)
        V = A.t([128, 16, 4, 192], BF16)
        work0 = A.off
        xt = [A.t([128, D], F32) for _ in range(2)]
        sqj = A.t([128, D], BF16)
        ub = A.t([128, D], BF16)
        uT = A.t([128, 8, 128], BF16)
        pinb = [A.t([128, 512], BF16) for _ in range(4)]
        cb = A.t([128, 640], BF16)
        cT2 = [A.t([128, 5, 128], BF16) for _ in range(2)]
        krs2 = [A.t([128, 32], F32) for _ in range(2)]
        sqjk = A.t([128, 32], BF16)
        sqq = A.t([128, 768], F32)
        sqk = A.t([128, 512], F32)
        Tr = A.t([128, 8, 32], F32)
        Ar = A.t([128, 8, 32], F32)
        Br = A.t([128, 8, 32], F32)
        qfin = A.t([128, 8, 96], BF16)
        kfin = A.t([128, 8, 96], BF16)
        dsb = A.t([128, 4, 128], BF16)
        endA = A.off
        A.off = work0
        pT = [A.t([128, 2, 512], BF16) for _ in range(3)]
        rdh = [A.t([128, 512], BF16) for _ in range(2)]
        rdl = [A.t([128, 512], BF16) for _ in range(2)]
        bsb = [A.t([128, 512], F32) for _ in range(2)]
        assert A.off <= endA

        A.off = region0
        RING = 6
        ring = [A.t([128, 4096], BF16) for _ in range(RING)]
        assert A.off - region0 <= 51488, "ring must only alias phase-1-only buffers (weights/gains/rope tables)"
        Wdown = A.t([128, NF, D], BF16)
        sqjB = A.t([128, D], BF16)
        ubB = [A.t([128, D], BF16) for _ in range(2)]
        u2T = A.t([128, 8, 512], BF16)
        u3T = [A.t([128, 8, 128], BF16) for _ in range(2)]
        actT = A.t([128, NF, 512], BF16)
        sgb = [A.t([128, 512], F32) for _ in range(2)]
        gln2 = A.t([128, D], F32)
        gple = A.t([128, D], F32)
        ptl4 = A.t([128, 4, 256], F32)
        pb4 = A.t([128, 4, 256], BF16)
        pTt4 = A.t([128, 4, 2, 128], BF16)
        gsig = A.t([128, D], F32)
        assert A.off >= work0 + 14336, "h4 must start after the attention working set"
        h4 = A.t([128, 4, D], F32)
        assert A.off <= endA, "h4 must stay inside the phase-1 working area"
        endB = A.off
        print("SBUF: always", region0, "A", endA - region0, "B", endB - region0, "peak", A.peak)

        STOP = int(os.environ.get("KSTOP", "99"))
        RE = os.environ.get("KROPE", "pool")
        def rstd_from_ssq(ssq_ap, n, tmp_ap, out_ap, rs_in, rs_tmp, rs_out, scale_n):
            P.op("act", lambda e: e.activation(tmp_ap, ssq_ap, AF.Sqrt, bias=EPS, scale=1.0 / scale_n),
                 reads=[rs_in], writes=[rs_tmp])
            P.op("dve", lambda e: e.reciprocal(out_ap, tmp_ap), reads=[rs_tmp], writes=[rs_out])

        P.op("sp", lambda e: e.dma_start(out=identf, in_=ident_d), writes=["identf"], dma="c_identf")
        P.op("dve", lambda e: e.tensor_copy(ident, identf), reads=["identf"], writes=["ident"])
        P.op("pool", lambda e: e.memset(ones_f, 1.0), writes=["ones_f"])
        P.op("pool", lambda e: e.memset(ones_b, 1.0), writes=["ones_b"])
        P.op("pool", lambda e: e.memset(negh, -0.5), writes=["negh"])
        EARLY = os.environ.get("KEARLY", "1") == "1" and STOP > 5
        TAILFILL = os.environ.get("KTAILFILL", "1") == "1" and STOP > 5
        NFILL = int(os.environ.get("KNFILL", "2"))
        POOL_RSTD = os.environ.get("KPOOLRSTD", "1") == "1"
        POOL_RSTD_A = os.environ.get("KPOOLRSTDA", "1") == "1"
        DIRECT = os.environ.get("KDIRECT", "1") == "1"
        DIRECT_WD = os.environ.get("KDIRECTWD", "1") == "1"

        def cast_scratch():
            todo = []
            if not DIRECT:
                todo += [("wo", wo_b, w_o_d), ("wg", wg_b, w_gate_d), ("wu", wu_b, w_up_d),
                         ("wpg", wpg_b, w_pg_d), ("wpp", wpp_b, w_pp_d)]
            if not DIRECT_WD:
                todo += [("wd", wd_b, w_down_d)]
            for nm, dst, src in todo:
                P.op("pool", lambda e, dst=dst, src=src: e.dma_start(out=dst, in_=src),
                     writes=["scr_" + nm], dma="scr_" + nm)

        def prep_A():
            w_in3 = w_in_d.rearrange("(c p) n -> p c n", p=128)
            for (nm_, n0_, nn_) in (("Win_q", 512, 384), ("Win_kv", 896, 288), ("Win_p", 0, 512)):
                P.op("pool", lambda e, n0_=n0_, nn_=nn_: e.dma_start(out=Win[:, :, n0_:n0_ + nn_], in_=w_in3[:, :, n0_:n0_ + nn_]),
                     writes=[nm_], dma=nm_)
            P.op("pool", lambda e: e.dma_start(out=Wqb, in_=w_qb_d.rearrange("(c p) n -> p c n", p=128)),
                 writes=["Wqb"], dma="Wqb")
            P.op("pool", lambda e: e.dma_start(out=Wkvb, in_=w_kvb_d.rearrange("(c p) n -> p c n", p=128)),
                 writes=["Wkvb"], dma="Wkvb")
            P.op("pool", lambda e: e.dma_start(out=Wpool.rearrange("p g d -> p (g d)"), in_=w_pool_d),
                 writes=["Wpool"], dma="Wpool")
            P.op("pool", lambda e: e.dma_start(out=Band.rearrange("p m t -> p (m t)"), in_=band_d),
                 writes=["Band"], dma="Band")
            P.op("sp", lambda e: [
                e.dma_start(out=gln1, in_=ln1_d.partition_broadcast(128)),
                e.dma_start(out=gqa, in_=qan_d.partition_broadcast(128)),
                e.dma_start(out=gkva, in_=kvan_d.partition_broadcast(128)),
                e.dma_start(out=gq, in_=qn_d.partition_broadcast(128)),
                e.dma_start(out=gk, in_=kn_d.partition_broadcast(128)),
                e.dma_start(out=psc, in_=psc_d),
                e.dma_start(out=cos_t.rearrange("p t j -> p (t j)"), in_=cos_d),
                e.dma_start(out=sin_t.rearrange("p t j -> p (t j)"), in_=sin_d),
            ], writes=["gainsA"], dma="gainsA", ndma=8)

            def bc(g_ap, lo):
                return g_ap[:, lo:lo + 16].unsqueeze(1).to_broadcast([128, 16, 16])
            for (CG, SG1, SG2, g_ap, nm) in ((CGq, SG1q, SG2q, gq, "q"), (CGk, SG1k, SG2k, gk, "k")):
                P.op("dve", lambda e, CG=CG, g_ap=g_ap: e.tensor_tensor(CG[:, :, 0:16], cos_t, bc(g_ap, 64), ALU.mult),
                     reads=["gainsA"], writes=["rope" + nm])
                P.op("dve", lambda e, CG=CG, g_ap=g_ap: e.tensor_tensor(CG[:, :, 16:32], cos_t, bc(g_ap, 80), ALU.mult),
                     reads=["gainsA"], writes=["rope" + nm])
                P.op("dve", lambda e, SG2=SG2, g_ap=g_ap: e.tensor_tensor(SG2, sin_t, bc(g_ap, 64), ALU.mult),
                     reads=["gainsA"], writes=["rope" + nm])
                P.op("dve", lambda e, SG1=SG1, g_ap=g_ap: e.scalar_tensor_tensor(SG1, sin_t, -1.0, bc(g_ap, 80), ALU.mult, ALU.mult),
                     reads=["gainsA"], writes=["rope" + nm])
            P.op("pool", lambda e: e.memset(V, 0.0), writes=["V"])
            P.op("pool", lambda e: e.memset(V[:, :, :, 64:128], 1.0), reads=["V"], writes=["V"])

        def pool_tile(s, i):
            PV_ = int(os.environ.get("POOLV", "0"))
            P.stage = "B4a"
            for g in range(4):
                terms = []
                if i > 0:
                    terms.append((i - 1, 0))
                terms.append((i, 3 if i == 0 else (4 if i == NT - 1 else 1)))
                if i < NT - 1:
                    terms.append((i + 1, 2))
                for n_, (j, kind) in enumerate(terms):
                    lhs_ = ub[:, g * 128:(g + 1) * 128] if PV_ == 1 else pinb[j % 4][:, g * 128:(g + 1) * 128]
                    rhs_ = ident if PV_ == 2 else Band[:, g * 5 + kind, :]
                    P.op("pe", lambda e, g=g, lhs_=lhs_, rhs_=rhs_, n_=n_, L=len(terms): e.matmul(
                        psum[:, 6, g * 128:(g + 1) * 128], lhs_, rhs_, start=(n_ == 0), stop=(n_ == L - 1)),
                        reads=[("pin", j % 4) if PV_ != 1 else "ub", "Band" if PV_ != 2 else "ident"], writes=PS(6),
                        nosig=(n_ != len(terms) - 1))
            if PV_ == 3:
                return
            dsb_ = sqj[:, 0:512].rearrange("p (g t) -> p g t", g=4) if PV_ == 5 else dsb
            if PV_ in (0, 6):
                P.op("dve", lambda e: e.tensor_copy(dsb_, psum[:, 6, :].rearrange("p (g t) -> p g t", g=4)),
                     reads=PS(6), writes=["dsb"])
            else:
                P.op("act", lambda e: e.copy(dsb_, psum[:, 6, :].rearrange("p (g t) -> p g t", g=4)),
                     reads=PS(6), writes=["dsb"] + (["sqj"] if PV_ == 5 else []))
            if os.environ.get("POOLA"):
                return
            P.stage = "B4b"
            for g in range(4):
                P.op("pe", lambda e, g=g: e.matmul(psum[:, 7, g * 128:(g + 1) * 128], Wpool[:, g, :], dsb[:, g, :],
                                                   start=True, stop=True),
                     reads=["Wpool", "dsb"], writes=PS(7))
            for g in range(4):
                P.op("dve", lambda e, g=g: e.tensor_scalar(yT[:, g, i * 128:(i + 1) * 128],
                                                           psum[:, 7, g * 128:(g + 1) * 128], psc[:, g:g + 1], None, ALU.mult),
                     reads=PS(7) + ["gainsA"], writes=[("yT", i)])

        def phase1_tile(s, t, part):
            slot = t % 2
            xs = xt[slot]
            tp = ps_bf(0)
            cT = cT2[t % 2]
            krs = krs2[t % 2]
            RcT = ("cT", t % 2)
            Rkrs = ("krs", t % 2)
            if part == "front":
                phase1_front(s, t, slot, xs, tp, cT, krs, RcT, Rkrs)
            else:
                phase1_back(s, t, cT, krs, RcT, Rkrs)

        def phase1_front(s, t, slot, xs, tp, cT, krs, RcT, Rkrs):
            P.stage = "F1"
            P.op("sp", lambda e: e.dma_start(out=xs, in_=x[s, t * 128:(t + 1) * 128, :]),
                 writes=[("xt", slot)], dma="xt%d" % slot)
            P.op("act", lambda e: e.activation(sqj, xs, AF.Square, accum_out=st[:, 0:1]),
                 reads=[("xt", slot)], writes=["sqj", "st0"])
            rstd_from_ssq(st[:, 0:1], D, st[:, 1:2], st[:, 2:3], "st0", "st1", "st2", D)
            P.op("dve", lambda e: e.scalar_tensor_tensor(ub, xs, st[:, 2:3], gln1, ALU.mult, ALU.mult),
                 reads=[("xt", slot), "st2", "gainsA"], writes=["ub"])
            P.stage = "F2a"
            for c in range(8):
                P.op("pe", lambda e, c=c: e.transpose(tp[:, c * 128:(c + 1) * 128], ub[:, c * 128:(c + 1) * 128], ident),
                     reads=["ub", "ident"], writes=PS(0))
            P.op("act", lambda e: e.copy(uT, tp.rearrange("p (c t) -> p c t", c=8)), reads=PS(0), writes=["uT"])
            for (bk, n0, nn, wres) in ((2, 512, 384, "Win_q"), (3, 896, 288, "Win_kv"), (1, 0, 512, "Win_p")):
                for c in range(8):
                    P.op("pe", lambda e, bk=bk, n0=n0, nn=nn, c=c: e.matmul(
                        psum[:, bk, 0:nn], uT[:, c, :], Win[:, c, n0:n0 + nn], start=(c == 0), stop=(c == 7)),
                        reads=["uT", wres], writes=PS(bk))
            P.stage = "F2b"
            PIN_LATER = True
            P.op("act", lambda e: e.activation(sqj[:, 0:384], psum[:, 2, 0:384], AF.Square, accum_out=st[:, 4:5]),
                 reads=PS(2), writes=["sqj", "st4"])
            P.op("act", lambda e: e.activation(sqj[:, 0:256], psum[:, 3, 0:256], AF.Square, accum_out=st[:, 5:6]),
                 reads=PS(3), writes=["sqj", "st5"])
            P.op("act", lambda e: e.activation(st[:, 6:7], st[:, 4:5], AF.Sqrt, bias=EPS, scale=1.0 / 384),
                 reads=["st4"], writes=["st6"])
            P.op("act", lambda e: e.activation(st[:, 7:8], st[:, 5:6], AF.Sqrt, bias=EPS, scale=1.0 / 256),
                 reads=["st5"], writes=["st7"])
            P.op("dve", lambda e: e.reciprocal(st[:, 8:10], st[:, 6:8]), reads=["st6", "st7"], writes=["st8"])
            P.op("dve", lambda e: e.scalar_tensor_tensor(cb[:, 0:384], psum[:, 2, 0:384], st[:, 8:9], gqa, ALU.mult, ALU.mult),
                 reads=PS(2) + ["st8", "gainsA"], writes=["cb"])
            P.op("dve", lambda e: e.scalar_tensor_tensor(cb[:, 384:640], psum[:, 3, 0:256], st[:, 9:10], gkva, ALU.mult, ALU.mult),
                 reads=PS(3) + ["st8", "gainsA"], writes=["cb"])
            P.op("act", lambda e: e.copy(krs, psum[:, 3, 256:288]), reads=PS(3), writes=[Rkrs])
            P.op("act", lambda e: e.copy(pinb[t % 4], psum[:, 1, :]), reads=PS(1), writes=[("pin", t % 4)])
            P.stage = "F3"
            for c in range(5):
                P.op("pe", lambda e, c=c: e.transpose(tp[:, c * 128:(c + 1) * 128], cb[:, c * 128:(c + 1) * 128], ident),
                     reads=["cb", "ident"], writes=PS(0))
            P.op("act", lambda e: e.copy(cT, tp[:, 0:640].rearrange("p (c t) -> p c t", c=5)), reads=PS(0), writes=[RcT])

        def phase1_back(s, t, cT, krs, RcT, Rkrs):
            P.stage = "B1"
            for hf in range(2):
                for c in range(3):
                    P.op("pe", lambda e, hf=hf, c=c: e.matmul(psum[:, 4 + hf, 0:384], cT[:, c, :],
                                                              Wqb[:, c, hf * 384:(hf + 1) * 384], start=(c == 0), stop=(c == 2)),
                         reads=[RcT, "Wqb"], writes=PS(4 + hf))
            for hf in range(2):
                for c in range(2):
                    P.op("pe", lambda e, hf=hf, c=c: e.matmul(psum[:, 6 + hf, :], cT[:, 3 + c, :],
                                                              Wkvb[:, c, hf * 512:(hf + 1) * 512], start=(c == 0), stop=(c == 1)),
                         reads=[RcT, "Wkvb"], writes=PS(6 + hf))
            P.stage = "B2q"
            psq = psum[:, 4:6, 0:384]
            sqq3 = sqq.rearrange("p (a b) -> p a b", a=2)
            P.op("act", lambda e: e.activation(sqq3, psq, AF.Square), reads=PS(4, 2), writes=["sqq"])
            P.op("dve", lambda e: e.tensor_reduce(st[:, 16:24], sqq.rearrange("p (h d) -> p h d", h=8), AX.X, ALU.add),
                 reads=["sqq"], writes=["st16"])
            P.op("act", lambda e: e.activation(st[:, 24:32], st[:, 16:24], AF.Sqrt, bias=EPS, scale=1.0 / 96),
                 reads=["st16"], writes=["st24"])
            P.op("dve", lambda e: e.reciprocal(st[:, 32:40], st[:, 24:32]), reads=["st24"], writes=["st32"])
            Tq = sqq.rearrange("p (h d) -> p h d", h=8)
            for hf in range(2):
                P.op("dve", lambda e, hf=hf: e.tensor_tensor(
                    Tq[:, hf * 4:(hf + 1) * 4, :], psum[:, 4 + hf, 0:384].rearrange("p (h d) -> p h d", h=4),
                    st[:, 32 + hf * 4:36 + hf * 4].unsqueeze(2).to_broadcast([128, 4, 96]), ALU.mult),
                    reads=PS(4 + hf) + ["st32", "sqq"], writes=["sqq"])
            P.op("dve", lambda e: e.tensor_tensor(qfin[:, :, 0:64], Tq[:, :, 0:64],
                                                  gq[:, 0:64].unsqueeze(1).to_broadcast([128, 8, 64]), ALU.mult),
                 reads=["sqq", "gainsA"], writes=["qfin_n"])
            P.op(RE, lambda e: e.tensor_tensor(Ar, Tq[:, :, 64:96], CGq[:, t, :].unsqueeze(1).to_broadcast([128, 8, 32]), ALU.mult),
                 reads=["sqq", "ropeq"], writes=["Ar"])
            P.op(RE, lambda e: e.tensor_tensor(Br[:, :, 0:16], Tq[:, :, 80:96], SG1q[:, t, :].unsqueeze(1).to_broadcast([128, 8, 16]), ALU.mult),
                 reads=["sqq", "ropeq"], writes=["Br"])
            P.op(RE, lambda e: e.tensor_tensor(Br[:, :, 16:32], Tq[:, :, 64:80], SG2q[:, t, :].unsqueeze(1).to_broadcast([128, 8, 16]), ALU.mult),
                 reads=["sqq", "ropeq"], writes=["Br"])
            P.op(RE, lambda e: e.tensor_tensor(qfin[:, :, 64:96], Ar, Br, ALU.add), reads=["Ar", "Br"], writes=["qfin_r"])
            P.stage = "B2k"
            kv3 = psum[:, 6:8, :].rearrange("p a (h d) -> p (a h) d", d=128)
            sqk3 = sqk.rearrange("p (h d) -> p h d", h=8)
            P.op("act", lambda e: e.activation(sqk3, kv3[:, :, 0:64], AF.Square), reads=PS(6, 2), writes=["sqk"])
            P.op("dve", lambda e: e.tensor_reduce(st[:, 40:48], sqk3, AX.X, ALU.add), reads=["sqk"], writes=["st40"])
            P.op("act", lambda e: e.activation(sqjk, krs, AF.Square, accum_out=st[:, 10:11]),
                 reads=[Rkrs], writes=["sqjk", "st10"])
            kv4 = psum[:, 6:8, :].rearrange("p a (j e d) -> p (a j) e d", e=2, d=128)
            P.op("act", lambda e: e.copy(V[:, t, :, 0:64], kv4[:, :, 0, 64:128]), reads=PS(6, 2), writes=["V"])
            P.op("act", lambda e: e.copy(V[:, t, :, 128:192], kv4[:, :, 1, 64:128]), reads=PS(6, 2), writes=["V"])
            P.op("dve", lambda e: e.tensor_scalar(st[:, 40:48], st[:, 40:48], st[:, 10:11], None, ALU.add),
                 reads=["st40", "st10"], writes=["st40"])
            P.op("act", lambda e: e.activation(st[:, 48:56], st[:, 40:48], AF.Sqrt, bias=EPS, scale=1.0 / 96),
                 reads=["st40"], writes=["st48"])
            P.op("dve", lambda e: e.reciprocal(st[:, 56:64], st[:, 48:56]), reads=["st48"], writes=["st56"])
            P.op("dve", lambda e: e.tensor_tensor(sqk3, kv3[:, :, 0:64], st[:, 56:64].unsqueeze(2).to_broadcast([128, 8, 64]), ALU.mult),
                 reads=PS(6, 2) + ["st56", "sqk"], writes=["sqk"])
            P.op("dve", lambda e: e.tensor_tensor(kfin[:, :, 0:64], sqk3, gk[:, 0:64].unsqueeze(1).to_broadcast([128, 8, 64]), ALU.mult),
                 reads=["sqk", "gainsA"], writes=["kfin_n"])
            P.op(RE, lambda e: e.tensor_tensor(Tr, krs.unsqueeze(1).to_broadcast([128, 8, 32]),
                                                  st[:, 56:64].unsqueeze(2).to_broadcast([128, 8, 32]), ALU.mult),
                 reads=[Rkrs, "st56"], writes=["Tr"])
            P.op(RE, lambda e: e.tensor_tensor(Ar, Tr, CGk[:, t, :].unsqueeze(1).to_broadcast([128, 8, 32]), ALU.mult),
                 reads=["Tr", "ropek"], writes=["Ar"])
            P.op(RE, lambda e: e.tensor_tensor(Br[:, :, 0:16], Tr[:, :, 16:32], SG1k[:, t, :].unsqueeze(1).to_broadcast([128, 8, 16]), ALU.mult),
                 reads=["Tr", "ropek"], writes=["Br"])
            P.op(RE, lambda e: e.tensor_tensor(Br[:, :, 16:32], Tr[:, :, 0:16], SG2k[:, t, :].unsqueeze(1).to_broadcast([128, 8, 16]), ALU.mult),
                 reads=["Tr", "ropek"], writes=["Br"])
            P.op(RE, lambda e: e.tensor_tensor(kfin[:, :, 64:96], Ar, Br, ALU.add), reads=["Ar", "Br"], writes=["kfin_r"])
            for (src, dst, bk, nm) in ((qfin, qT, 4, "qT"), (kfin, kT, 5, "kT")):
                P.stage = "B3q" if nm == "qT" else "B3k"
                tpb = ps_bf(bk)
                for h in range(8):
                    P.op("pe", lambda e, src=src, tpb=tpb, h=h: e.transpose(tpb[0:96, h * 128:(h + 1) * 128], src[:, h, :], ident),
                         reads=[nm[0] + "fin_n", nm[0] + "fin_r", "ident"], writes=PS(bk))
                EV = os.environ.get("KEVAC", "ad")
                ev_ = EV[0] if nm == "qT" else EV[1]
                P.op("act" if ev_ == "a" else "dve",
                     lambda e, dst=dst, tpb=tpb, ev_=ev_: (e.copy if ev_ == "a" else e.tensor_copy)(
                         dst[0:96, :, t * 128:(t + 1) * 128], tpb[0:96, :].rearrange("p (h t) -> p h t", h=8)),
                     reads=PS(bk), writes=[nm])
            P.stage = "B4"
            if t >= 1 and not os.environ.get('NOPOOL'):
                pool_tile(s, t - 1)

        def attention(s):
            groups = [(h, qb, g2) for h in range(8) for qb in range(4) for g2 in range(8)]
            NG = len(groups)
            NSB = 3

            def emit_S(n):
                h, qb, g2 = groups[n]
                sb_ = (n % NSB) * 2
                for j in range(2):
                    kc = g2 * 2 + j
                    P.op("pe", lambda e, sb_=sb_, j=j, kc=kc, h=h, qb=qb: e.matmul(
                        psum[:, sb_ + j, :], kT[0:96, h, kc * 128:(kc + 1) * 128],
                        qT[0:96, h, qb * 512:(qb + 1) * 512], start=True, stop=True),
                        reads=["kT", "qT"], writes=PS(sb_ + j))

            def emit_exp(n):
                sb_ = (n % NSB) * 2
                slot = n % 3
                P.op("act", lambda e, sb_=sb_, slot=slot: e.activation(pT[slot], psum[:, sb_:sb_ + 2, :], AF.Exp, scale=ATT_SCALE),
                     reads=PS(sb_, 2), writes=[("pT", slot)])

            def emit_PV(n):
                h, qb, g2 = groups[n]
                it = n // 8
                pair, odd = h // 2, h % 2
                ob = 6 + (it % 2)
                slot = n % 3
                for j in range(2):
                    kc = g2 * 2 + j
                    lhsT = V[:, kc, pair, 64:192] if odd else V[:, kc, pair, 0:128]
                    P.op("pe", lambda e, lhsT=lhsT, ob=ob, slot=slot, j=j, kc=kc: e.matmul(
                        psum[:, ob, :], lhsT, pT[slot][:, j, :], start=(kc == 0), stop=(kc == 15)),
                        reads=["V", ("pT", slot)], writes=PS(ob))

            def norm(it):
                h, qb, _ = groups[it * 8]
                pair, odd = h // 2, h % 2
                ob = 6 + (it % 2)
                r = it % 2
                orow = slice(64, 128) if odd else slice(0, 64)
                drow = slice(0, 64) if odd else slice(64, 128)
                P.op("dve", lambda e, r=r, ob=ob, orow=orow, drow=drow: e.reciprocal(bsb[r][orow, :], psum[drow, ob, :]),
                     reads=PS(ob), writes=[("bsb", r)])
                P.op("dve", lambda e, r=r, ob=ob, orow=orow, pair=pair, qb=qb: e.tensor_tensor(
                    yT[orow, 4 + pair, qb * 512:(qb + 1) * 512], psum[orow, ob, :], bsb[r][orow, :], ALU.mult),
                    reads=PS(ob) + [("bsb", r)], writes=[("yTa", h, qb)])

            for n0 in range(NSB):
                emit_S(n0)
            for n in range(NG):
                emit_exp(n)
                if n + NSB < NG:
                    emit_S(n + NSB)
                emit_PV(n)
                if n % 8 == 7:
                    norm(n // 8)

        stream_items = []
        state = {"next_load": 0, "next_use": 0, "consumed": 0}

        def ring_load(idx, srcs):
            slot = idx % RING
            def fn(e, slot=slot, srcs=srcs):
                return [e.dma_start(out=o_(ring[slot]), in_=i_) for (o_, i_) in srcs]
            P.op("pool", fn, reads=([] if DIRECT else [s_ for s_ in ("scr_wo", "scr_wg", "scr_wu", "scr_wpg", "scr_wpp")]),
                 writes=[("ring", slot)], dma="ring%d" % slot, ndma=len(srcs), extra=state.get("extra"))

        def block_stream():
            items = []
            wo3 = (w_o_d if DIRECT else wo_b).rearrange("(c p) n -> p c n", p=128)
            for hh in range(2):
                items.append([(lambda r: r.rearrange("p (c n) -> p c n", c=4), wo3[:, hh * 4:(hh + 1) * 4, :])])
            wg3 = (w_gate_d if DIRECT else wg_b).rearrange("(c p) n -> p c n", p=128)
            wu3 = (w_up_d if DIRECT else wu_b).rearrange("(c p) n -> p c n", p=128)
            for fp in range(NF // 2):
                items.append([
                    (lambda r: r[:, 0:2048].rearrange("p (c n) -> p c n", c=8), wg3[:, :, fp * 256:(fp + 1) * 256]),
                    (lambda r: r[:, 2048:4096].rearrange("p (c n) -> p c n", c=8), wu3[:, :, fp * 256:(fp + 1) * 256]),
                ])
            wpg3 = (w_pg_d if DIRECT else wpg_b).rearrange("(c p) n -> p c n", p=128)
            for hh in range(2):
                items.append([(lambda r: r.rearrange("p (c n) -> p c n", c=4), wpg3[:, hh * 4:(hh + 1) * 4, :])])
            wpp3 = (w_pp_d if DIRECT else wpp_b).rearrange("(c p) n -> p c n", p=128)
            items.append([(lambda r: r[:, 0:2048].rearrange("p (c n) -> p c n", c=2), wpp3)])
            return items

        def ensure_loaded(upto):
            while state["next_load"] <= upto and state["next_load"] < len(stream_items):
                assert state["next_load"] < state["consumed"] + RING, "ring slot still has un-emitted consumers"
                ring_load(state["next_load"], stream_items[state["next_load"]])
                state["next_load"] += 1

        def use_item():
            idx = state["next_use"]
            state["next_use"] += 1
            ensure_loaded(idx)
            return idx % RING

        def done_items(k):
            state["consumed"] += k
            ensure_loaded(state["consumed"] + RING - 1)

        def load_Wdown():
            assert 45056 <= 18944 + 4608 + 4096 + 1024 + 5120 + 4096 + 1536 + 1024 + 384 + 384 + 32 + 1024 + 1024 + 2048 + 2048
            P.op("pool", lambda e: e.dma_start(out=Wdown, in_=(w_down_d if DIRECT_WD else wd_b).rearrange("(c p) n -> p c n", p=128)),
                 reads=([] if DIRECT_WD else ["scr_wd"]),
                 writes=["Wdown", "Win_q", "Win_kv", "Win_p", "Wqb", "Wkvb", "Wpool", "Band", "gainsA", "ropeq", "ropek"], dma="Wdown")

        def prep_B():
            load_Wdown()
            P.op("sp", lambda e: [e.dma_start(out=gln2, in_=ln2_d.partition_broadcast(128)),
                                  e.dma_start(out=gple, in_=plen_d.partition_broadcast(128))],
                 writes=["gainsB"], dma="gainsB", ndma=2)

        def norm_chain(src, gain, res_src, ui, pool_rstd=False):
            P.op("act", lambda e: e.activation(sqjB, src, AF.Square, accum_out=st[:, 0:1]),
                 reads=[res_src], writes=["sqjB", "st0"])
            if pool_rstd:
                P.op("pool", lambda e: e.tensor_scalar(st[:, 1:2], st[:, 0:1], 1.0 / D, EPS, ALU.mult, ALU.add),
                     reads=["st0"], writes=["st1"])
                P.op("pool", lambda e: e.tensor_tensor(st[:, 2:3], st[:, 1:2], negh[:, 0:1], ALU.pow),
                     reads=["st1", "negh"], writes=["st2"])
            else:
                rstd_from_ssq(st[:, 0:1], D, st[:, 1:2], st[:, 2:3], "st0", "st1", "st2", D)
            P.op("dve", lambda e: e.scalar_tensor_tensor(ubB[ui], src, st[:, 2:3], gain, ALU.mult, ALU.mult),
                 reads=[res_src, "st2", "gainsB"], writes=[("ubB", ui)])

        def transp8(dstT, res_dst, ui):
            tp = ps_bf(0)
            for c in range(8):
                P.op("pe", lambda e, c=c: e.transpose(tp[:, c * 128:(c + 1) * 128], ubB[ui][:, c * 128:(c + 1) * 128], ident),
                     reads=[("ubB", ui), "ident"], writes=PS(0))
            P.op("act", lambda e: e.copy(dstT, tp.rearrange("p (c t) -> p c t", c=8)), reads=PS(0), writes=[res_dst])

        def phase4_block(s, b):
            T0 = b * 512
            h4v = lambda tt: h4[:, tt, :].rearrange("p (a n) -> p a n", a=2)
            def load_x(bb, tt):
                tok_ = bb * 512 + tt * 128
                P.op("sp", lambda e, tt=tt, tok_=tok_: e.dma_start(out=h4[:, tt, :], in_=x[s, tok_:tok_ + 128, :]),
                     writes=[("h4", tt)], dma="h4_%d" % tt, extra=state.get("extra"))

            if b == -1:
                for tt in range(4):
                    load_x(0, tt)
                return

            def load_p(bb):
                P.op("sp", lambda e, bb=bb: e.dma_start(
                    out=ptl4, in_=pin_d[s, bb * 512:(bb + 1) * 512, :].rearrange("(t p) n -> p t n", p=128)),
                    writes=["ptl4"], dma="ptl4")

            if b == 0 or STOP == 5:
                if not EARLY:
                    for tt in range(4):
                        load_x(b, tt)
                load_p(b)
            so = [use_item(), use_item()]

            def mix(tt):
                tok = T0 + tt * 128
                setb = 1 + 2 * (tt % 2)
                for hf in range(2):
                    for c in range(8):
                        P.op("pe", lambda e, hf=hf, c=c, tok=tok, setb=setb: e.matmul(
                            psum[:, setb + hf, :], yT[:, c, tok:tok + 128],
                            ring[so[c // 4]].rearrange("p (c n) -> p c n", c=4)[:, c % 4, hf * 512:(hf + 1) * 512],
                            start=(c == 0), stop=(c == 7)),
                            reads=[("yT", tok // 128)] + [("yTa", hh, b) for hh in range(8)] + [("ring", so[c // 4])],
                            writes=PS(setb + hf))

            def add_a(tt):
                setb = 1 + 2 * (tt % 2)
                P.op("dve", lambda e, tt=tt, setb=setb: e.tensor_tensor(h4v(tt), h4v(tt), psum[:, setb:setb + 2, :], ALU.add),
                     reads=PS(setb, 2) + [("h4", tt)], writes=[("h4", tt)])

            def rest_a(tt):
                norm_chain(h4[:, tt, :], gln2, ("h4", tt), tt % 2, pool_rstd=POOL_RSTD_A)

            def Ta(tt):
                transp8(u2T[:, :, tt * 128:(tt + 1) * 128], "u2T", tt % 2)

            mix(0); add_a(0)
            mix(1); add_a(1); rest_a(0)
            mix(2); add_a(2); rest_a(1)
            Ta(0)
            mix(3); add_a(3); rest_a(2)
            Ta(1)
            rest_a(3)
            sl0 = use_item()
            rg0 = ring[sl0][:, 0:2048].rearrange("p (c n) -> p c n", c=8)
            ru0 = ring[sl0][:, 2048:4096].rearrange("p (c n) -> p c n", c=8)

            GUB = {0: (5, 6), 1: (1, 2)}

            def gu0_half(hh):
                for f_ in range(NFILL):
                    for (bk, rw) in ((GUB[f_][0], rg0), (GUB[f_][1], ru0)):
                        for c in range(8):
                            P.op("pe", lambda e, bk=bk, rw=rw, c=c, hh=hh, f_=f_: e.matmul(
                                psum[:, bk, hh * 256:(hh + 1) * 256], rw[:, c, f_ * 128:(f_ + 1) * 128],
                                u2T[:, c, hh * 256:(hh + 1) * 256], start=(c == 0), stop=(c == 7)),
                                reads=["u2T", ("ring", sl0)], writes=PS(bk))
            if TAILFILL:
                gu0_half(0)
            Ta(2)
            Ta(3)
            if TAILFILL:
                gu0_half(1)
            done_items(2)
            if STOP == 5:
                for tt in range(4):
                    tok = T0 + tt * 128
                    P.op("sp", lambda e, tt=tt, tok=tok: e.dma_start(out=y[s, tok:tok + 128, :], in_=h4[:, tt, :]),
                         reads=[("h4", tt)], dma="yo%d" % tt)
                state["next_use"] = state["next_load"] = state["consumed"] = len(stream_items)
                return
            for fp in range(NF // 2):
                sl = sl0 if fp == 0 else use_item()
                rg = ring[sl][:, 0:2048].rearrange("p (c n) -> p c n", c=8)
                ru = ring[sl][:, 2048:4096].rearrange("p (c n) -> p c n", c=8)
                for j in range(2):
                    f = fp * 2 + j
                    early = TAILFILL and f < NFILL
                    gb, ub_ = GUB[f] if early else (5, 6)
                    for (bk, rw) in ((5, rg), (6, ru)):
                        if early:
                            continue
                        for c in range(8):
                            P.op("pe", lambda e, bk=bk, rw=rw, c=c, j=j: e.matmul(
                                psum[:, bk, :], rw[:, c, j * 128:(j + 1) * 128], u2T[:, c, :], start=(c == 0), stop=(c == 7)),
                                reads=["u2T", ("ring", sl)], writes=PS(bk))
                    P.op("act", lambda e, f=f, gb=gb: e.activation(sgb[f % 2], psum[:, gb, :], AF.Silu),
                         reads=PS(gb), writes=[("sgb", f % 2)])
                    P.op("dve", lambda e, f=f, ub_=ub_: e.tensor_tensor(actT[:, f, :], sgb[f % 2], psum[:, ub_, :], ALU.mult),
                         reads=PS(ub_) + [("sgb", f % 2)], writes=["actT"])
                done_items(1)
            sg_ = [use_item(), use_item()]
            sp_ = use_item()
            P.op("dve", lambda e: e.tensor_copy(pb4, ptl4), reads=["ptl4"], writes=["pb4"])
            if b < 3:
                load_p(b + 1)
            tp7 = ps_bf(7)
            for tt in range(4):
                for c in range(2):
                    P.op("pe", lambda e, tt=tt, c=c: e.transpose(tp7[:, (tt * 2 + c) * 128:(tt * 2 + c + 1) * 128],
                                                                 pb4[:, tt, c * 128:(c + 1) * 128], ident),
                         reads=["pb4", "ident"], writes=PS(7))

            def down(tt):
                for hf in range(2):
                    for f in range(NF):
                        P.op("pe", lambda e, hf=hf, f=f, tt=tt: e.matmul(
                            psum[:, 1 + hf, :], actT[:, f, tt * 128:(tt + 1) * 128],
                            Wdown[:, f, hf * 512:(hf + 1) * 512], start=(f == 0), stop=(f == NF - 1)),
                            reads=["actT", "Wdown"], writes=PS(1 + hf))

            def chain_d(tt):
                for hf in range(2):
                    P.op("dve", lambda e, tt=tt, hf=hf: e.tensor_tensor(h4[:, tt, hf * 512:(hf + 1) * 512], h4[:, tt, hf * 512:(hf + 1) * 512],
                                                                        psum[:, 1 + hf, :], ALU.add),
                         reads=PS(1 + hf) + [("h4", tt)], writes=[("h4", tt)])
                norm_chain(h4[:, tt, :], gple, ("h4", tt), tt % 2, pool_rstd=POOL_RSTD)

            def Td(tt):
                transp8(u3T[tt % 2], ("u3T", tt % 2), tt % 2)

            def ple(tt):
                tok = T0 + tt * 128
                for hf in range(2):
                    for c in range(8):
                        P.op("pe", lambda e, hf=hf, c=c, tt=tt: e.matmul(
                            psum[:, 3 + hf, :], u3T[tt % 2][:, c, :],
                            ring[sg_[c // 4]].rearrange("p (c n) -> p c n", c=4)[:, c % 4, hf * 512:(hf + 1) * 512],
                            start=(c == 0), stop=(c == 7)),
                            reads=[("u3T", tt % 2), ("ring", sg_[c // 4])], writes=PS(3 + hf))
                for hf in range(2):
                    for c in range(2):
                        P.op("pe", lambda e, hf=hf, c=c, tt=tt: e.matmul(
                            psum[:, 5 + hf, :], pTt4[:, tt, c, :],
                            ring[sp_][:, 0:2048].rearrange("p (c n) -> p c n", c=2)[:, c, hf * 512:(hf + 1) * 512],
                            start=(c == 0), stop=(c == 1)),
                            reads=["pTt4", ("ring", sp_)], writes=PS(5 + hf))
                g2 = gsig.rearrange("p (a n) -> p a n", a=2)
                P.op("act", lambda e: e.activation(g2, psum[:, 3:5, :], AF.Tanh, scale=0.5),
                     reads=PS(3, 2), writes=["gsig"])
                P.op("dve", lambda e: e.scalar_tensor_tensor(g2, g2, 1.0, psum[:, 5:7, :], ALU.add, ALU.mult),
                     reads=["gsig"] + PS(5, 2), writes=["gsig"])
                P.op("dve", lambda e, tt=tt: e.scalar_tensor_tensor(h4[:, tt, :], gsig, 0.5, h4[:, tt, :], ALU.mult, ALU.add),
                     reads=["gsig", ("h4", tt)], writes=[("h4", tt)])
                P.op("sp", lambda e, tt=tt, tok=tok: e.dma_start(out=y[s, tok:tok + 128, :], in_=h4[:, tt, :]),
                     reads=[("h4", tt)], dma="yo%d" % tt)
                if b < 3:
                    load_x(b + 1, tt)

            down(0)
            P.op("dve", lambda e: e.tensor_copy(pTt4.rearrange("p t c n -> p (t c n)"), tp7), reads=PS(7), writes=["pTt4"])
            chain_d(0)
            down(1); chain_d(1)
            Td(0)
            down(2); chain_d(2)
            ple(0)
            Td(1)
            down(3); chain_d(3)
            ple(1)
            Td(2)
            ple(2)
            Td(3)
            ple(3)
            done_items(3)

        for s in range(NSEQ):
            if s > 0:
                P.barrier()
            if STOP <= 0:
                break
            if s == 0:
                cast_scratch()
            prep_A()
            if s == 0 and not (DIRECT and DIRECT_WD):
                P.barrier()
            if STOP <= 1:
                break
            NTL = int(os.environ.get("KNT", NT)) if STOP > 2 else 2
            def ph1(t, part, stages):
                P.filter = set(stages)
                phase1_tile(s, t, part)
                P.filter = None
                P.stage = None
            ph1(0, "front", ("F1", "F2a", "F2b", "F3"))
            for t in range(NTL):
                nxt = t + 1 < NTL
                if nxt:
                    ph1(t + 1, "front", ("F1",))
                ph1(t, "back", ("B1",))
                ph1(t, "back", ("B2q",))
                if nxt:
                    ph1(t + 1, "front", ("F2a",))
                ph1(t, "back", ("B2k",))
                if nxt:
                    ph1(t + 1, "front", ("F2b",))
                ph1(t, "back", ("B3q",))
                if nxt:
                    ph1(t + 1, "front", ("F3",))
                ph1(t, "back", ("B4a",))
                ph1(t, "back", ("B3k",))
                ph1(t, "back", ("B4b",))
            if NTL == NT:
                pool_tile(s, NT - 1)
                P.stage = None
            if STOP <= 3:
                break
            def start_stream():
                base = len(stream_items)
                for b in range(4):
                    stream_items.extend(block_stream())
                assert state["next_load"] == base and state["next_use"] == base and state["consumed"] == base
                done_items(0)
            if EARLY:
                state["extra"] = P._all_last()
                start_stream()
                phase4_block(s, -1)
                state["extra"] = None
            attention(s)
            if STOP <= 4:
                break
            P.barrier()
            if not EARLY:
                start_stream()
            prep_B()
            for b in range(4 if STOP != 5 else 1):
                phase4_block(s, b)
            assert state["next_use"] == len(stream_items) and state["next_load"] == len(stream_items)
        if DBG:
            P.barrier()
            P.op("pool", lambda e: e.dma_start(out=dbg_y, in_=yT), dma="dbg_y")
            P.op("pool", lambda e: e.dma_start(out=dbg_q, in_=qT), dma="dbg_q")
            P.op("pool", lambda e: e.dma_start(out=dbg_k, in_=kT), dma="dbg_k")
            P.op("pool", lambda e: e.dma_start(out=dbg_v, in_=V), dma="dbg_v")
        P.finish()
        nw, counts = P.emit(lambda name: es.enter_context(nc.semaphore(name)))
        print("ops", len(P.ops), "waits", nw, "sems", len(counts), {k: v for k, v in counts.items() if k[0] == "eng"})
    return nc


def _consts():
    ident = np.eye(128, dtype=np.float32)
    band = np.zeros((20, 128, 128), np.float32)
    for g, w in enumerate(POOL_WINDOWS):
        half = w // 2
        Bf = np.zeros((S, S), np.float32)
        tt = np.arange(S)
        lo = np.clip(tt - half, 0, S)
        hi = np.clip(tt - half + w, 0, S)
        for t_ in range(S):
            Bf[lo[t_]:hi[t_], t_] = np.float32(1.0) / np.float32(hi[t_] - lo[t_])
            Bf[t_, t_] -= 1.0
        band[g * 5 + 0] = Bf[0:128, 128:256]
        band[g * 5 + 1] = Bf[128:256, 128:256]
        band[g * 5 + 2] = Bf[256:384, 128:256]
        band[g * 5 + 3] = Bf[0:128, 0:128]
        band[g * 5 + 4] = Bf[S - 128:, S - 128:]
    inv = np.float32(10000.0) ** (-np.arange(0, 32, 2, dtype=np.float32) / np.float32(32))
    ang = np.arange(S, dtype=np.float32)[:, None] * inv[None, :].astype(np.float32)
    cosT = np.cos(ang).astype(np.float32)
    sinT = np.sin(ang).astype(np.float32)
    band = np.ascontiguousarray(band.transpose(1, 0, 2).reshape(128, 20 * 128))
    cosT = np.ascontiguousarray(cosT.reshape(NT, 128, 16).transpose(1, 0, 2).reshape(128, 256))
    sinT = np.ascontiguousarray(sinT.reshape(NT, 128, 16).transpose(1, 0, 2).reshape(128, 256))
    return ident, band, cosT, sinT


_NC_CACHE = {}


def kernel(x_prompt, x_sample, p_prompt, p_sample, ln1, w_in, w_pool, pool_scale, q_a_norm, w_qb,
           kv_a_norm, w_kvb, q_norm, k_norm, w_o, ln2, w_gate, w_up, w_down, ple_norm, w_ple_gate,
           w_ple_proj):
    f = lambda a: np.ascontiguousarray(np.asarray(a, dtype=np.float32))
    X = np.concatenate([f(x_prompt), f(x_sample)], axis=0)
    Pm = np.concatenate([f(p_prompt)[0], f(p_sample)[0]], axis=0)
    ident, band, cosT, sinT = _consts()
    common = {
        "ln1": f(ln1).reshape(1, D), "ln2": f(ln2).reshape(1, D), "ple_norm": f(ple_norm).reshape(1, D),
        "q_a_norm": f(q_a_norm).reshape(1, 384), "kv_a_norm": f(kv_a_norm).reshape(1, 256),
        "q_norm": f(q_norm).reshape(1, 96), "k_norm": f(k_norm).reshape(1, 96),
        "pool_scale": np.ascontiguousarray(f(pool_scale).reshape(4, 128).T),
        "w_in": f(w_in)[0], "w_pool": np.ascontiguousarray(f(w_pool)[0].transpose(1, 0, 2).reshape(128, 512)), "w_qb": f(w_qb)[0], "w_kvb": f(w_kvb)[0],
        "w_o": f(w_o)[0], "w_gate": f(w_gate)[0], "w_up": f(w_up)[0], "w_down": f(w_down)[0],
        "w_ple_gate": f(w_ple_gate)[0], "w_ple_proj": f(w_ple_proj)[0],
        "ident": ident, "band": band, "cosT": cosT, "sinT": sinT,
    }
    if "nc" not in _NC_CACHE:
        _NC_CACHE["nc"] = build_nc(NSEQ_CORE)
    nc = _NC_CACHE["nc"]
    in_maps = []
    for c in range(NCORES):
        m = dict(common)
        m["x"] = np.ascontiguousarray(X[c * NSEQ_CORE:(c + 1) * NSEQ_CORE])
        m["p"] = np.ascontiguousarray(Pm[c * NSEQ_CORE:(c + 1) * NSEQ_CORE])
        in_maps.append(m)
    res = run_bass_kernel_spmd(nc, in_maps, core_ids=list(range(NCORES)))
    Y = np.concatenate([r["y"] for r in res.results], axis=0)
    nb = x_prompt.shape[0]
    return (np.ascontiguousarray(Y[:nb]).astype(np.float32), np.ascontiguousarray(Y[nb:]).astype(np.float32))
```

```python
import math
import os
from contextlib import ExitStack
import numpy as np
import concourse.bass as bass
import concourse.mybir as mybir
from concourse.bass_utils import run_bass_kernel_spmd

F32 = mybir.dt.float32
BF16 = mybir.dt.bfloat16
U8 = mybir.dt.uint8
AF = mybir.ActivationFunctionType
ALU = mybir.AluOpType
AX = mybir.AxisListType

S = 2048
D = 1024
NT = S // 128
IN_W = 1184
DFF = 2816
NF = DFF // 128
EPS = 1e-6
ATT_SCALE = 1.0 / math.sqrt(96.0)
POOL_WINDOWS = (2, 4, 8, 16)
NCORES = 8
NSEQ_CORE = 3


SAME_ENG_EDGES = bool(os.environ.get("KSE"))


class _Op:
    __slots__ = ("eng", "fn", "deps", "is_dma", "semkey", "ndma", "signal", "token", "idx", "nosig")


class Prog:
    ENGS = ("pe", "act", "dve", "pool", "sp")

    def __init__(self, nc):
        self.nc = nc
        self.engs = {"pe": nc.tensor, "act": nc.scalar, "dve": nc.vector,
                     "pool": nc.gpsimd, "sp": nc.sync}
        self.ops = []
        self.last_writer = {}
        self.readers = {}
        self.pending = {e: set() for e in self.ENGS}
        self.filter = None
        self.stage = None

    def op(self, eng, fn, reads=(), writes=(), dma=None, ndma=1, nosig=False, extra=None):
        if self.filter is not None and self.stage not in self.filter:
            return None
        o = _Op()
        o.nosig = nosig
        o.eng = eng
        o.fn = fn
        o.is_dma = dma is not None
        o.semkey = dma
        o.ndma = ndma
        o.signal = o.is_dma
        o.token = None
        o.idx = len(self.ops)
        deps = set()
        ops = self.ops
        for r in reads:
            w = self.last_writer.get(r)
            if w is not None:
                wo = ops[w]
                if not (wo.eng == "pe" and eng == "pe" and not wo.is_dma and not o.is_dma):
                    deps.add(w)
        for r in writes:
            w = self.last_writer.get(r)
            if w is not None:
                wo = ops[w]
                if wo.is_dma or o.is_dma or wo.eng != eng or (SAME_ENG_EDGES and eng != "pe"):
                    deps.add(w)
            for rd in self.readers.get(r, {}).values():
                ro = ops[rd]
                if ro.is_dma or o.is_dma or ro.eng != eng or (SAME_ENG_EDGES and eng != "pe"):
                    deps.add(rd)
        if self.pending[eng]:
            deps |= self.pending[eng]
            self.pending[eng] = set()
        if extra:
            deps |= set(extra)
        o.deps = deps
        rkey = ("dma", dma) if o.is_dma else eng
        for r in reads:
            self.readers.setdefault(r, {})[rkey] = o.idx
        for r in writes:
            self.last_writer[r] = o.idx
            self.readers[r] = {}
        ops.append(o)
        return o.idx

    def _all_last(self):
        last = {}
        for o in self.ops:
            if o.is_dma:
                last[("dma", o.semkey)] = o.idx
            else:
                last[o.eng] = o.idx
        return set(last.values())

    def barrier(self):
        deps = self._all_last()
        for e in self.ENGS:
            self.pending[e] = set(deps)

    def finish(self):
        deps = self._all_last()
        self.pending["sp"] = set(deps)
        self.op("sp", lambda e: None)

    def emit(self, get_sem):
        ops = self.ops
        nxt = {}
        last_ok = {}
        for o in reversed(ops):
            if o.is_dma:
                continue
            if o.nosig:
                nxt[o.idx] = last_ok.get(o.eng)
            else:
                last_ok[o.eng] = o.idx
        for o in ops:
            nd = set()
            for d in o.deps:
                if ops[d].nosig:
                    r = nxt[d]
                    assert r is not None and r < o.idx, ("cannot redirect nosig dep", d, r, o.idx)
                    nd.add(r)
                else:
                    nd.add(d)
            o.deps = nd
        for o in ops:
            for d in o.deps:
                ops[d].signal = True
        counts = {}
        for o in ops:
            if o.is_dma:
                key = ("dma", o.semkey)
                counts[key] = counts.get(key, 0) + 16 * o.ndma
                o.token = (key, counts[key])
            elif o.signal:
                key = ("eng", o.eng)
                counts[key] = counts.get(key, 0) + 1
                o.token = (key, counts[key])
        sems = {key: get_sem("s_" + "_".join(str(k) for k in key)) for key in counts}
        eng_know = {e: {} for e in self.ENGS}
        know = [None] * len(ops)
        prev_dma_tok = {}
        nwaits = 0
        plan = {e: [] for e in self.ENGS}
        for o in ops:
            ek = eng_know[o.eng]
            need = {}
            for d in o.deps:
                k, v = ops[d].token
                if ek.get(k, 0) < v and need.get(k, 0) < v:
                    need[k] = v
            if need:
                newk = dict(ek)
                for d in o.deps:
                    for k, v in know[d].items():
                        if newk.get(k, 0) < v:
                            newk[k] = v
                for k in list(need.keys()):
                    v = need[k]
                    for d in o.deps:
                        tk, tv = ops[d].token
                        if tk == k and tv >= v:
                            continue
                        if know[d].get(k, 0) >= v:
                            del need[k]
                            break
                nwaits += len(need)
                eng_know[o.eng] = newk
                ek = newk
            if o.is_dma:
                key = o.token[0]
                pv = prev_dma_tok.get(key, 0)
                assert ek.get(key, 0) >= pv, f"DMA slot {key} reissued while previous group may be in flight"
                prev_dma_tok[key] = o.token[1]
            plan[o.eng].append((o, list(need.items())))
            if o.is_dma or o.signal:
                kk = dict(ek)
                kk[o.token[0]] = o.token[1]
                know[o.idx] = kk
            else:
                know[o.idx] = ek

        semv = {k: 0 for k in counts}
        ptr = {e: 0 for e in self.ENGS}
        progress = True
        while progress:
            progress = False
            for e_ in self.ENGS:
                while ptr[e_] < len(plan[e_]):
                    o, waits = plan[e_][ptr[e_]]
                    if all(semv[k] >= v for k, v in waits):
                        if o.is_dma:
                            semv[o.token[0]] += 16 * o.ndma
                        elif o.signal:
                            semv[o.token[0]] += 1
                        ptr[e_] += 1
                        progress = True
                    else:
                        break
        for e_ in self.ENGS:
            assert ptr[e_] == len(plan[e_]), ("DEADLOCK", e_, ptr[e_], len(plan[e_]), plan[e_][ptr[e_]][1], semv)

        def mk(engname):
            def body(eng):
                for (o, waits) in plan[engname]:
                    for k, v in waits:
                        eng.wait_ge(sems[k], v)
                    res = o.fn(eng)
                    if o.is_dma:
                        if not isinstance(res, (list, tuple)):
                            res = [res]
                        assert len(res) == o.ndma, (len(res), o.ndma)
                        for r in res:
                            r.then_inc(sems[o.token[0]], 16)
                    elif o.signal:
                        res.then_inc(sems[o.token[0]], 1)
            return body

        with self.nc.Block() as block:
            block.tensor(mk("pe"))
            block.scalar(mk("act"))
            block.vector(mk("dve"))
            block.gpsimd(mk("pool"))
            block.sync(mk("sp"))
        return nwaits, counts


class Arena:
    def __init__(self, big, limit):
        self.big = big
        self.off = 0
        self.limit = limit
        self.peak = 0

    def t(self, shape, dt):
        esz = 4 if dt == F32 else 2
        n = esz
        for d in shape[1:]:
            n *= d
        off = self.off
        self.off += (n + 31) // 32 * 32
        self.peak = max(self.peak, self.off)
        assert self.off <= self.limit, (self.off, self.limit)
        ap = self.big[:, off:off + n].bitcast(dt)
        if len(shape) == 3:
            ap = ap.rearrange("p (a b) -> p a b", a=shape[1])
        elif len(shape) == 4:
            ap = ap.rearrange("p (a b c) -> p a b c", a=shape[1], b=shape[2])
        return ap


def build_nc(NSEQ=NSEQ_CORE):
    nc = bass.Bass("TRN2", target_bir_lowering=False)

    def din(name, shape):
        return nc.dram_tensor(name, shape, F32, kind="ExternalInput").ap()

    x = din("x", [NSEQ, S, D])
    pin_d = din("p", [NSEQ, S, 256])
    ln1_d = din("ln1", [1, D])
    ln2_d = din("ln2", [1, D])
    plen_d = din("ple_norm", [1, D])
    qan_d = din("q_a_norm", [1, 384])
    kvan_d = din("kv_a_norm", [1, 256])
    qn_d = din("q_norm", [1, 96])
    kn_d = din("k_norm", [1, 96])
    psc_d = din("pool_scale", [128, 4])
    w_in_d = din("w_in", [D, IN_W])
    w_pool_d = din("w_pool", [128, 512])
    w_qb_d = din("w_qb", [384, 768])
    w_kvb_d = din("w_kvb", [256, 1024])
    w_o_d = din("w_o", [D, D])
    w_gate_d = din("w_gate", [D, DFF])
    w_up_d = din("w_up", [D, DFF])
    w_down_d = din("w_down", [DFF, D])
    w_pg_d = din("w_ple_gate", [D, D])
    w_pp_d = din("w_ple_proj", [256, D])
    ident_d = din("ident", [128, 128])
    band_d = din("band", [128, 2560])
    cos_d = din("cosT", [128, 256])
    sin_d = din("sinT", [128, 256])
    y = nc.dram_tensor("y", [NSEQ, S, D], F32, kind="ExternalOutput").ap()
    DBG = bool(os.environ.get("KDBG"))
    if DBG:
        dbg_y = nc.dram_tensor("dbg_y", [128, 8, S], F32, kind="ExternalOutput").ap()
        dbg_q = nc.dram_tensor("dbg_q", [128, 8, S], F32, kind="ExternalOutput").ap()
        dbg_k = nc.dram_tensor("dbg_k", [128, 8, S], F32, kind="ExternalOutput").ap()
        dbg_v = nc.dram_tensor("dbg_v", [128, 16, 4, 192], F32, kind="ExternalOutput").ap()

    def dscr(name, shape):
        return nc.dram_tensor(name, shape, BF16, kind="Internal").ap()

    wo_b = dscr("wo_b", [D, D])
    wg_b = dscr("wg_b", [D, DFF])
    wu_b = dscr("wu_b", [D, DFF])
    wd_b = dscr("wd_b", [DFF, D])
    wpg_b = dscr("wpg_b", [D, D])
    wpp_b = dscr("wpp_b", [256, D])

    es = ExitStack()
    P = Prog(nc)
    with es:
        TOT = 212000
        big = es.enter_context(nc.sbuf_tensor("big", [128, TOT], U8))
        psum = es.enter_context(nc.psum_tensor("psum", [128, 8, 512], F32))
        A = Arena(big, TOT)

        def PS(b, n=1):
            return [("ps", b + i) for i in range(n)]

        def ps_bf(b):
            return psum[:, b, :].bitcast(BF16)

        ident = A.t([128, 128], BF16)
        identf = A.t([128, 128], F32)
        ones_f = A.t([128, 128], F32)
        ones_b = A.t([128, 128], BF16)
        yT = A.t([128, 8, S], BF16)
        st = A.t([128, 64], F32)
        negh = A.t([128, 8], F32)
        region0 = A.off

        Win = A.t([128, 8, IN_W], BF16)
        Wqb = A.t([128, 3, 768], BF16)
        Wkvb = A.t([128, 2, 1024], BF16)
        Wpool = A.t([128, 4, 128], BF16)
        Band = A.t([128, 20, 128], BF16)
        gln1 = A.t([128, D], F32)
        gqa = A.t([128, 384], F32)
        gkva = A.t([128, 256], F32)
        gq = A.t([128, 96], F32)
        gk = A.t([128, 96], F32)
        psc = A.t([128, 4], F32)
        cos_t = A.t([128, 16, 16], F32)
        sin_t = A.t([128, 16, 16], F32)
        CGq = A.t([128, 16, 32], F32)
        CGk = A.t([128, 16, 32], F32)
        SG1q = A.t([128, 16, 16], F32)
        SG2q = A.t([128, 16, 16], F32)
        SG1k = A.t([128, 16, 16], F32)
        SG2k = A.t([128, 16, 16], F32)
        qT = A.t([128, 8, S], BF16)
        kT = A.t([128, 8, S], BF16)
        V = A.t([128, 16, 4, 192], BF16)
        work0 = A.off
        xt = [A.t([128, D], F32) for _ in range(2)]
        sqj = A.t([128, D], BF16)
        ub = A.t([128, D], BF16)
        uT = A.t([128, 8, 128], BF16)
        pinb = [A.t([128, 512], BF16) for _ in range(4)]
        cb = A.t([128, 640], BF16)
        cT2 = [A.t([128, 5, 128], BF16) for _ in range(2)]
        krs2 = [A.t([128, 32], F32) for _ in range(2)]
        sqjk = A.t([128, 32], BF16)
        sqq = A.t([128, 768], F32)
        sqk = A.t([128, 512], F32)
        Tr = A.t([128, 8, 32], F32)
        Ar = A.t([128, 8, 32], F32)
        Br = A.t([128, 8, 32], F32)
        qfin = A.t([128, 8, 96], BF16)
        kfin = A.t([128, 8, 96], BF16)
        dsb = A.t([128, 4, 128], BF16)
        endA = A.off
        A.off = work0
        pT = [A.t([128, 2, 512], BF16) for _ in range(3)]
        rdh = [A.t([128, 512], BF16) for _ in range(2)]
        rdl = [A.t([128, 512], BF16) for _ in range(2)]
        bsb = [A.t([128, 512], F32) for _ in range(2)]
        assert A.off <= endA

        A.off = region0
        RING = 6
        ring = [A.t([128, 4096], BF16) for _ in range(RING)]
        assert A.off - region0 <= 51488, "ring must only alias phase-1-only buffers (weights/gains/rope tables)"
        Wdown = A.t([128, NF, D], BF16)
        sqjB = A.t([128, D], BF16)
        ubB = [A.t([128, D], BF16) for _ in range(2)]
        u2T = A.t([128, 8, 512], BF16)
        u3T = [A.t([128, 8, 128], BF16) for _ in range(2)]
        actT = A.t([128, NF, 512], BF16)
        sgb = [A.t([128, 512], F32) for _ in range(2)]
        gln2 = A.t([128, D], F32)
        gple = A.t([128, D], F32)
        ptl4 = A.t([128, 4, 256], F32)
        pb4 = A.t([128, 4, 256], BF16)
        pTt4 = A.t([128, 4, 2, 128], BF16)
        gsig = A.t([128, D], F32)
        assert A.off >= work0 + 14336, "h4 must start after the attention working set"
        h4 = A.t([128, 4, D], F32)
        assert A.off <= endA, "h4 must stay inside the phase-1 working area"
        endB = A.off
        print("SBUF: always", region0, "A", endA - region0, "B", endB - region0, "peak", A.peak)

        STOP = int(os.environ.get("KSTOP", "99"))
        RE = os.environ.get("KROPE", "pool")
        def rstd_from_ssq(ssq_ap, n, tmp_ap, out_ap, rs_in, rs_tmp, rs_out, scale_n):
            P.op("act", lambda e: e.activation(tmp_ap, ssq_ap, AF.Sqrt, bias=EPS, scale=1.0 / scale_n),
                 reads=[rs_in], writes=[rs_tmp])
            P.op("dve", lambda e: e.reciprocal(out_ap, tmp_ap), reads=[rs_tmp], writes=[rs_out])

        P.op("sp", lambda e: e.dma_start(out=identf, in_=ident_d), writes=["identf"], dma="c_identf")
        P.op("dve", lambda e: e.tensor_copy(ident, identf), reads=["identf"], writes=["ident"])
        P.op("pool", lambda e: e.memset(ones_f, 1.0), writes=["ones_f"])
        P.op("pool", lambda e: e.memset(ones_b, 1.0), writes=["ones_b"])
        P.op("pool", lambda e: e.memset(negh, -0.5), writes=["negh"])
        EARLY = os.environ.get("KEARLY", "1") == "1" and STOP > 5
        TAILFILL = os.environ.get("KTAILFILL", "1") == "1" and STOP > 5
        NFILL = int(os.environ.get("KNFILL", "2"))
        POOL_RSTD = os.environ.get("KPOOLRSTD", "1") == "1"
        POOL_RSTD_A = os.environ.get("KPOOLRSTDA", "1") == "1"
        DIRECT = os.environ.get("KDIRECT", "1") == "1"
        DIRECT_WD = os.environ.get("KDIRECTWD", "1") == "1"

        def cast_scratch():
            todo = []
            if not DIRECT:
                todo += [("wo", wo_b, w_o_d), ("wg", wg_b, w_gate_d), ("wu", wu_b, w_up_d),
                         ("wpg", wpg_b, w_pg_d), ("wpp", wpp_b, w_pp_d)]
            if not DIRECT_WD:
                todo += [("wd", wd_b, w_down_d)]
            for nm, dst, src in todo:
                P.op("pool", lambda e, dst=dst, src=src: e.dma_start(out=dst, in_=src),
                     writes=["scr_" + nm], dma="scr_" + nm)

        def prep_A():
            w_in3 = w_in_d.rearrange("(c p) n -> p c n", p=128)
            for (nm_, n0_, nn_) in (("Win_q", 512, 384), ("Win_kv", 896, 288), ("Win_p", 0, 512)):
                P.op("pool", lambda e, n0_=n0_, nn_=nn_: e.dma_start(out=Win[:, :, n0_:n0_ + nn_], in_=w_in3[:, :, n0_:n0_ + nn_]),
                     writes=[nm_], dma=nm_)
            P.op("pool", lambda e: e.dma_start(out=Wqb, in_=w_qb_d.rearrange("(c p) n -> p c n", p=128)),
                 writes=["Wqb"], dma="Wqb")
            P.op("pool", lambda e: e.dma_start(out=Wkvb, in_=w_kvb_d.rearrange("(c p) n -> p c n", p=128)),
                 writes=["Wkvb"], dma="Wkvb")
            P.op("pool", lambda e: e.dma_start(out=Wpool.rearrange("p g d -> p (g d)"), in_=w_pool_d),
                 writes=["Wpool"], dma="Wpool")
            P.op("pool", lambda e: e.dma_start(out=Band.rearrange("p m t -> p (m t)"), in_=band_d),
                 writes=["Band"], dma="Band")
            P.op("sp", lambda e: [
                e.dma_start(out=gln1, in_=ln1_d.partition_broadcast(128)),
                e.dma_start(out=gqa, in_=qan_d.partition_broadcast(128)),
                e.dma_start(out=gkva, in_=kvan_d.partition_broadcast(128)),
                e.dma_start(out=gq, in_=qn_d.partition_broadcast(128)),
                e.dma_start(out=gk, in_=kn_d.partition_broadcast(128)),
                e.dma_start(out=psc, in_=psc_d),
                e.dma_start(out=cos_t.rearrange("p t j -> p (t j)"), in_=cos_d),
                e.dma_start(out=sin_t.rearrange("p t j -> p (t j)"), in_=sin_d),
            ], writes=["gainsA"], dma="gainsA", ndma=8)

            def bc(g_ap, lo):
                return g_ap[:, lo:lo + 16].unsqueeze(1).to_broadcast([128, 16, 16])
            for (CG, SG1, SG2, g_ap, nm) in ((CGq, SG1q, SG2q, gq, "q"), (CGk, SG1k, SG2k, gk, "k")):
                P.op("dve", lambda e, CG=CG, g_ap=g_ap: e.tensor_tensor(CG[:, :, 0:16], cos_t, bc(g_ap, 64), ALU.mult),
                     reads=["gainsA"], writes=["rope" + nm])
                P.op("dve", lambda e, CG=CG, g_ap=g_ap: e.tensor_tensor(CG[:, :, 16:32], cos_t, bc(g_ap, 80), ALU.mult),
                     reads=["gainsA"], writes=["rope" + nm])
                P.op("dve", lambda e, SG2=SG2, g_ap=g_ap: e.tensor_tensor(SG2, sin_t, bc(g_ap, 64), ALU.mult),
                     reads=["gainsA"], writes=["rope" + nm])
                P.op("dve", lambda e, SG1=SG1, g_ap=g_ap: e.scalar_tensor_tensor(SG1, sin_t, -1.0, bc(g_ap, 80), ALU.mult, ALU.mult),
                     reads=["gainsA"], writes=["rope" + nm])
            P.op("pool", lambda e: e.memset(V, 0.0), writes=["V"])
            P.op("pool", lambda e: e.memset(V[:, :, :, 64:128], 1.0), reads=["V"], writes=["V"])

        def pool_tile(s, i):
            PV_ = int(os.environ.get("POOLV", "0"))
            P.stage = "B4a"
            for g in range(4):
                terms = []
                if i > 0:
                    terms.append((i - 1, 0))
                terms.append((i, 3 if i == 0 else (4 if i == NT - 1 else 1)))
                if i < NT - 1:
                    terms.append((i + 1, 2))
                for n_, (j, kind) in enumerate(terms):
                    lhs_ = ub[:, g * 128:(g + 1) * 128] if PV_ == 1 else pinb[j % 4][:, g * 128:(g + 1) * 128]
                    rhs_ = ident if PV_ == 2 else Band[:, g * 5 + kind, :]
                    P.op("pe", lambda e, g=g, lhs_=lhs_, rhs_=rhs_, n_=n_, L=len(terms): e.matmul(
                        psum[:, 6, g * 128:(g + 1) * 128], lhs_, rhs_, start=(n_ == 0), stop=(n_ == L - 1)),
                        reads=[("pin", j % 4) if PV_ != 1 else "ub", "Band" if PV_ != 2 else "ident"], writes=PS(6),
                        nosig=(n_ != len(terms) - 1))
            if PV_ == 3:
                return
            dsb_ = sqj[:, 0:512].rearrange("p (g t) -> p g t", g=4) if PV_ == 5 else dsb
            if PV_ in (0, 6):
                P.op("dve", lambda e: e.tensor_copy(dsb_, psum[:, 6, :].rearrange("p (g t) -> p g t", g=4)),
                     reads=PS(6), writes=["dsb"])
            else:
                P.op("act", lambda e: e.copy(dsb_, psum[:, 6, :].rearrange("p (g t) -> p g t", g=4)),
                     reads=PS(6), writes=["dsb"] + (["sqj"] if PV_ == 5 else []))
            if os.environ.get("POOLA"):
                return
            P.stage = "B4b"
            for g in range(4):
                P.op("pe", lambda e, g=g: e.matmul(psum[:, 7, g * 128:(g + 1) * 128], Wpool[:, g, :], dsb[:, g, :],
                                                   start=True, stop=True),
                     reads=["Wpool", "dsb"], writes=PS(7))
            for g in range(4):
                P.op("dve", lambda e, g=g: e.tensor_scalar(yT[:, g, i * 128:(i + 1) * 128],
                                                           psum[:, 7, g * 128:(g + 1) * 128], psc[:, g:g + 1], None, ALU.mult),
                     reads=PS(7) + ["gainsA"], writes=[("yT", i)])

        def phase1_tile(s, t, part):
            slot = t % 2
            xs = xt[slot]
            tp = ps_bf(0)
            cT = cT2[t % 2]
            krs = krs2[t % 2]
            RcT = ("cT", t % 2)
            Rkrs = ("krs", t % 2)
            if part == "front":
                phase1_front(s, t, slot, xs, tp, cT, krs, RcT, Rkrs)
            else:
                phase1_back(s, t, cT, krs, RcT, Rkrs)

        def phase1_front(s, t, slot, xs, tp, cT, krs, RcT, Rkrs):
            P.stage = "F1"
            P.op("sp", lambda e: e.dma_start(out=xs, in_=x[s, t * 128:(t + 1) * 128, :]),
                 writes=[("xt", slot)], dma="xt%d" % slot)
            P.op("act", lambda e: e.activation(sqj, xs, AF.Square, accum_out=st[:, 0:1]),
                 reads=[("xt", slot)], writes=["sqj", "st0"])
            rstd_from_ssq(st[:, 0:1], D, st[:, 1:2], st[:, 2:3], "st0", "st1", "st2", D)
            P.op("dve", lambda e: e.scalar_tensor_tensor(ub, xs, st[:, 2:3], gln1, ALU.mult, ALU.mult),
                 reads=[("xt", slot), "st2", "gainsA"], writes=["ub"])
            P.stage = "F2a"
            for c in range(8):
                P.op("pe", lambda e, c=c: e.transpose(tp[:, c * 128:(c + 1) * 128], ub[:, c * 128:(c + 1) * 128], ident),
                     reads=["ub", "ident"], writes=PS(0))
            P.op("act", lambda e: e.copy(uT, tp.rearrange("p (c t) -> p c t", c=8)), reads=PS(0), writes=["uT"])
            for (bk, n0, nn, wres) in ((2, 512, 384, "Win_q"), (3, 896, 288, "Win_kv"), (1, 0, 512, "Win_p")):
                for c in range(8):
                    P.op("pe", lambda e, bk=bk, n0=n0, nn=nn, c=c: e.matmul(
                        psum[:, bk, 0:nn], uT[:, c, :], Win[:, c, n0:n0 + nn], start=(c == 0), stop=(c == 7)),
                        reads=["uT", wres], writes=PS(bk))
            P.stage = "F2b"
            PIN_LATER = True
            P.op("act", lambda e: e.activation(sqj[:, 0:384], psum[:, 2, 0:384], AF.Square, accum_out=st[:, 4:5]),
                 reads=PS(2), writes=["sqj", "st4"])
            P.op("act", lambda e: e.activation(sqj[:, 0:256], psum[:, 3, 0:256], AF.Square, accum_out=st[:, 5:6]),
                 reads=PS(3), writes=["sqj", "st5"])
            P.op("act", lambda e: e.activation(st[:, 6:7], st[:, 4:5], AF.Sqrt, bias=EPS, scale=1.0 / 384),
                 reads=["st4"], writes=["st6"])
            P.op("act", lambda e: e.activation(st[:, 7:8], st[:, 5:6], AF.Sqrt, bias=EPS, scale=1.0 / 256),
                 reads=["st5"], writes=["st7"])
            P.op("dve", lambda e: e.reciprocal(st[:, 8:10], st[:, 6:8]), reads=["st6", "st7"], writes=["st8"])
            P.op("dve", lambda e: e.scalar_tensor_tensor(cb[:, 0:384], psum[:, 2, 0:384], st[:, 8:9], gqa, ALU.mult, ALU.mult),
                 reads=PS(2) + ["st8", "gainsA"], writes=["cb"])
            P.op("dve", lambda e: e.scalar_tensor_tensor(cb[:, 384:640], psum[:, 3, 0:256], st[:, 9:10], gkva, ALU.mult, ALU.mult),
                 reads=PS(3) + ["st8", "gainsA"], writes=["cb"])
            P.op("act", lambda e: e.copy(krs, psum[:, 3, 256:288]), reads=PS(3), writes=[Rkrs])
            P.op("act", lambda e: e.copy(pinb[t % 4], psum[:, 1, :]), reads=PS(1), writes=[("pin", t % 4)])
            P.stage = "F3"
            for c in range(5):
                P.op("pe", lambda e, c=c: e.transpose(tp[:, c * 128:(c + 1) * 128], cb[:, c * 128:(c + 1) * 128], ident),
                     reads=["cb", "ident"], writes=PS(0))
            P.op("act", lambda e: e.copy(cT, tp[:, 0:640].rearrange("p (c t) -> p c t", c=5)), reads=PS(0), writes=[RcT])

        def phase1_back(s, t, cT, krs, RcT, Rkrs):
            P.stage = "B1"
            for hf in range(2):
                for c in range(3):
                    P.op("pe", lambda e, hf=hf, c=c: e.matmul(psum[:, 4 + hf, 0:384], cT[:, c, :],
                                                              Wqb[:, c, hf * 384:(hf + 1) * 384], start=(c == 0), stop=(c == 2)),
                         reads=[RcT, "Wqb"], writes=PS(4 + hf))
            for hf in range(2):
                for c in range(2):
                    P.op("pe", lambda e, hf=hf, c=c: e.matmul(psum[:, 6 + hf, :], cT[:, 3 + c, :],
                                                              Wkvb[:, c, hf * 512:(hf + 1) * 512], start=(c == 0), stop=(c == 1)),
                         reads=[RcT, "Wkvb"], writes=PS(6 + hf))
            P.stage = "B2q"
            psq = psum[:, 4:6, 0:384]
            sqq3 = sqq.rearrange("p (a b) -> p a b", a=2)
            P.op("act", lambda e: e.activation(sqq3, psq, AF.Square), reads=PS(4, 2), writes=["sqq"])
            P.op("dve", lambda e: e.tensor_reduce(st[:, 16:24], sqq.rearrange("p (h d) -> p h d", h=8), AX.X, ALU.add),
                 reads=["sqq"], writes=["st16"])
            P.op("act", lambda e: e.activation(st[:, 24:32], st[:, 16:24], AF.Sqrt, bias=EPS, scale=1.0 / 96),
                 reads=["st16"], writes=["st24"])
            P.op("dve", lambda e: e.reciprocal(st[:, 32:40], st[:, 24:32]), reads=["st24"], writes=["st32"])
            Tq = sqq.rearrange("p (h d) -> p h d", h=8)
            for hf in range(2):
                P.op("dve", lambda e, hf=hf: e.tensor_tensor(
                    Tq[:, hf * 4:(hf + 1) * 4, :], psum[:, 4 + hf, 0:384].rearrange("p (h d) -> p h d", h=4),
                    st[:, 32 + hf * 4:36 + hf * 4].unsqueeze(2).to_broadcast([128, 4, 96]), ALU.mult),
                    reads=PS(4 + hf) + ["st32", "sqq"], writes=["sqq"])
            P.op("dve", lambda e: e.tensor_tensor(qfin[:, :, 0:64], Tq[:, :, 0:64],
                                                  gq[:, 0:64].unsqueeze(1).to_broadcast([128, 8, 64]), ALU.mult),
                 reads=["sqq", "gainsA"], writes=["qfin_n"])
            P.op(RE, lambda e: e.tensor_tensor(Ar, Tq[:, :, 64:96], CGq[:, t, :].unsqueeze(1).to_broadcast([128, 8, 32]), ALU.mult),
                 reads=["sqq", "ropeq"], writes=["Ar"])
            P.op(RE, lambda e: e.tensor_tensor(Br[:, :, 0:16], Tq[:, :, 80:96], SG1q[:, t, :].unsqueeze(1).to_broadcast([128, 8, 16]), ALU.mult),
                 reads=["sqq", "ropeq"], writes=["Br"])
            P.op(RE, lambda e: e.tensor_tensor(Br[:, :, 16:32], Tq[:, :, 64:80], SG2q[:, t, :].unsqueeze(1).to_broadcast([128, 8, 16]), ALU.mult),
                 reads=["sqq", "ropeq"], writes=["Br"])
            P.op(RE, lambda e: e.tensor_tensor(qfin[:, :, 64:96], Ar, Br, ALU.add), reads=["Ar", "Br"], writes=["qfin_r"])
            P.stage = "B2k"
            kv3 = psum[:, 6:8, :].rearrange("p a (h d) -> p (a h) d", d=128)
            sqk3 = sqk.rearrange("p (h d) -> p h d", h=8)
            P.op("act", lambda e: e.activation(sqk3, kv3[:, :, 0:64], AF.Square), reads=PS(6, 2), writes=["sqk"])
            P.op("dve", lambda e: e.tensor_reduce(st[:, 40:48], sqk3, AX.X, ALU.add), reads=["sqk"], writes=["st40"])
            P.op("act", lambda e: e.activation(sqjk, krs, AF.Square, accum_out=st[:, 10:11]),
                 reads=[Rkrs], writes=["sqjk", "st10"])
            kv4 = psum[:, 6:8, :].rearrange("p a (j e d) -> p (a j) e d", e=2, d=128)
            P.op("act", lambda e: e.copy(V[:, t, :, 0:64], kv4[:, :, 0, 64:128]), reads=PS(6, 2), writes=["V"])
            P.op("act", lambda e: e.copy(V[:, t, :, 128:192], kv4[:, :, 1, 64:128]), reads=PS(6, 2), writes=["V"])
            P.op("dve", lambda e: e.tensor_scalar(st[:, 40:48], st[:, 40:48], st[:, 10:11], None, ALU.add),
                 reads=["st40", "st10"], writes=["st40"])
            P.op("act", lambda e: e.activation(st[:, 48:56], st[:, 40:48], AF.Sqrt, bias=EPS, scale=1.0 / 96),
                 reads=["st40"], writes=["st48"])
            P.op("dve", lambda e: e.reciprocal(st[:, 56:64], st[:, 48:56]), reads=["st48"], writes=["st56"])
            P.op("dve", lambda e: e.tensor_tensor(sqk3, kv3[:, :, 0:64], st[:, 56:64].unsqueeze(2).to_broadcast([128, 8, 64]), ALU.mult),
                 reads=PS(6, 2) + ["st56", "sqk"], writes=["sqk"])
            P.op("dve", lambda e: e.tensor_tensor(kfin[:, :, 0:64], sqk3, gk[:, 0:64].unsqueeze(1).to_broadcast([128, 8, 64]), ALU.mult),
                 reads=["sqk", "gainsA"], writes=["kfin_n"])
            P.op(RE, lambda e: e.tensor_tensor(Tr, krs.unsqueeze(1).to_broadcast([128, 8, 32]),
                                                  st[:, 56:64].unsqueeze(2).to_broadcast([128, 8, 32]), ALU.mult),
                 reads=[Rkrs, "st56"], writes=["Tr"])
            P.op(RE, lambda e: e.tensor_tensor(Ar, Tr, CGk[:, t, :].unsqueeze(1).to_broadcast([128, 8, 32]), ALU.mult),
                 reads=["Tr", "ropek"], writes=["Ar"])
            P.op(RE, lambda e: e.tensor_tensor(Br[:, :, 0:16], Tr[:, :, 16:32], SG1k[:, t, :].unsqueeze(1).to_broadcast([128, 8, 16]), ALU.mult),
                 reads=["Tr", "ropek"], writes=["Br"])
            P.op(RE, lambda e: e.tensor_tensor(Br[:, :, 16:32], Tr[:, :, 0:16], SG2k[:, t, :].unsqueeze(1).to_broadcast([128, 8, 16]), ALU.mult),
                 reads=["Tr", "ropek"], writes=["Br"])
            P.op(RE, lambda e: e.tensor_tensor(kfin[:, :, 64:96], Ar, Br, ALU.add), reads=["Ar", "Br"], writes=["kfin_r"])
            for (src, dst, bk, nm) in ((qfin, qT, 4, "qT"), (kfin, kT, 5, "kT")):
                P.stage = "B3q" if nm == "qT" else "B3k"
                tpb = ps_bf(bk)
                for h in range(8):
                    P.op("pe", lambda e, src=src, tpb=tpb, h=h: e.transpose(tpb[0:96, h * 128:(h + 1) * 128], src[:, h, :], ident),
                         reads=[nm[0] + "fin_n", nm[0] + "fin_r", "ident"], writes=PS(bk))
                EV = os.environ.get("KEVAC", "ad")
                ev_ = EV[0] if nm == "qT" else EV[1]
                P.op("act" if ev_ == "a" else "dve",
                     lambda e, dst=dst, tpb=tpb, ev_=ev_: (e.copy if ev_ == "a" else e.tensor_copy)(
                         dst[0:96, :, t * 128:(t + 1) * 128], tpb[0:96, :].rearrange("p (h t) -> p h t", h=8)),
                     reads=PS(bk), writes=[nm])
            P.stage = "B4"
            if t >= 1 and not os.environ.get('NOPOOL'):
                pool_tile(s, t - 1)

        def attention(s):
            groups = [(h, qb, g2) for h in range(8) for qb in range(4) for g2 in range(8)]
            NG = len(groups)
            NSB = 3

            def emit_S(n):
                h, qb, g2 = groups[n]
                sb_ = (n % NSB) * 2
                for j in range(2):
                    kc = g2 * 2 + j
                    P.op("pe", lambda e, sb_=sb_, j=j, kc=kc, h=h, qb=qb: e.matmul(
                        psum[:, sb_ + j, :], kT[0:96, h, kc * 128:(kc + 1) * 128],
                        qT[0:96, h, qb * 512:(qb + 1) * 512], start=True, stop=True),
                        reads=["kT", "qT"], writes=PS(sb_ + j))

            def emit_exp(n):
                sb_ = (n % NSB) * 2
                slot = n % 3
                P.op("act", lambda e, sb_=sb_, slot=slot: e.activation(pT[slot], psum[:, sb_:sb_ + 2, :], AF.Exp, scale=ATT_SCALE),
                     reads=PS(sb_, 2), writes=[("pT", slot)])

            def emit_PV(n):
                h, qb, g2 = groups[n]
                it = n // 8
                pair, odd = h // 2, h % 2
                ob = 6 + (it % 2)
                slot = n % 3
                for j in range(2):
                    kc = g2 * 2 + j
                    lhsT = V[:, kc, pair, 64:192] if odd else V[:, kc, pair, 0:128]
                    P.op("pe", lambda e, lhsT=lhsT, ob=ob, slot=slot, j=j, kc=kc: e.matmul(
                        psum[:, ob, :], lhsT, pT[slot][:, j, :], start=(kc == 0), stop=(kc == 15)),
                        reads=["V", ("pT", slot)], writes=PS(ob))

            def norm(it):
                h, qb, _ = groups[it * 8]
                pair, odd = h // 2, h % 2
                ob = 6 + (it % 2)
                r = it % 2
                orow = slice(64, 128) if odd else slice(0, 64)
                drow = slice(0, 64) if odd else slice(64, 128)
                P.op("dve", lambda e, r=r, ob=ob, orow=orow, drow=drow: e.reciprocal(bsb[r][orow, :], psum[drow, ob, :]),
                     reads=PS(ob), writes=[("bsb", r)])
                P.op("dve", lambda e, r=r, ob=ob, orow=orow, pair=pair, qb=qb: e.tensor_tensor(
                    yT[orow, 4 + pair, qb * 512:(qb + 1) * 512], psum[orow, ob, :], bsb[r][orow, :], ALU.mult),
                    reads=PS(ob) + [("bsb", r)], writes=[("yTa", h, qb)])

            for n0 in range(NSB):
                emit_S(n0)
            for n in range(NG):
                emit_exp(n)
                if n + NSB < NG:
                    emit_S(n + NSB)
                emit_PV(n)
                if n % 8 == 7:
                    norm(n // 8)

        stream_items = []
        state = {"next_load": 0, "next_use": 0, "consumed": 0}

        def ring_load(idx, srcs):
            slot = idx % RING
            def fn(e, slot=slot, srcs=srcs):
                return [e.dma_start(out=o_(ring[slot]), in_=i_) for (o_, i_) in srcs]
            P.op("pool", fn, reads=([] if DIRECT else [s_ for s_ in ("scr_wo", "scr_wg", "scr_wu", "scr_wpg", "scr_wpp")]),
                 writes=[("ring", slot)], dma="ring%d" % slot, ndma=len(srcs), extra=state.get("extra"))

        def block_stream():
            items = []
            wo3 = (w_o_d if DIRECT else wo_b).rearrange("(c p) n -> p c n", p=128)
            for hh in range(2):
                items.append([(lambda r: r.rearrange("p (c n) -> p c n", c=4), wo3[:, hh * 4:(hh + 1) * 4, :])])
            wg3 = (w_gate_d if DIRECT else wg_b).rearrange("(c p) n -> p c n", p=128)
            wu3 = (w_up_d if DIRECT else wu_b).rearrange("(c p) n -> p c n", p=128)
            for fp in range(NF // 2):
                items.append([
                    (lambda r: r[:, 0:2048].rearrange("p (c n) -> p c n", c=8), wg3[:, :, fp * 256:(fp + 1) * 256]),
                    (lambda r: r[:, 2048:4096].rearrange("p (c n) -> p c n", c=8), wu3[:, :, fp * 256:(fp + 1) * 256]),
                ])
            wpg3 = (w_pg_d if DIRECT else wpg_b).rearrange("(c p) n -> p c n", p=128)
            for hh in range(2):
                items.append([(lambda r: r.rearrange("p (c n) -> p c n", c=4), wpg3[:, hh * 4:(hh + 1) * 4, :])])
            wpp3 = (w_pp_d if DIRECT else wpp_b).rearrange("(c p) n -> p c n", p=128)
            items.append([(lambda r: r[:, 0:2048].rearrange("p (c n) -> p c n", c=2), wpp3)])
            return items

        def ensure_loaded(upto):
            while state["next_load"] <= upto and state["next_load"] < len(stream_items):
                assert state["next_load"] < state["consumed"] + RING, "ring slot still has un-emitted consumers"
                ring_load(state["next_load"], stream_items[state["next_load"]])
                state["next_load"] += 1

        def use_item():
            idx = state["next_use"]
            state["next_use"] += 1
            ensure_loaded(idx)
            return idx % RING

        def done_items(k):
            state["consumed"] += k
            ensure_loaded(state["consumed"] + RING - 1)

        def load_Wdown():
            assert 45056 <= 18944 + 4608 + 4096 + 1024 + 5120 + 4096 + 1536 + 1024 + 384 + 384 + 32 + 1024 + 1024 + 2048 + 2048
            P.op("pool", lambda e: e.dma_start(out=Wdown, in_=(w_down_d if DIRECT_WD else wd_b).rearrange("(c p) n -> p c n", p=128)),
                 reads=([] if DIRECT_WD else ["scr_wd"]),
                 writes=["Wdown", "Win_q", "Win_kv", "Win_p", "Wqb", "Wkvb", "Wpool", "Band", "gainsA", "ropeq", "ropek"], dma="Wdown")

        def prep_B():
            load_Wdown()
            P.op("sp", lambda e: [e.dma_start(out=gln2, in_=ln2_d.partition_broadcast(128)),
                                  e.dma_start(out=gple, in_=plen_d.partition_broadcast(128))],
                 writes=["gainsB"], dma="gainsB", ndma=2)

        def norm_chain(src, gain, res_src, ui, pool_rstd=False):
            P.op("act", lambda e: e.activation(sqjB, src, AF.Square, accum_out=st[:, 0:1]),
                 reads=[res_src], writes=["sqjB", "st0"])
            if pool_rstd:
                P.op("pool", lambda e: e.tensor_scalar(st[:, 1:2], st[:, 0:1], 1.0 / D, EPS, ALU.mult, ALU.add),
                     reads=["st0"], writes=["st1"])
                P.op("pool", lambda e: e.tensor_tensor(st[:, 2:3], st[:, 1:2], negh[:, 0:1], ALU.pow),
                     reads=["st1", "negh"], writes=["st2"])
            else:
                rstd_from_ssq(st[:, 0:1], D, st[:, 1:2], st[:, 2:3], "st0", "st1", "st2", D)
            P.op("dve", lambda e: e.scalar_tensor_tensor(ubB[ui], src, st[:, 2:3], gain, ALU.mult, ALU.mult),
                 reads=[res_src, "st2", "gainsB"], writes=[("ubB", ui)])

        def transp8(dstT, res_dst, ui):
            tp = ps_bf(0)
            for c in range(8):
                P.op("pe", lambda e, c=c: e.transpose(tp[:, c * 128:(c + 1) * 128], ubB[ui][:, c * 128:(c + 1) * 128], ident),
                     reads=[("ubB", ui), "ident"], writes=PS(0))
            P.op("act", lambda e: e.copy(dstT, tp.rearrange("p (c t) -> p c t", c=8)), reads=PS(0), writes=[res_dst])

        def phase4_block(s, b):
            T0 = b * 512
            h4v = lambda tt: h4[:, tt, :].rearrange("p (a n) -> p a n", a=2)
            def load_x(bb, tt):
                tok_ = bb * 512 + tt * 128
                P.op("sp", lambda e, tt=tt, tok_=tok_: e.dma_start(out=h4[:, tt, :], in_=x[s, tok_:tok_ + 128, :]),
                     writes=[("h4", tt)], dma="h4_%d" % tt, extra=state.get("extra"))

            if b == -1:
                for tt in range(4):
                    load_x(0, tt)
                return

            def load_p(bb):
                P.op("sp", lambda e, bb=bb: e.dma_start(
                    out=ptl4, in_=pin_d[s, bb * 512:(bb + 1) * 512, :].rearrange("(t p) n -> p t n", p=128)),
                    writes=["ptl4"], dma="ptl4")

            if b == 0 or STOP == 5:
                if not EARLY:
                    for tt in range(4):
                        load_x(b, tt)
                load_p(b)
            so = [use_item(), use_item()]

            def mix(tt):
                tok = T0 + tt * 128
                setb = 1 + 2 * (tt % 2)
                for hf in range(2):
                    for c in range(8):
                        P.op("pe", lambda e, hf=hf, c=c, tok=tok, setb=setb: e.matmul(
                            psum[:, setb + hf, :], yT[:, c, tok:tok + 128],
                            ring[so[c // 4]].rearrange("p (c n) -> p c n", c=4)[:, c % 4, hf * 512:(hf + 1) * 512],
                            start=(c == 0), stop=(c == 7)),
                            reads=[("yT", tok // 128)] + [("yTa", hh, b) for hh in range(8)] + [("ring", so[c // 4])],
                            writes=PS(setb + hf))

            def add_a(tt):
                setb = 1 + 2 * (tt % 2)
                P.op("dve", lambda e, tt=tt, setb=setb: e.tensor_tensor(h4v(tt), h4v(tt), psum[:, setb:setb + 2, :], ALU.add),
                     reads=PS(setb, 2) + [("h4", tt)], writes=[("h4", tt)])

            def rest_a(tt):
                norm_chain(h4[:, tt, :], gln2, ("h4", tt), tt % 2, pool_rstd=POOL_RSTD_A)

            def Ta(tt):
                transp8(u2T[:, :, tt * 128:(tt + 1) * 128], "u2T", tt % 2)

            mix(0); add_a(0)
            mix(1); add_a(1); rest_a(0)
            mix(2); add_a(2); rest_a(1)
            Ta(0)
            mix(3); add_a(3); rest_a(2)
            Ta(1)
            rest_a(3)
            sl0 = use_item()
            rg0 = ring[sl0][:, 0:2048].rearrange("p (c n) -> p c n", c=8)
            ru0 = ring[sl0][:, 2048:4096].rearrange("p (c n) -> p c n", c=8)

            GUB = {0: (5, 6), 1: (1, 2)}

            def gu_piece(t0_, t1_):
                for f_ in range(NFILL):
                    for (bk, rw) in ((GUB[f_][0], rg0), (GUB[f_][1], ru0)):
                        for c in range(8):
                            P.op("pe", lambda e, bk=bk, rw=rw, c=c, f_=f_: e.matmul(
                                psum[:, bk, t0_:t1_], rw[:, c, f_ * 128:(f_ + 1) * 128],
                                u2T[:, c, t0_:t1_], start=(c == 0), stop=(c == 7)),
                                reads=["u2T", ("ring", sl0)], writes=PS(bk))
            if TAILFILL:
                gu_piece(0, 256)
            Ta(2)
            if TAILFILL:
                gu_piece(256, 384)
            Ta(3)
            if TAILFILL:
                gu_piece(384, 512)
            done_items(2)
            if STOP == 5:
                for tt in range(4):
                    tok = T0 + tt * 128
                    P.op("sp", lambda e, tt=tt, tok=tok: e.dma_start(out=y[s, tok:tok + 128, :], in_=h4[:, tt, :]),
                         reads=[("h4", tt)], dma="yo%d" % tt)
                state["next_use"] = state["next_load"] = state["consumed"] = len(stream_items)
                return
            for fp in range(NF // 2):
                sl = sl0 if fp == 0 else use_item()
                rg = ring[sl][:, 0:2048].rearrange("p (c n) -> p c n", c=8)
                ru = ring[sl][:, 2048:4096].rearrange("p (c n) -> p c n", c=8)
                for j in range(2):
                    f = fp * 2 + j
                    early = TAILFILL and f < NFILL
                    gb, ub_ = GUB[f] if early else (5, 6)
                    for (bk, rw) in ((5, rg), (6, ru)):
                        if early:
                            continue
                        for c in range(8):
                            P.op("pe", lambda e, bk=bk, rw=rw, c=c, j=j: e.matmul(
                                psum[:, bk, :], rw[:, c, j * 128:(j + 1) * 128], u2T[:, c, :], start=(c == 0), stop=(c == 7)),
                                reads=["u2T", ("ring", sl)], writes=PS(bk))
                    P.op("act", lambda e, f=f, gb=gb: e.activation(sgb[f % 2], psum[:, gb, :], AF.Silu),
                         reads=PS(gb), writes=[("sgb", f % 2)])
                    P.op("dve", lambda e, f=f, ub_=ub_: e.tensor_tensor(actT[:, f, :], sgb[f % 2], psum[:, ub_, :], ALU.mult),
                         reads=PS(ub_) + [("sgb", f % 2)], writes=["actT"])
                done_items(1)
            sg_ = [use_item(), use_item()]
            sp_ = use_item()
            P.op("dve", lambda e: e.tensor_copy(pb4, ptl4), reads=["ptl4"], writes=["pb4"])
            if b < 3:
                load_p(b + 1)
            tp7 = ps_bf(7)
            for tt in range(4):
                for c in range(2):
                    P.op("pe", lambda e, tt=tt, c=c: e.transpose(tp7[:, (tt * 2 + c) * 128:(tt * 2 + c + 1) * 128],
                                                                 pb4[:, tt, c * 128:(c + 1) * 128], ident),
                         reads=["pb4", "ident"], writes=PS(7))

            def down(tt):
                for hf in range(2):
                    for f in range(NF):
                        P.op("pe", lambda e, hf=hf, f=f, tt=tt: e.matmul(
                            psum[:, 1 + hf, :], actT[:, f, tt * 128:(tt + 1) * 128],
                            Wdown[:, f, hf * 512:(hf + 1) * 512], start=(f == 0), stop=(f == NF - 1)),
                            reads=["actT", "Wdown"], writes=PS(1 + hf))

            def chain_d(tt):
                for hf in range(2):
                    P.op("dve", lambda e, tt=tt, hf=hf: e.tensor_tensor(h4[:, tt, hf * 512:(hf + 1) * 512], h4[:, tt, hf * 512:(hf + 1) * 512],
                                                                        psum[:, 1 + hf, :], ALU.add),
                         reads=PS(1 + hf) + [("h4", tt)], writes=[("h4", tt)])
                norm_chain(h4[:, tt, :], gple, ("h4", tt), tt % 2, pool_rstd=POOL_RSTD)

            def Td(tt):
                transp8(u3T[tt % 2], ("u3T", tt % 2), tt % 2)

            def ple(tt):
                tok = T0 + tt * 128
                for hf in range(2):
                    for c in range(8):
                        P.op("pe", lambda e, hf=hf, c=c, tt=tt: e.matmul(
                            psum[:, 3 + hf, :], u3T[tt % 2][:, c, :],
                            ring[sg_[c // 4]].rearrange("p (c n) -> p c n", c=4)[:, c % 4, hf * 512:(hf + 1) * 512],
                            start=(c == 0), stop=(c == 7)),
                            reads=[("u3T", tt % 2), ("ring", sg_[c // 4])], writes=PS(3 + hf))
                for hf in range(2):
                    for c in range(2):
                        P.op("pe", lambda e, hf=hf, c=c, tt=tt: e.matmul(
                            psum[:, 5 + hf, :], pTt4[:, tt, c, :],
                            ring[sp_][:, 0:2048].rearrange("p (c n) -> p c n", c=2)[:, c, hf * 512:(hf + 1) * 512],
                            start=(c == 0), stop=(c == 1)),
                            reads=["pTt4", ("ring", sp_)], writes=PS(5 + hf))
                g2 = gsig.rearrange("p (a n) -> p a n", a=2)
                P.op("act", lambda e: e.activation(g2, psum[:, 3:5, :], AF.Tanh, scale=0.5),
                     reads=PS(3, 2), writes=["gsig"])
                P.op("dve", lambda e: e.scalar_tensor_tensor(g2, g2, 1.0, psum[:, 5:7, :], ALU.add, ALU.mult),
                     reads=["gsig"] + PS(5, 2), writes=["gsig"])
                P.op("dve", lambda e, tt=tt: e.scalar_tensor_tensor(h4[:, tt, :], gsig, 0.5, h4[:, tt, :], ALU.mult, ALU.add),
                     reads=["gsig", ("h4", tt)], writes=[("h4", tt)])
                P.op("sp", lambda e, tt=tt, tok=tok: e.dma_start(out=y[s, tok:tok + 128, :], in_=h4[:, tt, :]),
                     reads=[("h4", tt)], dma="yo%d" % tt)
                if b < 3:
                    load_x(b + 1, tt)

            down(0)
            P.op("dve", lambda e: e.tensor_copy(pTt4.rearrange("p t c n -> p (t c n)"), tp7), reads=PS(7), writes=["pTt4"])
            chain_d(0)
            down(1); chain_d(1)
            Td(0)
            down(2); chain_d(2)
            ple(0)
            Td(1)
            down(3); chain_d(3)
            ple(1)
            Td(2)
            ple(2)
            Td(3)
            ple(3)
            done_items(3)

        for s in range(NSEQ):
            if s > 0:
                P.barrier()
            if STOP <= 0:
                break
            if s == 0:
                cast_scratch()
            prep_A()
            if s == 0 and not (DIRECT and DIRECT_WD):
                P.barrier()
            if STOP <= 1:
                break
            NTL = int(os.environ.get("KNT", NT)) if STOP > 2 else 2
            def ph1(t, part, stages):
                P.filter = set(stages)
                phase1_tile(s, t, part)
                P.filter = None
                P.stage = None
            ph1(0, "front", ("F1", "F2a", "F2b", "F3"))
            for t in range(NTL):
                nxt = t + 1 < NTL
                if nxt:
                    ph1(t + 1, "front", ("F1",))
                ph1(t, "back", ("B1",))
                ph1(t, "back", ("B2q",))
                if nxt:
                    ph1(t + 1, "front", ("F2a",))
                ph1(t, "back", ("B2k",))
                if nxt:
                    ph1(t + 1, "front", ("F2b",))
                ph1(t, "back", ("B3q",))
                if nxt:
                    ph1(t + 1, "front", ("F3",))
                ph1(t, "back", ("B4a",))
                ph1(t, "back", ("B3k",))
                ph1(t, "back", ("B4b",))
            if NTL == NT:
                pool_tile(s, NT - 1)
                P.stage = None
            if STOP <= 3:
                break
            def start_stream():
                base = len(stream_items)
                for b in range(4):
                    stream_items.extend(block_stream())
                assert state["next_load"] == base and state["next_use"] == base and state["consumed"] == base
                done_items(0)
            if EARLY:
                state["extra"] = P._all_last()
                start_stream()
                phase4_block(s, -1)
                state["extra"] = None
            attention(s)
            if STOP <= 4:
                break
            P.barrier()
            if not EARLY:
                start_stream()
            prep_B()
            for b in range(4 if STOP != 5 else 1):
                phase4_block(s, b)
            assert state["next_use"] == len(stream_items) and state["next_load"] == len(stream_items)
        if DBG:
            P.barrier()
            P.op("pool", lambda e: e.dma_start(out=dbg_y, in_=yT), dma="dbg_y")
            P.op("pool", lambda e: e.dma_start(out=dbg_q, in_=qT), dma="dbg_q")
            P.op("pool", lambda e: e.dma_start(out=dbg_k, in_=kT), dma="dbg_k")
            P.op("pool", lambda e: e.dma_start(out=dbg_v, in_=V), dma="dbg_v")
        P.finish()
        nw, counts = P.emit(lambda name: es.enter_context(nc.semaphore(name)))
        print("ops", len(P.ops), "waits", nw, "sems", len(counts), {k: v for k, v in counts.items() if k[0] == "eng"})
    return nc


def _consts():
    ident = np.eye(128, dtype=np.float32)
    band = np.zeros((20, 128, 128), np.float32)
    for g, w in enumerate(POOL_WINDOWS):
        half = w // 2
        Bf = np.zeros((S, S), np.float32)
        tt = np.arange(S)
        lo = np.clip(tt - half, 0, S)
        hi = np.clip(tt - half + w, 0, S)
        for t_ in range(S):
            Bf[lo[t_]:hi[t_], t_] = np.float32(1.0) / np.float32(hi[t_] - lo[t_])
            Bf[t_, t_] -= 1.0
        band[g * 5 + 0] = Bf[0:128, 128:256]
        band[g * 5 + 1] = Bf[128:256, 128:256]
        band[g * 5 + 2] = Bf[256:384, 128:256]
        band[g * 5 + 3] = Bf[0:128, 0:128]
        band[g * 5 + 4] = Bf[S - 128:, S - 128:]
    inv = np.float32(10000.0) ** (-np.arange(0, 32, 2, dtype=np.float32) / np.float32(32))
    ang = np.arange(S, dtype=np.float32)[:, None] * inv[None, :].astype(np.float32)
    cosT = np.cos(ang).astype(np.float32)
    sinT = np.sin(ang).astype(np.float32)
    band = np.ascontiguousarray(band.transpose(1, 0, 2).reshape(128, 20 * 128))
    cosT = np.ascontiguousarray(cosT.reshape(NT, 128, 16).transpose(1, 0, 2).reshape(128, 256))
    sinT = np.ascontiguousarray(sinT.reshape(NT, 128, 16).transpose(1, 0, 2).reshape(128, 256))
    return ident, band, cosT, sinT


_NC_CACHE = {}


def kernel(x_prompt, x_sample, p_prompt, p_sample, ln1, w_in, w_pool, pool_scale, q_a_norm, w_qb,
           kv_a_norm, w_kvb, q_norm, k_norm, w_o, ln2, w_gate, w_up, w_down, ple_norm, w_ple_gate,
           w_ple_proj):
    f = lambda a: np.ascontiguousarray(np.asarray(a, dtype=np.float32))
    X = np.concatenate([f(x_prompt), f(x_sample)], axis=0)
    Pm = np.concatenate([f(p_prompt)[0], f(p_sample)[0]], axis=0)
    ident, band, cosT, sinT = _consts()
    common = {
        "ln1": f(ln1).reshape(1, D), "ln2": f(ln2).reshape(1, D), "ple_norm": f(ple_norm).reshape(1, D),
        "q_a_norm": f(q_a_norm).reshape(1, 384), "kv_a_norm": f(kv_a_norm).reshape(1, 256),
        "q_norm": f(q_norm).reshape(1, 96), "k_norm": f(k_norm).reshape(1, 96),
        "pool_scale": np.ascontiguousarray(f(pool_scale).reshape(4, 128).T),
        "w_in": f(w_in)[0], "w_pool": np.ascontiguousarray(f(w_pool)[0].transpose(1, 0, 2).reshape(128, 512)), "w_qb": f(w_qb)[0], "w_kvb": f(w_kvb)[0],
        "w_o": f(w_o)[0], "w_gate": f(w_gate)[0], "w_up": f(w_up)[0], "w_down": f(w_down)[0],
        "w_ple_gate": f(w_ple_gate)[0], "w_ple_proj": f(w_ple_proj)[0],
        "ident": ident, "band": band, "cosT": cosT, "sinT": sinT,
    }
    if "nc" not in _NC_CACHE:
        _NC_CACHE["nc"] = build_nc(NSEQ_CORE)
    nc = _NC_CACHE["nc"]
    in_maps = []
    for c in range(NCORES):
        m = dict(common)
        m["x"] = np.ascontiguousarray(X[c * NSEQ_CORE:(c + 1) * NSEQ_CORE])
        m["p"] = np.ascontiguousarray(Pm[c * NSEQ_CORE:(c + 1) * NSEQ_CORE])
        in_maps.append(m)
    res = run_bass_kernel_spmd(nc, in_maps, core_ids=list(range(NCORES)))
    Y = np.concatenate([r["y"] for r in res.results], axis=0)
    nb = x_prompt.shape[0]
    return (np.ascontiguousarray(Y[:nb]).astype(np.float32), np.ascontiguousarray(Y[nb:]).astype(np.float32))
```
